# Optimizing a Trainium2 kernel written in Bass

```python
import jax, jax.numpy as jnp
from jax import lax
import numpy as np

D_MODEL = 1024
BATCH = 16
SEQ = 256
DEPTH = 2
DEC_BATCH = 8
DEC_SEQ = 4096
PAST_LEN = 256

F32 = jnp.float32
GRID_W = 64
N_BRANCH = 4
BRANCH_W = D_MODEL // 2
HG_DK = 128
HG_DV = 128
HG_HEADS = BRANCH_W // HG_DK
HG_CHUNK = 64
NA_DH = 64
NA_HEADS = BRANCH_W // NA_DH
WIN_R = 8
WIN_C = 16
NOPE_DIM = 128
ROPE_DIM = 64
V_DIM = 128
MLA_HEADS = BRANCH_W // V_DIM
Q_RANK = 256
KV_RANK = 128
ROPE_THETA = 10000.0
CONV_W = 3
Q_BLOCK = 128
EPS = 1e-6
DEEPNORM_ALPHA = (2 * DEPTH) ** 0.25
DEEPNORM_BETA = (8 * DEPTH) ** -0.25

SPLIT_SIZES = (
    BRANCH_W, BRANCH_W, BRANCH_W, BRANCH_W, BRANCH_W,
    BRANCH_W, BRANCH_W, BRANCH_W, BRANCH_W,
    Q_RANK, KV_RANK, ROPE_DIM, BRANCH_W,
    BRANCH_W, BRANCH_W, BRANCH_W, BRANCH_W,
    N_BRANCH * D_MODEL,
)
IN_W = sum(SPLIT_SIZES)

kernel_name = 'hybrid_diffusion_parallel_branch_step'


def split_columns(z):
    points = np.cumsum(np.array(SPLIT_SIZES))[:-1].tolist()
    return jnp.split(z, points, axis=-1)


def layer_norm(x, g, b):
    xf = x.astype(F32)
    mu = jnp.mean(xf, axis=-1, keepdims=True)
    var = jnp.mean(jnp.square(xf - mu), axis=-1, keepdims=True)
    return ((xf - mu) * lax.rsqrt(var + EPS)).astype(x.dtype) * g + b


def rms_norm(x, g):
    xf = x.astype(F32)
    y = xf * lax.rsqrt(jnp.mean(jnp.square(xf), axis=-1, keepdims=True) + EPS)
    return y.astype(x.dtype) * g


def axial_rope(x):
    T = x.shape[1]
    t = jnp.arange(T)
    row = (t // GRID_W).astype(F32)
    col = (t % GRID_W).astype(F32)
    n_pair_axis = ROPE_DIM // 4
    inv = 1.0 / (ROPE_THETA ** (jnp.arange(n_pair_axis, dtype=F32) / n_pair_axis))
    ang = jnp.concatenate([row[:, None] * inv, col[:, None] * inv], axis=-1)
    cos = jnp.cos(ang)[None, :, None, :].astype(x.dtype)
    sin = jnp.sin(ang)[None, :, None, :].astype(x.dtype)
    x1, x2 = x[..., :ROPE_DIM // 2], x[..., ROPE_DIM // 2:]
    return jnp.concatenate([x1 * cos - x2 * sin, x1 * sin + x2 * cos], axis=-1)


def blocked_attention(q, k, v, scale):
    B, Tq, H, dq = q.shape
    qb = jnp.moveaxis(q.reshape(B, Tq // Q_BLOCK, Q_BLOCK, H, dq), 1, 0)

    def attend(q_blk):
        s = jnp.einsum('bqhd,bkhd->bhqk', q_blk, k).astype(F32) * scale
        p = jax.nn.softmax(s, axis=-1).astype(v.dtype)
        return jnp.einsum('bhqk,bkhe->bqhe', p, v)

    o = lax.map(attend, qb)
    return jnp.moveaxis(o, 0, 1).reshape(B, Tq, H, v.shape[-1])


def neighbourhood_attention(q, k, v, ck, cv, rpb):
    B, T, H, d = q.shape
    rows = T // GRID_W
    wr = min(WIN_R, rows)
    scale = d ** -0.5
    q5 = q.reshape(B, rows, GRID_W, H, d)
    k5 = k.reshape(B, rows, GRID_W, H, d)
    v5 = v.reshape(B, rows, GRID_W, H, d)
    cols = jnp.arange(GRID_W)
    col_start = jnp.clip(cols - WIN_C // 2, 0, GRID_W - WIN_C)
    col_idx = col_start[:, None] + jnp.arange(WIN_C)[None, :]
    col_off = col_idx - cols[:, None] + (WIN_C - 1)

    def row_block(r):
        rs = jnp.clip(r - wr // 2, 0, rows - wr)
        qr = lax.dynamic_index_in_dim(q5, r, axis=1, keepdims=False)
        kg = lax.dynamic_slice_in_dim(k5, rs, wr, axis=1)[:, :, col_idx]
        vg = lax.dynamic_slice_in_dim(v5, rs, wr, axis=1)[:, :, col_idx]
        row_off = rs + jnp.arange(wr) - r + (WIN_R - 1)
        bias = jnp.transpose(rpb[:, row_off][:, :, col_off], (0, 2, 1, 3))
        s_loc = jnp.einsum('bqhd,brqchd->bhqrc', qr, kg).astype(F32) * scale + bias[None].astype(F32)
        s_ctx = jnp.einsum('bqhd,bkhd->bhqk', qr, ck).astype(F32) * scale
        n_loc = wr * WIN_C
        s = jnp.concatenate([s_loc.reshape(B, H, GRID_W, n_loc), s_ctx], axis=-1)
        p = jax.nn.softmax(s, axis=-1).astype(v.dtype)
        p_loc = p[..., :n_loc].reshape(B, H, GRID_W, wr, WIN_C)
        return (jnp.einsum('bhqrc,brqchd->bqhd', p_loc, vg)
                + jnp.einsum('bhqk,bkhd->bqhd', p[..., n_loc:], cv))

    out = lax.map(row_block, jnp.arange(rows))
    return jnp.moveaxis(out, 0, 1).reshape(B, T, H, d)


def hgrn_gates(f_raw, lb):
    log_g = jnp.logaddexp(jnp.log(lb), jnp.log1p(-lb) + jax.nn.log_sigmoid(f_raw.astype(F32)))
    return log_g, -jnp.expm1(log_g)


def chunk_gla(q, k, v, log_g, s0):
    B, T, H, dk = q.shape
    dv = v.shape[-1]
    L = HG_CHUNK
    n = T // L

    def to_chunks(a):
        return jnp.transpose(a.astype(F32).reshape(B, n, L, H, a.shape[-1]), (1, 0, 3, 2, 4))

    causal = jnp.tril(jnp.ones((L, L), dtype=bool))

    def step(S, inp):
        qc, kc, vc, gc = inp
        b = jnp.cumsum(gc, axis=2)
        diff = b[:, :, :, None, :] - b[:, :, None, :, :]
        decay = jnp.exp(jnp.where(causal[:, :, None], diff, -jnp.inf))
        attn = jnp.einsum('bhtd,bhsd,bhtsd->bhts', qc, kc, decay)
        o = (jnp.einsum('bhts,bhse->bhte', attn, vc)
             + jnp.einsum('bhtd,bhde->bhte', qc * jnp.exp(b), S))
        b_last = b[:, :, -1:, :]
        S = (jnp.exp(b_last[:, :, 0, :, None]) * S
             + jnp.einsum('bhsd,bhse->bhde', kc * jnp.exp(b_last - b), vc))
        return S, o

    s_fin, o = lax.scan(step, s0.astype(F32),
                        (to_chunks(q), to_chunks(k), to_chunks(v), to_chunks(log_g)))
    return jnp.transpose(o, (1, 0, 3, 2, 4)).reshape(B, T, H, dv), s_fin


def hgrn2_branch(q_raw, ff_raw, fb_raw, i_raw, g_raw, lb_f, lb_b, norm_g, s0_f, s0_b):
    B, T, _ = q_raw.shape

    def heads(a, d):
        return a.reshape(B, T, HG_HEADS, d)

    def rev(a):
        return jnp.flip(a, axis=1)

    q = heads(jax.nn.silu(q_raw.astype(F32)), HG_DK)
    v = heads(i_raw.astype(F32), HG_DV)
    lg_f, k_f = hgrn_gates(ff_raw, lb_f)
    lg_b, k_b = hgrn_gates(fb_raw, lb_b)
    o_f, s_f = chunk_gla(q, heads(k_f, HG_DK), v, heads(lg_f, HG_DK), s0_f)
    o_b, s_b = chunk_gla(rev(q), rev(heads(k_b, HG_DK)), rev(v), rev(heads(lg_b, HG_DK)), s0_b)
    o = rms_norm(o_f + rev(o_b), norm_g).reshape(B, T, BRANCH_W).astype(g_raw.dtype)
    return o * jax.nn.silu(g_raw), s_f, s_b


def mla_expand_kv(ckv, w_kvb):
    B, T, _ = ckv.shape
    kv = (ckv @ w_kvb).reshape(B, T, MLA_HEADS, NOPE_DIM + V_DIM)
    return kv[..., :NOPE_DIM], kv[..., NOPE_DIM:]


def mla_keys(k_nope, kpe):
    kpe_h = jnp.broadcast_to(kpe[:, :, None, :], k_nope.shape[:3] + (ROPE_DIM,))
    return jnp.concatenate([k_nope, kpe_h], axis=-1)


def short_conv(u, w):
    return lax.conv_general_dilated(u, w[:, None, :], window_strides=(1,),
                                    padding=((CONV_W // 2, CONV_W // 2),),
                                    dimension_numbers=('NWC', 'WIO', 'NWC'),
                                    feature_group_count=u.shape[-1])


def trunk_layer(x, cvec, ctx_cache, w_ada, b_ada, w_in, b_in, lb_f, lb_b, hg_norm_g, na_rpb,
                mla_qnorm_g, mla_w_qb, mla_kvnorm_g, mla_w_kvb, conv_w, w_branch, w_out, ln_g, ln_b):
    B, T, _ = x.shape
    is_ctx = ctx_cache is None
    mod = (jax.nn.silu(cvec) @ w_ada + b_ada).reshape(-1, 1, 3 * D_MODEL)
    shift, scale, gate = jnp.split(mod, 3, axis=-1)
    h = x * (1 + scale) + shift
    (a_q, a_ff, a_fb, a_i, a_g, b_q, b_k, b_v, b_g, c_qd, c_kvd, c_kpe, c_g,
     d_b, d_c, d_x, d_g, merge) = split_columns(h @ w_in + b_in)

    if is_ctx:
        s0 = jnp.zeros((B, 2, HG_HEADS, HG_DK, HG_DV), F32)
    else:
        s0 = ctx_cache[0].astype(F32)
    out_a, s_f, s_b = hgrn2_branch(a_q, a_ff, a_fb, a_i, a_g, lb_f, lb_b, hg_norm_g,
                                   s0[:, 0], s0[:, 1])

    q_na = b_q.reshape(B, T, NA_HEADS, NA_DH)
    k_na = b_k.reshape(B, T, NA_HEADS, NA_DH)
    v_na = b_v.reshape(B, T, NA_HEADS, NA_DH)
    if is_ctx:
        o_na = blocked_attention(q_na, k_na, v_na, NA_DH ** -0.5)
    else:
        o_na = neighbourhood_attention(q_na, k_na, v_na, ctx_cache[1], ctx_cache[2], na_rpb)
    out_b = o_na.reshape(B, T, BRANCH_W) * jax.nn.silu(b_g)

    q_mla = (rms_norm(c_qd, mla_qnorm_g) @ mla_w_qb).reshape(B, T, MLA_HEADS, NOPE_DIM + ROPE_DIM)
    ckv = rms_norm(c_kvd, mla_kvnorm_g)
    k_nope, v_mla = mla_expand_kv(ckv, mla_w_kvb)
    if is_ctx:
        keys = mla_keys(k_nope, c_kpe)
        vals = v_mla
    else:
        q_mla = jnp.concatenate([q_mla[..., :NOPE_DIM], axial_rope(q_mla[..., NOPE_DIM:])], axis=-1)
        kpe_lat = axial_rope(c_kpe[:, :, None, :])[:, :, 0]
        kn_ctx, v_ctx = mla_expand_kv(ctx_cache[3], mla_w_kvb)
        keys = jnp.concatenate([mla_keys(k_nope, kpe_lat), mla_keys(kn_ctx, ctx_cache[4])], axis=1)
        vals = jnp.concatenate([v_mla, v_ctx], axis=1)
    o_mla = blocked_attention(q_mla, keys, vals, (NOPE_DIM + ROPE_DIM) ** -0.5)
    out_c = o_mla.reshape(B, T, BRANCH_W) * jax.nn.silu(c_g)

    out_d = d_b * short_conv(d_c * d_x, conv_w) * jax.nn.silu(d_g)

    branches = jnp.stack([out_a, out_b, out_c, out_d], axis=2)
    proj = jnp.einsum('btnw,nwd->btnd', branches, w_branch)
    gates = jax.nn.sigmoid(merge).reshape(B, T, N_BRANCH, D_MODEL)
    mixed = jnp.einsum('btnd,btnd->btd', gates, proj) @ w_out
    y = layer_norm(DEEPNORM_ALPHA * x + gate * mixed, ln_g, ln_b)
    if is_ctx:
        states = jnp.stack([s_f, s_b], axis=1).astype(x.dtype)
        return y, (states, k_na, v_na, ckv, c_kpe)
    return y, None


def setup_inputs(seed: int = 0) -> dict:
    key = jax.random.key(seed)
    ks = jax.random.split(key, 26)

    def nrm(i, shape, s):
        return jax.random.normal(ks[i], shape, F32) * s

    return {
        'x_prompt': nrm(0, (BATCH, SEQ, D_MODEL), 1.0),
        'x_sample': nrm(1, (DEC_BATCH, DEC_SEQ, D_MODEL), 1.0),
        'state_hgrn': nrm(2, (DEC_BATCH, DEPTH, 2, HG_HEADS, HG_DK, HG_DV), 0.5),
        'cache_na_k': nrm(3, (DEC_BATCH, DEPTH, PAST_LEN, NA_HEADS, NA_DH), 1.0),
        'cache_na_v': nrm(4, (DEC_BATCH, DEPTH, PAST_LEN, NA_HEADS, NA_DH), 1.0),
        'cache_mla_ckv': nrm(5, (DEC_BATCH, DEPTH, PAST_LEN, KV_RANK), 1.0),
        'cache_mla_kpe': nrm(6, (DEC_BATCH, DEPTH, PAST_LEN, ROPE_DIM), 1.0),
        'c': nrm(7, (DEC_BATCH, D_MODEL), 1.0),
        'c_ctx': nrm(8, (D_MODEL,), 1.0),
        'w_ada': nrm(9, (DEPTH, D_MODEL, 3 * D_MODEL), 0.5 * D_MODEL ** -0.5),
        'b_ada': nrm(10, (DEPTH, 3 * D_MODEL), 0.01),
        'w_in': nrm(11, (DEPTH, D_MODEL, IN_W), D_MODEL ** -0.5),
        'b_in': nrm(12, (DEPTH, IN_W), 0.01),
        'hg_lb_logits': nrm(13, (2, DEPTH, BRANCH_W), 0.5),
        'hg_norm_g': 1.0 + nrm(14, (DEPTH, HG_DV), 0.02),
        'na_rpb': nrm(15, (DEPTH, NA_HEADS, 2 * WIN_R - 1, 2 * WIN_C - 1), 0.02),
        'mla_qnorm_g': 1.0 + nrm(16, (DEPTH, Q_RANK), 0.02),
        'mla_w_qb': nrm(17, (DEPTH, Q_RANK, MLA_HEADS * (NOPE_DIM + ROPE_DIM)), Q_RANK ** -0.5),
        'mla_kvnorm_g': 1.0 + nrm(18, (DEPTH, KV_RANK), 0.02),
        'mla_w_kvb': nrm(19, (DEPTH, KV_RANK, MLA_HEADS * (NOPE_DIM + V_DIM)), KV_RANK ** -0.5),
        'conv_w': nrm(20, (DEPTH, CONV_W, BRANCH_W), CONV_W ** -0.5),
        'w_branch': nrm(21, (DEPTH, N_BRANCH, BRANCH_W, D_MODEL), DEEPNORM_BETA * BRANCH_W ** -0.5),
        'w_out': nrm(22, (DEPTH, D_MODEL, D_MODEL), DEEPNORM_BETA * D_MODEL ** -0.5),
        'ln_g': 1.0 + nrm(23, (DEPTH, D_MODEL), 0.02),
        'ln_b': nrm(24, (DEPTH, D_MODEL), 0.01),
    }


def reference(x_prompt, x_sample, state_hgrn, cache_na_k, cache_na_v, cache_mla_ckv, cache_mla_kpe,
              c, c_ctx, w_ada, b_ada, w_in, b_in, hg_lb_logits, hg_norm_g, na_rpb, mla_qnorm_g,
              mla_w_qb, mla_kvnorm_g, mla_w_kvb, conv_w, w_branch, w_out, ln_g, ln_b):
    lb = jnp.cumsum(jax.nn.softmax(hg_lb_logits.astype(F32), axis=1), axis=1)
    lb = lb - lb[:, :1]
    y_p = x_prompt
    y_s = x_sample
    hg_s, na_k, na_v, mla_ckv, mla_kpe = [], [], [], [], []
    for l in range(DEPTH):
        weights = (w_ada[l], b_ada[l], w_in[l], b_in[l], lb[0, l], lb[1, l], hg_norm_g[l], na_rpb[l],
                   mla_qnorm_g[l], mla_w_qb[l], mla_kvnorm_g[l], mla_w_kvb[l], conv_w[l],
                   w_branch[l], w_out[l], ln_g[l], ln_b[l])
        y_p, ctx_l = trunk_layer(y_p, c_ctx, None, *weights)
        hg_s.append(ctx_l[0])
        na_k.append(ctx_l[1])
        na_v.append(ctx_l[2])
        mla_ckv.append(ctx_l[3])
        mla_kpe.append(ctx_l[4])
        cache_l = (state_hgrn[:, l], cache_na_k[:, l], cache_na_v[:, l],
                   cache_mla_ckv[:, l], cache_mla_kpe[:, l])
        y_s, _ = trunk_layer(y_s, c, cache_l, *weights)
    new_state_hgrn = jnp.stack(hg_s, axis=1)
    new_na_k = jnp.stack(na_k, axis=1)
    new_na_v = jnp.stack(na_v, axis=1)
    new_mla_ckv = jnp.stack(mla_ckv, axis=1)
    new_mla_kpe = jnp.stack(mla_kpe, axis=1)
    return (y_p, y_s, new_state_hgrn, new_na_k, new_na_v, new_mla_ckv, new_mla_kpe)
```

```python
import contextlib
import numpy as np
import ml_dtypes
import concourse.bass as bass
import concourse.mybir as mybir
from concourse.bass_utils import run_bass_kernel_spmd

F32 = mybir.dt.float32
BF16 = mybir.dt.bfloat16
AF = mybir.ActivationFunctionType
ALU = mybir.AluOpType

NCORES = 8
D = 1024
DEPTH = 2
TS = 4096
TP = 256
TB = 256
INW = 11712
EPS = 1e-6
ALPHA = (2 * DEPTH) ** 0.25
SEG = dict(a_q=0, a_ff=512, a_fb=1024, a_i=1536, a_g=2048, b_q=2560, b_k=3072, b_v=3584, b_g=4096,
           c_qd=4608, c_kvd=4864, c_kpe=4992, c_g=5056, d_b=5568, d_c=6080, d_x=6592, d_g=7104, mrg=7616)
NVAR = 19
ENGS = ("tensor", "vector", "scalar", "gpsimd", "sync")


def bcol(off):
    if off < 4992:
        return off // 128
    if off == 4992:
        return 39
    return 40 + (off - 5056) // 128


class Buf:
    __slots__ = ("name", "last_w", "readers")

    def __init__(self, name=""):
        self.name = name
        self.last_w = None
        self.readers = {}


class Prog:
    def __init__(self, nc):
        self.nc = nc
        self.ops = {e: [] for e in ENGS}
        self.chan_n = {}
        self.waited = {e: {} for e in ENGS}
        self.dma_chans = []
        self.needed = set()

    def dma_chan(self, name):
        c = "dma:%d" % len(self.dma_chans)
        self.dma_chans.append(c)
        self.chan_n[c] = 0
        return c

    def _deps(self, eng, reads, writes):
        deps = {}

        def add(tok):
            if tok is None:
                return
            c, i = tok
            if c.startswith("dma:"):
                i = self.chan_n[c] - 1
            if deps.get(c, -1) < i:
                deps[c] = i

        for r in reads:
            add(r.last_w)
        for w in writes:
            add(w.last_w)
            for c, i in w.readers.items():
                add((c, i))
        out = []
        wd = self.waited[eng]
        for c, i in deps.items():
            if c == "tensor" and eng == "tensor":
                continue
            if wd.get(c, -1) >= i:
                continue
            wd[c] = i
            out.append((c, i))
            self.needed.add((c, i))
        return out

    def _record(self, tok, reads, writes):
        c, i = tok
        for r in reads:
            if r.readers.get(c, -1) < i:
                r.readers[c] = i
        for w in writes:
            w.last_w = tok
            w.readers = {}

    def op(self, eng, fn, reads=(), writes=()):
        waits = self._deps(eng, reads, writes)
        idx = self.chan_n.get(eng, 0)
        self.chan_n[eng] = idx + 1
        tok = (eng, idx)
        self.ops[eng].append((fn, waits, tok, False))
        self._record(tok, reads, writes)
        return tok

    def dma(self, queue, chan, fn, reads=(), writes=(), nodep_writes=()):
        waits = self._deps(queue, reads, writes)
        idx = self.chan_n[chan]
        self.chan_n[chan] = idx + 1
        tok = (chan, idx)
        self.ops[queue].append((fn, waits, tok, True))
        self._record(tok, reads, list(writes) + list(nodep_writes))
        return tok

    def wait_all(self, eng, bufs):
        waits = self._deps(eng, (), bufs)
        self.ops[eng].append((None, waits, None, False))

    def emit(self):
        nc = self.nc
        with contextlib.ExitStack() as es:
            sems = {}
            for e in ENGS:
                sems[e] = es.enter_context(nc.semaphore("c_" + e))
            for k, c in enumerate(self.dma_chans):
                sems[c] = es.enter_context(nc.semaphore("d%d" % k))
            val = {}
            for e in ENGS:
                k = 0
                for (fn, waits, t, isdma) in self.ops[e]:
                    if t is None or isdma:
                        continue
                    if t in self.needed:
                        k += 1
                        val[t] = k
            block = es.enter_context(nc.Block())

            def run(e, handle):
                for (fn, waits, t, isdma) in self.ops[e]:
                    for (c, i) in waits:
                        if c.startswith("dma:"):
                            handle.wait_ge(sems[c], 16 * (i + 1))
                        else:
                            handle.wait_ge(sems[c], val[(c, i)])
                    if fn is None:
                        continue
                    ins = fn(handle)
                    if isdma:
                        ins.then_inc(sems[t[0]], 16)
                    elif t in self.needed:
                        ins.then_inc(sems[t[0]], 1)

            @block.sync
            def _(h):
                run("sync", h)

            @block.tensor
            def _(h):
                run("tensor", h)

            @block.vector
            def _(h):
                run("vector", h)

            @block.scalar
            def _(h):
                run("scalar", h)

            @block.gpsimd
            def _(h):
                run("gpsimd", h)


class TL:
    def __init__(self, t, name):
        self.t = t
        self.b = Buf(name)
        self.chan = None


class DR:
    def __init__(self, name):
        self.b = Buf(name)


def build_program(dbg=None):
    nc = bass.Bass("TRN2", target_bir_lowering=False)
    P = Prog(nc)

    def din(name, shape, dt=F32):
        return nc.dram_tensor(name, list(shape), dt, kind="ExternalInput").ap()

    def dout(name, shape):
        return nc.dram_tensor(name, list(shape), F32, kind="ExternalOutput").ap()

    def dscr(name, shape, dt):
        return nc.dram_tensor(name, list(shape), dt, kind="Internal").ap()

    xs = din("xs", [TS, D])
    xp = din("xp", [2, TP, D])
    st_h = din("st_h", [DEPTH, 2, 4, 128, 128])
    cnk = din("cnk", [DEPTH, 256, 512])
    cnv = din("cnv", [DEPTH, 256, 512])
    cckv = din("cckv", [DEPTH, 256, 128])
    ckpe = din("ckpe", [DEPTH, 256, 64])
    cvec = din("cvec", [2, D])
    w_ada = din("w_ada", [DEPTH, D, 3 * D])
    b_ada = din("b_ada", [DEPTH, 3 * D])
    w_in = din("w_in", [DEPTH, D, INW])
    b_in = din("b_in", [DEPTH, INW])
    lbl = din("lbl", [2, DEPTH, 512])
    hgn = din("hgn", [DEPTH, 128])
    natab = din("natab", [DEPTH, 128, 8 * NVAR * 64])
    qng = din("qng", [DEPTH, 256])
    wqb = din("wqb", [DEPTH, 256, 768])
    kvg = din("kvg", [DEPTH, 128])
    wkvb = din("wkvb", [DEPTH, 128, 1024])
    convw = din("convw", [DEPTH, 3, 512])
    wbr = din("wbr", [DEPTH, 2048, D])
    wout = din("wout", [DEPTH, D, D])
    lng = din("lng", [DEPTH, D])
    lnb = din("lnb", [DEPTH, D])
    c_idf = din("c_idf", [128, 128])
    c_idb = din("c_idb", [128, 128], BF16)
    c_cos = din("c_cos", [64, TS])
    c_sin = din("c_sin", [64, TS])
    c_mf = din("c_mf", [128, 256], BF16)
    c_mb = din("c_mb", [128, 256], BF16)
    y_p = dout("y_p", [2, TP, D])
    y_s = dout("y_s", [TS, D])
    o_st = dout("o_st", [2, DEPTH, 2, 4, 128, 128])
    o_nk = dout("o_nk", [2, DEPTH, TP, 512])
    o_nv = dout("o_nv", [2, DEPTH, TP, 512])
    o_ckv = dout("o_ckv", [2, DEPTH, TP, 128])
    o_kpe = dout("o_kpe", [2, DEPTH, TP, 64])
    s_win = dscr("s_win", [DEPTH, D, INW], BF16)
    s_wbr = dscr("s_wbr", [DEPTH, 2048, D], BF16)
    s_wout = dscr("s_wout", [DEPTH, D, D], BF16)
    s_ys = dscr("s_ys", [TS, D], F32)
    s_yp = dscr("s_yp", [2, TP, D], F32)
    s_ob = dscr("s_ob", [4, 128, TS], F32)
    s_nk = dscr("s_nk", [4, 128, TS], BF16)
    s_nv = dscr("s_nv", [TS, 512], BF16)
    s_hT = dscr("s_hT", [8, 128, TS], BF16)
    r_hT = DR("hTscr")
    r_w = DR("wscr")
    r_ymid = [DR("ymid0"), DR("ymid1")]
    r_ob = DR("ob")
    r_nk = DR("nk")
    r_nv = DR("nv")

    es = contextlib.ExitStack()
    with es:
        def sb(name, shape, dt):
            return TL(es.enter_context(nc.sbuf_tensor(name, list(shape), dt)), name)

        def pst(name, shape, dt):
            return TL(es.enter_context(nc.psum_tensor(name, list(shape), dt)), name)

        def B(xs_):
            return [x.b for x in xs_]

        def mm(out, lhsT, rhs, start, stop, R, W):
            P.op("tensor", lambda e: e.matmul(out, lhsT=lhsT, rhs=rhs, start=start, stop=stop), B(R), B(W))

        def tr(out, in_, ident, R, W):
            P.op("tensor", lambda e: e.transpose(out, in_, ident), B(R), B(W))

        def act(out, in_, func, R, W, bias=None, scale=None, accum=None):
            kw = {}
            if bias is not None:
                kw["bias"] = bias
            if scale is not None:
                kw["scale"] = scale
            if accum is not None:
                kw["accum_out"] = accum
            P.op("scalar", lambda e: e.activation(out=out, in_=in_, func=func, **kw), B(R), B(W))

        def tt(out, in0, in1, op, R, W, eng="vector"):
            P.op(eng, lambda e: e.tensor_tensor(out=out, in0=in0, in1=in1, op=op), B(R), B(W))

        def ts(out, in0, s1, s2, op0, op1, R, W, eng="vector"):
            if s2 is None:
                P.op(eng, lambda e: e.tensor_scalar(out=out, in0=in0, scalar1=s1, scalar2=None, op0=op0), B(R), B(W))
            else:
                P.op(eng, lambda e: e.tensor_scalar(out=out, in0=in0, scalar1=s1, scalar2=s2, op0=op0, op1=op1),
                     B(R), B(W))

        def stt(out, in0, scalar, in1, op0, op1, R, W, eng="vector"):
            P.op(eng, lambda e: e.scalar_tensor_tensor(out=out, in0=in0, scalar=scalar, in1=in1, op0=op0, op1=op1),
                 B(R), B(W))

        def cp(out, in_, R, W, eng="vector"):
            P.op(eng, lambda e: e.tensor_copy(out=out, in_=in_), B(R), B(W))

        def acp(out, in_, R, W):
            P.op("scalar", lambda e: e.copy(out=out, in_=in_), B(R), B(W))

        def mset(ap, v, W, eng="vector"):
            P.op(eng, lambda e: e.memset(ap, v), (), B(W))

        def recip(out, in_, R, W):
            P.op("vector", lambda e: e.reciprocal(out=out, in_=in_), B(R), B(W))

        def scan(out, d0, d1, R, W):
            P.op("vector", lambda e: e.tensor_tensor_scan(out=out, data0=d0, data1=d1, initial=0.0,
                                                          op0=ALU.mult, op1=ALU.add), B(R), B(W))

        def dma(q, tile, out, in_, R=(), W=(), ND=(), slow=False):
            if tile.chan is None:
                tile.chan = {}
            if q not in tile.chan:
                tile.chan[q] = P.dma_chan(tile.b.name + q)
            if slow:
                P.dma(q, tile.chan[q], lambda e: e.dma_start(out=out, in_=in_, allow_slow_non_contiguous=True), B(R), B(W), B(ND))
            else:
                P.dma(q, tile.chan[q], lambda e: e.dma_start(out=out, in_=in_), B(R), B(W), B(ND))

        def load(tile, out, in_, src=None, q="gpsimd"):
            dma(q, tile, out, in_, R=([src] if src is not None else []), W=[tile])

        def store(tile, out, in_, dst=None, q="gpsimd", nodep=False):
            if dst is None:
                dma(q, tile, out, in_, R=[tile])
            elif nodep:
                dma(q, tile, out, in_, R=[tile], ND=[dst])
            else:
                dma(q, tile, out, in_, R=[tile], W=[dst])

        PSR = [pst("psr%d" % i, [128, 512], F32) for i in range(5)]
        PSA = [pst("psa%d" % i, [128, 512], F32) for i in range(2)]
        PSB = pst("psb", [128, 1024], BF16)
        ring = [0]

        def ps():
            t = PSR[ring[0] % 5]
            ring[0] += 1
            return t
        acc_i = [0]

        def psacc():
            t = PSA[acc_i[0] % 2]
            acc_i[0] += 1
            return t

        idf = sb("idf", [128, 128], F32)
        idb = sb("idb", [128, 128], BF16)
        onb = sb("onb", [128, 128], BF16)
        onf = sb("onf", [128, 256], F32)
        mfw = sb("mfw", [128, 256], BF16)
        mbw = sb("mbw", [128, 256], BF16)
        load(idf, idf.t[:], c_idf[:, :])
        load(idb, idb.t[:], c_idb[:, :])
        load(mfw, mfw.t[:], c_mf[:, :])
        load(mbw, mbw.t[:], c_mb[:, :])
        mset(onb.t[:], 1.0, [onb])

        mset(onf.t[:], 1.0, [onf])

        NF = 10
        Ft = [sb("F%d" % i, [128, 1024], F32) for i in range(NF)]
        NH = 12
        Ht = [sb("H%d" % i, [128, 1024], BF16) for i in range(NH)]
        NWS = 2
        WS = [sb("WS%d" % i, [128, 8 * 512], BF16) for i in range(NWS)]
        ws_i = [0]

        cvt_i = [0]

        def convert(src2d, dst2d, rows, cols):
            CW = 1024
            for r0 in range(0, rows, 128):
                for c0 in range(0, cols, CW):
                    cw = min(CW, cols - c0)
                    i = cvt_i[0]
                    cvt_i[0] += 1
                    f = Ft[i % 4]
                    h = Ht[i % 4]
                    load(f, f.t[:, 0:cw], src2d[r0:r0 + 128, c0:c0 + cw], q="sync")
                    if i % 2 == 0:
                        cp(h.t[:, 0:cw], f.t[:, 0:cw], [f], [h])
                    else:
                        acp(h.t[:, 0:cw], f.t[:, 0:cw], [f], [h])
                    store(h, dst2d[r0:r0 + 128, c0:c0 + cw], h.t[:, 0:cw], dst=r_w, q="gpsimd", nodep=True)

        for l in range((dbg or {}).get("cvt_layers", DEPTH)):
            convert(w_in[l], s_win[l], D, INW)
            convert(wbr[l], s_wbr[l], 2048, D)
            convert(wout[l], s_wout[l], D, D)

        P.wait_all("sync", [Ht[i].b for i in range(4)])

        def wload(src2d, nk, c0, ncols):
            t = WS[ws_i[0] % NWS]
            ws_i[0] += 1
            v = t.t[:, 0:nk * ncols].rearrange("p (k n) -> p k n", n=ncols)
            srcv = src2d.rearrange("(k p) n -> p k n", p=128)
            dma("sync", t, v, srcv[:, 0:nk, c0:c0 + ncols], R=[r_w], W=[t])
            return t, v

        binT = sb("binT", [128, 92], F32)
        binH = sb("binH", [128, 92], F32)
        bkpesw = sb("bkpesw", [64, 1], F32)
        brow = sb("brow", [1, 1728], BF16)
        modT = [sb("modT%d" % i, [128, 24], F32) for i in range(2)]
        gate_bc = [sb("gatebc%d" % i, [128, 1024], F32) for i in range(2)]
        lng_bc = sb("lngbc", [128, 1024], F32)
        lnb_bc = sb("lnbbc", [128, 1024], F32)
        kvg_bc = sb("kvgbc", [128, 128], F32)
        lbT = sb("lbT", [128, 2, 4], F32)
        omlT = sb("omlT", [128, 2, 4], F32)
        hgnT = sb("hgnT", [128, 1], F32)
        qngT = sb("qngT", [128, 2], F32)
        cwT = sb("cwT", [128, 3, 4], F32)
        wqb_b = sb("wqb_b", [128, 2, 768], BF16)
        wqb_sw = sb("wqb_sw", [128, 2, 4, 64], BF16)
        wkvb_b = sb("wkvb_b", [128, 1024], BF16)
        wkT = sb("wkT", [128, 4, 128], BF16)
        EBt = sb("EB", [128, 8 * NVAR * 64], BF16)
        scT = sb("scT", [128, 2, 8], F32)
        rowst = Ft[5]
        ckT = sb("ckT", [128, 4, 256], BF16)
        cvt_ = sb("cv", [128, 2, 512], BF16)
        NKT = TS // 128 + 2
        klat = sb("klat", [128, NKT * 128], BF16)
        kpeT = sb("kpeT", [64, NKT * 128], BF16)
        vlat = sb("vlat", [128, NKT, 128], BF16)
        Sst = sb("Sst", [128, 4, 128], F32)
        Sb16 = sb("Sb16", [128, 4, 128], BF16)
        tmpS = sb("tmpS", [128, 4, 128], F32)

        class VW_:
            def __init__(self, name):
                self.b = Buf(name)
        SstH = [VW_("Sst%d" % h) for h in range(4)]
        Sb16H = [VW_("Sb16%d" % h) for h in range(4)]
        tmpSH = [VW_("tmpS%d" % h) for h in range(4)]
        sc3 = sb("sc3", [128, 3, 16], F32)
        amF = sb("amF", [128, 4, 64], BF16)
        amB = sb("amB", [128, 4, 64], BF16)
        mset(amF.t[:], 0.0, [amF])
        mset(amB.t[:], 0.0, [amB])
        ktok = sb("ktok", [128, 4, 128], BF16)
        hT = sb("hT", [128, 8, TB], BF16)
        hTh = sb("hTh", [128, 8, 2], BF16)
        xin = [sb("xin%d" % i, [128, 1024], F32) for i in range(2)]
        xhalo = Ft[7]
        cosT = sb("cosT", [64, 2 * TB], F32)
        sinT = None
        small = sb("small", [128, 64], F32)
        bnst = sb("bnst", [128, 2, 6], F32)
        bnag = sb("bnag", [128, 2], F32)
        wkpe_sw = sb("wkpe_sw", [128, 8, 64], BF16)

        def col_form(dst_ap, src_rows_ap, nrows, ncols, dst_tile):
            load(rowst, rowst.t[0:nrows, 0:ncols], src_rows_ap)
            p = ps()
            tr(p.t[0:ncols, 0:nrows], rowst.t[0:nrows, 0:ncols], idf.t[0:nrows, 0:nrows], [rowst, idf], [p])
            cp(dst_ap, p.t[0:ncols, 0:nrows], [p], [dst_tile])

        def bcast_rows(dst_tile, dst_ap, row_ap_f32, n, R):
            for c0 in range(0, n, 512):
                cw = min(512, n - c0)
                p = ps()
                mm(p.t[:, 0:cw], onf.t[0:1, 0:128], row_ap_f32[:, c0:c0 + cw], True, True, [onf] + R, [p])
                cp(dst_ap[:, c0:c0 + cw], p.t[:, 0:cw], [p], [dst_tile])

        def layer_params(l):
            mset(binT.t[:], 0.0, [binT])
            col_form(binT.t[:, 0:39], b_in[l, 0:4992].rearrange("(c p) -> c p", p=128), 39, 128, binT)
            col_form(binT.t[0:64, 39:40], b_in[l, 4992:5056].rearrange("(c p) -> c p", p=64), 1, 64, binT)
            col_form(binT.t[:, 40:92], b_in[l, 5056:INW].rearrange("(c p) -> c p", p=128), 52, 128, binT)
            ts(binH.t[:], binT.t[:], 0.5, None, ALU.mult, None, [binT], [binH])
            load(bkpesw, bkpesw.t[0:32, :], b_in[l, 5024:5056].rearrange("(p o) -> p o", o=1))
            load(bkpesw, bkpesw.t[32:64, :], b_in[l, 4992:5024].rearrange("(p o) -> p o", o=1))
            for (o, w, d0) in ():
                load(Ft[9], Ft[9].t[0:1, d0 - (0 if d0 < 1024 else 1024):d0 - (0 if d0 < 1024 else 1024) + w] if False else Ft[9 if d0 < 1024 else 8].t[0:1, (d0 % 1024):(d0 % 1024) + w],
                     b_in[l, o:o + w].rearrange("(o n) -> o n", o=1)) if False else None
            browf_parts = ((SEG["a_i"], 512, 0), (SEG["b_v"], 512, 512))
            for (o, w, d0) in browf_parts:
                load(Ft[9], Ft[9].t[0:1, d0:d0 + w], b_in[l, o:o + w].rearrange("(o n) -> o n", o=1))
            cp(brow.t[0:1, 0:1024], Ft[9].t[0:1, 0:1024], [Ft[9]], [brow])
            for (o, w, d0) in ((SEG["c_kvd"], 128, 0), (SEG["c_kpe"], 64, 128), (SEG["b_k"], 512, 192)):
                load(Ft[8], Ft[8].t[0:1, d0:d0 + w], b_in[l, o:o + w].rearrange("(o n) -> o n", o=1))
            cp(brow.t[0:1, 1024:1728], Ft[8].t[0:1, 0:704], [Ft[8]], [brow])
            for cv in range(2):
                load(rowst, rowst.t[0:8, 0:128], cvec[cv].rearrange("(c p) -> c p", p=128))
                p = ps()
                tr(p.t[:, 0:8], rowst.t[0:8, 0:128], idf.t[0:8, 0:8], [rowst, idf], [p])
                act(scT.t[:, cv, :], p.t[:, 0:8], AF.Silu, [p], [scT])
            pm = psacc()
            pg = [psacc(), ps()]
            pg2 = [ps(), ps()]
            for nr in range(3):
                for kc in range(8):
                    f = Ft[4 + (nr * 8 + kc) % 4]
                    load(f, f.t[:], w_ada[l, kc * 128:(kc + 1) * 128, nr * 1024:(nr + 1) * 1024], q="sync")
                    if nr < 2:
                        for j in range(8):
                            for cv in range(2):
                                col = (nr * 8 + j) * 2 + cv
                                mm(pm.t[:, col:col + 1], f.t[:, j * 128:(j + 1) * 128], scT.t[:, cv, kc:kc + 1],
                                   (nr == 0 and kc == 0 and j == 0 and cv == 0), (nr == 1 and kc == 7 and j == 7 and cv == 1),
                                   [f, scT], [pm])
                    else:
                        for cv in range(2):
                            for hf in range(2):
                                pt = pg[cv] if hf == 0 else pg2[cv]
                                mm(pt.t[0:1, 0:512], scT.t[:, cv, kc:kc + 1], f.t[:, hf * 512:(hf + 1) * 512],
                                   kc == 0, kc == 7, [f, scT], [pt])
            f = Ft[8]
            load(f, f.t[0:1, :], b_ada[l, 2048:3072].rearrange("(o n) -> o n", o=1))
            for cv in range(2):
                g = Ft[6 + cv]
                for hf in range(2):
                    pt = pg[cv] if hf == 0 else pg2[cv]
                    tt(g.t[0:1, hf * 512:(hf + 1) * 512], pt.t[0:1, 0:512], f.t[0:1, hf * 512:(hf + 1) * 512],
                       ALU.add, [pt, f], [g])
            col_form(small.t[:, 0:16], b_ada[l, 0:2048].rearrange("(c p) -> c p", p=128), 16, 128, small)
            pmv = pm.t[:, 0:32].rearrange("p (c v) -> p c v", v=2)
            for cv in range(2):
                tt(modT[cv].t[:, 0:16], pmv[:, :, cv], small.t[:, 0:16], ALU.add, [pm, small], [modT[cv]])
                ts(modT[cv].t[:, 8:16], modT[cv].t[:, 8:16], 1.0, None, ALU.add, None, [modT[cv]], [modT[cv]])
            for cv in range(2):
                g = Ft[6 + cv]
                bcast_rows(gate_bc[cv], gate_bc[cv].t, g.t[0:1, :], 1024, [g])
            f = Ft[8]
            load(f, f.t[0:1, :], lng[l].rearrange("(o n) -> o n", o=1))
            bcast_rows(lng_bc, lng_bc.t, f.t[0:1, :], 1024, [f])
            f = Ft[9]
            load(f, f.t[0:1, :], lnb[l].rearrange("(o n) -> o n", o=1))
            bcast_rows(lnb_bc, lnb_bc.t, f.t[0:1, :], 1024, [f])
            f = Ft[7]
            load(f, f.t[0:1, 0:128], kvg[l].rearrange("(o n) -> o n", o=1))
            bcast_rows(kvg_bc, kvg_bc.t, f.t[0:1, 0:128], 128, [f])
            if l == 0:
                mset(lbT.t[:], 0.0, [lbT])
                mset(omlT.t[:], 1.0, [omlT])
            else:
                for d in range(2):
                    col_form(small.t[:, 16:20], lbl[d, 0].rearrange("(c p) -> c p", p=128), 4, 128, small)
                    col_form(small.t[:, 20:24], lbl[d, 1].rearrange("(c p) -> c p", p=128), 4, 128, small)
                    tt(small.t[:, 24:28], small.t[:, 20:24], small.t[:, 16:20], ALU.subtract, [small], [small])
                    act(lbT.t[:, d, :], small.t[:, 24:28], AF.Sigmoid, [small], [lbT])
                ts(omlT.t[:], lbT.t[:], -1.0, 1.0, ALU.mult, ALU.add, [lbT], [omlT])
            ts(omlT.t[:], omlT.t[:], 0.5, None, ALU.mult, None, [omlT], [omlT])
            tt(lbT.t[:], lbT.t[:], omlT.t[:], ALU.add, [lbT, omlT], [lbT])
            col_form(hgnT.t[:, 0:1], hgn[l].rearrange("(c p) -> c p", p=128), 1, 128, hgnT)
            col_form(qngT.t[:, 0:2], qng[l].rearrange("(c p) -> c p", p=128), 2, 128, qngT)
            for k in range(3):
                col_form(cwT.t[:, k, :], convw[l, k].rearrange("(c p) -> c p", p=128), 4, 128, cwT)
            for kc in range(2):
                f = Ft[8 + kc]
                load(f, f.t[:, 0:768], wqb[l, kc * 128:(kc + 1) * 128, :])
                cp(wqb_b.t[:, kc, :], f.t[:, 0:768], [f], [wqb_b])
                for h in range(4):
                    o = h * 192 + 128
                    cp(wqb_sw.t[:, kc, h, 0:32], f.t[:, o + 32:o + 64], [f], [wqb_sw])
                    cp(wqb_sw.t[:, kc, h, 32:64], f.t[:, o:o + 32], [f], [wqb_sw])
            f = Ft[7]
            load(f, f.t[:], wkvb[l])
            cp(wkvb_b.t[:], f.t[:], [f], [wkvb_b])
            for h in range(4):
                p = ps()
                tr(p.t[:, 0:128], f.t[:, h * 256:h * 256 + 128], idf.t[:], [f, idf], [p])
                cp(wkT.t[:, h, :], p.t[:, 0:128], [p], [wkT])
            for c in range(0, 8 * NVAR * 64, 1024):
                cw = min(1024, 8 * NVAR * 64 - c)
                f = Ft[4 + (c // 1024) % 4]
                load(f, f.t[:, 0:cw], natab[l, :, c:c + cw], q="sync")
                act(EBt.t[:, c:c + cw], f.t[:, 0:cw], AF.Exp, [f], [EBt])

        class Job:
            pass

        jobs = []
        for j in range(3):
            J = Job()
            J.ctx = j > 0
            J.T = TP if J.ctx else TS
            J.nb = J.T // TB
            J.cv = 1 if J.ctx else 0
            J.pi = j - 1
            J.x0 = (xp[j - 1] if J.ctx else xs)
            J.ymid = (s_yp[j - 1] if J.ctx else s_ys)
            J.yout = (y_p[j - 1] if J.ctx else y_s)
            jobs.append(J)

        def make_h(J, l, blk, halo=False, rope=False):
            src = J.x0 if l == 0 else J.ymid
            t0 = blk * TB
            cv = J.cv
            for k in range(TB // 128):
                xt = xin[k % 2]
                load(xt, xt.t[:], src[t0 + k * 128:t0 + (k + 1) * 128, :], src=(None if l == 0 else r_ymid[k % 2]))
                for g in range(2):
                    p = ps()
                    for j in range(4):
                        dch = g * 4 + j
                        tr(p.t[:, j * 128:(j + 1) * 128], xt.t[:, dch * 128:(dch + 1) * 128], idf.t[:], [xt, idf], [p])
                    for j in range(4):
                        dch = g * 4 + j
                        act(hT.t[:, dch, k * 128:(k + 1) * 128], p.t[:, j * 128:(j + 1) * 128], AF.Identity,
                            [p, modT[cv]], [hT], bias=modT[cv].t[:, dch:dch + 1], scale=modT[cv].t[:, 8 + dch:9 + dch])
            if halo:
                lo = t0 - 1
                hi = t0 + TB
                mset(xhalo.t[0:2, :], 0.0, [xhalo])
                if lo >= 0:
                    load(xhalo, xhalo.t[0:1, :], src[lo:lo + 1, :], src=(None if l == 0 else r_ymid[1]))
                if hi < J.T:
                    load(xhalo, xhalo.t[1:2, :], src[hi:hi + 1, :], src=(None if l == 0 else r_ymid[0]))
                p = ps()
                for dch in range(8):
                    tr(p.t[:, dch * 2:dch * 2 + 2], xhalo.t[0:2, dch * 128:(dch + 1) * 128], idf.t[0:2, 0:2],
                       [xhalo, idf], [p])
                for dch in range(8):
                    act(hTh.t[:, dch, :], p.t[:, dch * 2:dch * 2 + 2], AF.Identity, [p, modT[cv]], [hTh],
                        bias=modT[cv].t[:, dch:dch + 1], scale=modT[cv].t[:, 8 + dch:9 + dch])
            if rope:
                load(cosT, cosT.t[:, 0:TB], c_cos[:, t0:t0 + TB])
                load(cosT, cosT.t[:, TB:2 * TB], c_sin[:, t0:t0 + TB])

        def load_h(J, l, blk):
            t0 = blk * TB
            load(hT, hT.t[:], s_hT[:, :, t0:t0 + TB].rearrange("c p t -> p c t"), src=r_hT)
            mset(hTh.t[:], 0.0, [hTh])
            lo, hi = t0 - 1, t0 + TB
            if lo >= 0:
                dma("gpsimd", hTh, hTh.t[:, :, 0:1], s_hT[:, :, lo:lo + 1].rearrange("c p t -> p c t"), R=[r_hT], W=[hTh], slow=True)
            if hi < J.T:
                dma("gpsimd", hTh, hTh.t[:, :, 1:2], s_hT[:, :, hi:hi + 1].rearrange("c p t -> p c t"), R=[r_hT], W=[hTh], slow=True)
            if not J.ctx:
                load(cosT, cosT.t[:, 0:TB], c_cos[:, t0:t0 + TB])
                load(cosT, cosT.t[:, TB:2 * TB], c_sin[:, t0:t0 + TB])

        def seg_fm(wv, wt, c, p, ncol=128, rhs_t=None, ntok=TB, pout=None):
            rt = hT if rhs_t is None else rhs_t
            out = (p.t[0:ncol, 0:ntok] if pout is None else pout)
            for kc in range(8):
                mm(out, wv[:, kc, c * 128:c * 128 + ncol], rt.t[:, kc, 0:ntok], kc == 0, kc == 7, [wt, rt], [p])

        def seg_tm(wv, wt, k, p, ncol, brow_off):
            for kc in range(8):
                mm(p.t[:, 0:ncol], hT.t[:, kc, k * 128:(k + 1) * 128], wv[:, kc, 0:ncol], kc == 0, False, [wt, hT], [p])
            mm(p.t[:, 0:ncol], onb.t[0:1, :], brow.t[0:1, brow_off:brow_off + ncol], False, True, [onb, brow], [p])

        def hgrn_block(J, l, blk, dirn, o_dst):
            Q, G, K, Bc, Dd, E = Ft[0], Ft[1], Ft[2], Ft[3], Ft[4], Ft[5]
            qt, kt, vtok = Ht[0], Ht[1], Ht[2]
            nch = TB // 64
            sgn = 1.0 if dirn == 0 else -1.0

            def v3(t):
                return t.t[:].rearrange("p (h n) -> p h n", n=TB)

            def v4(t):
                return t.t[:].rearrange("p (c l) -> p c l", l=64)
            wt, wv = wload(s_win[l], 8, SEG["a_q"], 512)
            for h in range(4):
                p = ps()
                seg_fm(wv, wt, h, p)
                bq_ = bcol(SEG["a_q"]) + h
                act(v3(E)[:, h, :], p.t[:, 0:TB], AF.Tanh, [p, binH], [E], bias=binH.t[:, bq_:bq_ + 1], scale=0.5)
                stt(v3(Dd)[:, h, :], p.t[:, 0:TB], binT.t[:, bq_:bq_ + 1], v3(E)[:, h, :], ALU.add, ALU.mult, [p, binT, E], [Dd])
                stt(v3(Q)[:, h, :], p.t[:, 0:TB], binT.t[:, bq_:bq_ + 1], v3(Dd)[:, h, :], ALU.add, ALU.add, [p, binT, Dd], [Q])
            fo = SEG["a_ff"] if dirn == 0 else SEG["a_fb"]
            wt, wv = wload(s_win[l], 8, fo, 512)
            for h in range(4):
                p = ps()
                seg_fm(wv, wt, h, p)
                act(v3(G)[:, h, :], p.t[:, 0:TB], AF.Tanh, [p, binH], [G],
                    bias=binH.t[:, bcol(fo) + h:bcol(fo) + h + 1], scale=0.5)
                ts(v3(G)[:, h, :], v3(G)[:, h, :], omlT.t[:, dirn, h:h + 1], lbT.t[:, dirn, h:h + 1],
                   ALU.mult, ALU.add, [G, omlT, lbT], [G])
            ts(K.t[:], G.t[:], -1.0, 1.0, ALU.mult, ALU.add, [G], [K])
            act(G.t[:], G.t[:], AF.Ln, [G], [G])
            for h in range(4):
                scan(v3(Bc)[:, h, :], onf.t[:, 0:TB], v3(G)[:, h, :], [onf, G], [Bc])
            tt(G.t[:], Bc.t[:], G.t[:], ALU.subtract, [Bc, G], [G])
            C = Bc if dirn == 0 else G
            tt(v4(Dd), v4(C), v4(C)[:, :, 32:33].to_broadcast([128, 4 * nch, 64]), ALU.subtract, [C], [Dd])
            act(E.t[:], Dd.t[:], AF.Exp, [Dd], [E], scale=sgn)
            stt(qt.t[:], Q.t[:], 0.5, E.t[:], ALU.mult, ALU.mult, [Q, E], [qt])
            act(E.t[:], Dd.t[:], AF.Exp, [Dd], [E], scale=-sgn)
            tt(kt.t[:], K.t[:], E.t[:], ALU.mult, [K, E], [kt])
            Cs = v4(G)[:, :, 0]
            Ce = v4(Bc)[:, :, 63]
            Rr = v4(C)[:, :, 32]
            n4 = 4 * nch
            if dirn == 0:
                tt(sc3.t[:, 0, 0:n4], Rr, Cs, ALU.subtract, [G, Bc], [sc3])
                tt(sc3.t[:, 1, 0:n4], Ce, Rr, ALU.subtract, [G, Bc], [sc3])
            else:
                tt(sc3.t[:, 0, 0:n4], Ce, Rr, ALU.subtract, [G, Bc], [sc3])
                tt(sc3.t[:, 1, 0:n4], Rr, Cs, ALU.subtract, [G, Bc], [sc3])
            tt(sc3.t[:, 2, 0:n4], Ce, Cs, ALU.subtract, [G, Bc], [sc3])
            act(sc3.t[:], sc3.t[:], AF.Exp, [sc3], [sc3])
            wt, wv = wload(s_win[l], 8, SEG["a_i"], 512)
            vv = vtok.t[:].rearrange("p (k n) -> p k n", n=512)
            for k in range(TB // 128):
                p = ps()
                seg_tm(wv, wt, k, p, 512, 0)
                acp(vv[:, k, :], p.t[:, 0:512], [p], [vtok])
            qv = v3(qt)
            kv_ = v3(kt)
            od = v3(o_dst)
            mask = mfw if dirn == 0 else mbw
            am = amF if dirn == 0 else amB
            mk = mask.t[:].rearrange("p (h n) -> p h n", n=64)
            tiles = range(TB // 128) if dirn == 0 else range(TB // 128 - 1, -1, -1)
            for k in tiles:
                pa = ps()
                for h in range(4):
                    mm(pa.t[:, h * 128:(h + 1) * 128], kv_[:, h, k * 128:(k + 1) * 128], qv[:, h, k * 128:(k + 1) * 128],
                       True, True, [kt, qt], [pa])
                pav = pa.t[:].rearrange("p (h n) -> p h n", n=128)
                for p0 in (0, 64):
                    pieces = ((0, 0, 64), (32, 32, 64)) if dirn == 0 else ((0, 0, 32), (32, 0, 64))
                    for (ro, ca, cb) in pieces:
                        tt(am.t[p0 + ro:p0 + ro + 32, :, ca:cb], pav[p0 + ro:p0 + ro + 32, :, p0 + ca:p0 + cb],
                           mk[p0 + ro:p0 + ro + 32, :, ca:cb], ALU.mult, [pa, mask], [am])
                for h in range(4):
                    tr(PSB.t[:, h * 128:(h + 1) * 128], kv_[:, h, k * 128:(k + 1) * 128], idb.t[:], [kt, idb], [PSB])
                cp(ktok.t[:].rearrange("p h n -> p (h n)"), PSB.t[:, 0:512], [PSB], [ktok])
                halves = (0, 64) if dirn == 0 else (64, 0)
                for p0 in halves:
                    c = k * 2 + p0 // 64
                    for h in range(4):
                        act(Sb16.t[:, h, :], Sst.t[:, h, :], AF.Identity, [SstH[h], sc3], [Sb16H[h]],
                            scale=sc3.t[:, 0, h * nch + c:h * nch + c + 1])
                    po = ps()
                    for h in range(4):
                        mm(po.t[:, h * 64:(h + 1) * 64], vv[p0:p0 + 64, k, h * 128:(h + 1) * 128], am.t[p0:p0 + 64, h, :],
                           True, False, [vtok, am], [po])
                        mm(po.t[:, h * 64:(h + 1) * 64], Sb16.t[:, h, :], qv[:, h, c * 64:(c + 1) * 64],
                           False, True, [Sb16H[h], qt], [po])
                    cp(od[:, :, c * 64:(c + 1) * 64], po.t[:, 0:256].rearrange("p (h n) -> p h n", n=64), [po], [o_dst])
                    pq = ps()
                    for h in range(4):
                        mm(pq.t[:, h * 128:(h + 1) * 128], ktok.t[p0:p0 + 64, h, :], vv[p0:p0 + 64, k, h * 128:(h + 1) * 128],
                           True, True, [ktok, vtok], [pq])
                    for h in range(4):
                        act(tmpS.t[:, h, :], pq.t[:, h * 128:(h + 1) * 128], AF.Identity, [pq, sc3], [tmpSH[h]],
                            scale=sc3.t[:, 1, h * nch + c:h * nch + c + 1])
                        stt(Sst.t[:, h, :], Sst.t[:, h, :], sc3.t[:, 2, h * nch + c:h * nch + c + 1], tmpS.t[:, h, :],
                            ALU.mult, ALU.add, [SstH[h], sc3, tmpSH[h]], [SstH[h]])

        def hgrn_init_state(J, l, dirn):
            if J.ctx:
                mset(Sst.t[:], 0.0, [Sst] + SstH)
            else:
                dma("gpsimd", Sst, Sst.t[:], st_h[l, dirn].rearrange("h d e -> d h e"), W=[Sst] + SstH)

        def hgrn_store_state(J, l, dirn):
            if J.ctx:
                dma("gpsimd", Sst, o_st[J.pi, l, dirn].rearrange("h d e -> d h e"), Sst.t[:], R=[Sst] + SstH)

        na_i = [0]

        na_pend = [None]

        def na_stage_b(ctx):
            (keys, Pb, nq, p0, out_ap, den_ap, out_t, den_t) = ctx
            nt = len(keys)
            po = ps()
            for i, (kap, vap, R) in enumerate(keys):
                mm(po.t[:, 0:nq], vap, Pb.t[:, i * nq:(i + 1) * nq], i == 0, i == nt - 1, R + [Pb], [po])
            for i in range(nt):
                mm(po.t[:, nq:2 * nq], onb.t[:], Pb.t[:, i * nq:(i + 1) * nq], i == 0, i == nt - 1, [onb, Pb], [po])
            cp(out_ap, po.t[p0:p0 + 64, 0:nq], [po], [out_t])
            cp(den_ap, po.t[p0:p0 + 64, nq:2 * nq], [po], [den_t])

        def na_flush():
            if na_pend[0] is not None:
                na_stage_b(na_pend[0])
                na_pend[0] = None

        def na_unit(qap, nq, keys, eb, nloc, p0, den_ap, out_ap, out_t, qR, den_t=None):
            nt = len(keys)
            pS = ps()
            for i, (kap, vap, R) in enumerate(keys):
                mm(pS.t[:, i * nq:(i + 1) * nq], kap, qap, True, True, R + qR, [pS])
            Pb = (Ht[3], Ht[10], Ht[11])[na_i[0] % 3]
            na_i[0] += 1
            act(Pb.t[:, 0:nt * nq], pS.t[:, 0:nt * nq], AF.Exp, [pS], [Pb], scale=0.125)
            if eb is not None:
                pv = Pb.t[:, 0:nloc * nq].rearrange("p (t q) -> p t q", q=nq)
                tt(pv, pv, eb, ALU.mult, [Pb, EBt], [Pb])
            if na_pend[0] is not None:
                na_stage_b(na_pend[0])
            na_pend[0] = (keys, Pb, nq, p0, out_ap, den_ap, out_t, den_t)

        def merge_gen(l, n, outn):
            ov = outn.t[:].rearrange("p (c n) -> p c n", n=TB)
            mv = mT.t[:].rearrange("p (c n) -> p c n", n=TB)
            bsrc = s_wbr[l][n * 512:(n + 1) * 512, :].rearrange("(k p) n -> p k n", p=128)
            wsrc = s_win[l].rearrange("(k p) n -> p k n", p=128)
            bv = WMb.t[:].rearrange("p (k n) -> p k n", n=128)
            wv = WMw.t[:].rearrange("p (k n) -> p k n", n=256)
            alone = (n == 3)
            if alone:
                ball = WS[0].t[:, 0:4 * 1024].rearrange("p (k n) -> p k n", n=1024)
                dma("sync", WS[0], ball, bsrc[:, :, 0:1024], R=[r_w], W=[WS[0]])
                wv1 = WS[1].t[:, 0:8 * 256].rearrange("p (k n) -> p k n", n=256)
            def ld_w(dc_):
                hw = WMwH[dc_ % 2]
                c1 = SEG["mrg"] + n * 1024 + dc_ * 128
                dma("sync", hw, hw.v, wsrc[:, :, c1:c1 + 128], R=[r_w], W=[hw, WMw])

            def ld_b(dc_):
                dma("sync", WMb, bv, bsrc[:, :, dc_ * 128:(dc_ + 1) * 128], R=[r_w], W=[WMb])
            if not alone:
                ld_w(0)
                ld_b(0)
            for q4 in range(4):
                c0 = SEG["mrg"] + n * 1024 + q4 * 256
                if alone:
                    wt_t, wvq = (WS[1], wv1) if q4 % 2 == 1 else (WMw, wv)
                    dma("sync", wt_t, wvq, wsrc[:, :, c0:c0 + 256], R=[r_w], W=[wt_t] + (WMwH if wt_t is WMw else []))
                for j in range(2):
                    dc = q4 * 2 + j
                    pA = ps()
                    if alone:
                        for wc in range(4):
                            mm(pA.t[:, 0:TB], ball[:, wc, dc * 128:(dc + 1) * 128], ov[:, wc, :], wc == 0, wc == 3, [WS[0], outn], [pA])
                        wuse_t, wuse_v, wj = wt_t, wvq, j
                    else:
                        if dc + 1 < 8:
                            ld_w(dc + 1)
                        for wc in range(4):
                            mm(pA.t[:, 0:TB], bv[:, wc, :], ov[:, wc, :], wc == 0, wc == 3, [WMb, outn], [pA])
                        if dc + 1 < 8:
                            ld_b(dc + 1)
                        wuse_t, wuse_v, wj = WMwH[dc % 2], WMwH[dc % 2].v, 0
                    pB = ps()
                    for kc in range(8):
                        mm(pB.t[:, 0:TB], wuse_v[:, kc, wj * 128:(wj + 1) * 128], hT.t[:, kc, 0:TB], kc == 0, kc == 7, [wuse_t, hT], [pB])
                    sg = (Ft[7], Ft[5])[dc % 2]
                    bc = bcol(SEG["mrg"]) + n * 8 + dc
                    act(sg.t[:, 0:TB], pB.t[:, 0:TB], AF.Tanh, [pB, binH], [sg], bias=binH.t[:, bc:bc + 1], scale=0.5)
                    if n == 0:
                        stt(mv[:, dc, :], sg.t[:, 0:TB], 1.0, pA.t[:, 0:TB], ALU.add, ALU.mult, [pA, sg], [mT])
                    else:
                        stt(sg.t[:, 0:TB], sg.t[:, 0:TB], 1.0, pA.t[:, 0:TB], ALU.add, ALU.mult, [pA, sg], [sg])
                        tt(mv[:, dc, :], mv[:, dc, :], sg.t[:, 0:TB], ALU.add, [mT, sg], [mT])
                    yield

        def run_pair(g1, g2, r):
            live1, live2 = True, True
            while live1 or live2:
                if live1:
                    try:
                        next(g1)
                    except StopIteration:
                        live1 = False
                for _ in range(r if live1 else 1000000):
                    if not live2:
                        break
                    try:
                        next(g2)
                    except StopIteration:
                        live2 = False

        def silu2(p, bc, sg, npart=128):
            act(sg.t[0:npart, TB:2 * TB], p.t[0:npart, 0:TB], AF.Tanh, [p, binH], [sg], bias=binH.t[0:npart, bc:bc + 1], scale=0.5)
            stt(sg.t[0:npart, 2 * TB:3 * TB], p.t[0:npart, 0:TB], binT.t[0:npart, bc:bc + 1], sg.t[0:npart, TB:2 * TB],
                ALU.add, ALU.mult, [p, binT, sg], [sg])
            stt(sg.t[0:npart, 0:TB], p.t[0:npart, 0:TB], binT.t[0:npart, bc:bc + 1], sg.t[0:npart, 2 * TB:3 * TB],
                ALU.add, ALU.add, [p, binT, sg], [sg])

        def rstd_from(pt, n, scale_, out_t):
            ts(out_t.t[:, 0:n], pt.t[:, 0:n], scale_, EPS, ALU.mult, ALU.add, [pt], [out_t])
            act(out_t.t[:, 0:n], out_t.t[:, 0:n], AF.Sqrt, [out_t], [out_t])
            recip(out_t.t[:, 0:n], out_t.t[:, 0:n], [out_t], [out_t])

        def kv_pass(J, l, after_block=None):
            nkt = J.T // 128
            for blk in range(J.nb - 1, -1, -1):
                t0 = blk * TB
                make_h(J, l, blk, rope=not J.ctx)
                store(hT, s_hT[:, :, t0:t0 + TB].rearrange("c p t -> p c t"), hT.t[:], dst=r_hT, nodep=True)
                if after_block is not None:
                    after_block(blk)
                kvs = (dbg or {}).get("kvstop", 9)
                if kvs <= 1:
                    continue
                wt, wv = wload(s_win[l], 8, SEG["b_k"], 512)
                kst = Ht[4]
                ksv = kst.t[:].rearrange("p (c n) -> p c n", n=TB)
                for c in range(4):
                    p = ps()
                    seg_fm(wv, wt, c, p)
                    bc = bcol(SEG["b_k"]) + c
                    act(ksv[:, c, :], p.t[:, 0:TB], AF.Identity, [p, binT], [kst], bias=binT.t[:, bc:bc + 1])
                store(kst, s_nk[:, :, t0:t0 + TB].rearrange("c p t -> p c t"), ksv, dst=r_nk, nodep=True)
                if J.ctx:
                    for k in range(TB // 128):
                        p = ps()
                        seg_tm(wv, wt, k, p, 512, 1216)
                        f = Ft[8 + k % 2]
                        cp(f.t[:, 0:512], p.t[:, 0:512], [p], [f])
                        store(f, o_nk[J.pi, l, t0 + k * 128:t0 + (k + 1) * 128, :], f.t[:, 0:512])
                if kvs <= 2:
                    continue
                wt, wv = wload(s_win[l], 8, SEG["b_v"], 512)
                vst = Ht[5]
                vsv = vst.t[:].rearrange("p (k n) -> p k n", n=512)
                for k in range(TB // 128):
                    p = ps()
                    seg_tm(wv, wt, k, p, 512, 512)
                    if J.ctx:
                        f = Ft[8 + k % 2]
                        cp(f.t[:, 0:512], p.t[:, 0:512], [p], [f])
                        acp(vsv[:, k, :], f.t[:, 0:512], [f], [vst])
                        store(f, o_nv[J.pi, l, t0 + k * 128:t0 + (k + 1) * 128, :], f.t[:, 0:512])
                    else:
                        acp(vsv[:, k, :], p.t[:, 0:512], [p], [vst])
                store(vst, s_nv[t0:t0 + TB, :].rearrange("(k p) n -> p k n", p=128), vsv, dst=r_nv, nodep=True)
                if kvs <= 3:
                    continue
                wt, wv = wload(s_win[l], 8, SEG["c_kvd"], 192)
                for k in range(TB // 128):
                    kt_i = blk * (TB // 128) + k
                    p = ps()
                    seg_tm(wv, wt, k, p, 128, 1024)
                    junk = Ft[6]
                    act(junk.t[:, 0:128], p.t[:, 0:128], AF.Square, [p], [junk])
                    P.op("vector", (lambda o, i: (lambda e: e.reduce_sum(out=o, in_=i, axis=mybir.AxisListType.X)))(small.t[:, 32:33], junk.t[:, 0:128]),
                         B([junk]), B([small]))
                    ts(small.t[:, 33:34], small.t[:, 32:33], 1.0 / 128, EPS, ALU.mult, ALU.add, [small], [small])
                    act(small.t[:, 34:35], small.t[:, 33:34], AF.Sqrt, [small], [small])
                    recip(small.t[:, 35:36], small.t[:, 34:35], [small], [small])
                    f = Ft[8 + k % 2]
                    stt(f.t[:, 512:640], p.t[:, 0:128], small.t[:, 35:36], kvg_bc.t[:], ALU.mult, ALU.mult,
                        [p, small, kvg_bc], [f])
                    if J.ctx:
                        store(f, o_ckv[J.pi, l, t0 + k * 128:t0 + (k + 1) * 128, :], f.t[:, 512:640])
                    cp(vlat.t[:, kt_i, :], f.t[:, 512:640], [f], [vlat])
                    tr(PSB.t[:, 0:128], vlat.t[:, kt_i, :], idb.t[:], [vlat, idb], [PSB])
                    cp(klat.t[:, kt_i * 128:(kt_i + 1) * 128], PSB.t[:, 0:128], [PSB], [klat])
                    if J.ctx:
                        p2 = ps()
                        for kc in range(8):
                            mm(p2.t[:, 0:64], hT.t[:, kc, k * 128:(k + 1) * 128], wv[:, kc, 128:192], kc == 0, False,
                               [wt, hT], [p2])
                        mm(p2.t[:, 0:64], onb.t[0:1, :], brow.t[0:1, 1152:1216], False, True, [onb, brow], [p2])
                        f2 = Ft[8 + (k + 1) % 2]
                        cp(f2.t[:, 512:576], p2.t[:, 0:64], [p2], [f2])
                        store(f2, o_kpe[J.pi, l, t0 + k * 128:t0 + (k + 1) * 128, :], f2.t[:, 512:576])
                p = ps()
                seg_fm(wv, wt, 1, p, ncol=64)
                bc = bcol(SEG["c_kpe"])
                kdst = kpeT.t[0:64, t0:t0 + TB]
                if J.ctx:
                    act(kdst, p.t[0:64, 0:TB], AF.Identity, [p, binT], [kpeT], bias=binT.t[0:64, bc:bc + 1])
                else:
                    cp(wkpe_sw.t[:, :, 0:32], wv[:, :, 160:192], [wt], [wkpe_sw])
                    cp(wkpe_sw.t[:, :, 32:64], wv[:, :, 128:160], [wt], [wkpe_sw])
                    p2 = ps()
                    for kc in range(8):
                        mm(p2.t[0:64, 0:TB], wkpe_sw.t[:, kc, :], hT.t[:, kc, :], kc == 0, kc == 7, [wkpe_sw, hT], [p2])
                    t1 = Ft[6]
                    t2 = Ft[7]
                    stt(t1.t[0:64, 0:TB], p.t[0:64, 0:TB], binT.t[0:64, bc:bc + 1], cosT.t[:, 0:TB], ALU.add, ALU.mult,
                        [p, binT, cosT], [t1])
                    stt(t2.t[0:64, 0:TB], p2.t[0:64, 0:TB], bkpesw.t[:, 0:1], cosT.t[:, TB:2 * TB], ALU.add, ALU.mult,
                        [p2, bkpesw, cosT], [t2])
                    tt(kdst, t1.t[0:64, 0:TB], t2.t[0:64, 0:TB], ALU.add, [t1, t2], [kpeT])
            if not J.ctx:
                for k in range(2):
                    kt_i = nkt + k
                    f = Ft[8 + k]
                    load(f, f.t[:, 0:128], cckv[l, k * 128:(k + 1) * 128, :])
                    load(f, f.t[:, 128:192], ckpe[l, k * 128:(k + 1) * 128, :])
                    cp(vlat.t[:, kt_i, :], f.t[:, 0:128], [f], [vlat])
                    tr(PSB.t[:, 0:128], vlat.t[:, kt_i, :], idb.t[:], [vlat, idb], [PSB])
                    cp(klat.t[:, kt_i * 128:(kt_i + 1) * 128], PSB.t[:, 0:128], [PSB], [klat])
                    p = ps()
                    tr(p.t[0:64, 0:128], f.t[:, 128:192], idf.t[:], [f, idf], [p])
                    cp(kpeT.t[0:64, kt_i * 128:(kt_i + 1) * 128], p.t[0:64, 0:128], [p], [kpeT])
                    g = Ft[6 + k]
                    load(g, g.t[:, 0:512], cnk[l, k * 128:(k + 1) * 128, :])
                    p = ps()
                    for c in range(4):
                        tr(p.t[:, c * 128:(c + 1) * 128], g.t[:, c * 128:(c + 1) * 128], idf.t[:], [g, idf], [p])
                    cp(ckT.t[:, :, k * 128:(k + 1) * 128], p.t[:, 0:512].rearrange("p (c n) -> p c n", n=128), [p], [ckT])
                    load(g, g.t[:, 512:1024], cnv[l, k * 128:(k + 1) * 128, :])
                    cp(cvt_.t[:, k, :], g.t[:, 512:1024], [g], [cvt_])

        def bwd_block(J, l, blk):
            if True:
                t0 = blk * TB
                OB = Ft[8]
                hgrn_block(J, l, blk, 1, OB)
                store(OB, s_ob[:, :, t0:t0 + TB].rearrange("h p t -> p h t"),
                      OB.t[:].rearrange("p (h n) -> p h n", n=TB), dst=r_ob, nodep=True)

        def main_pass(J, l):
            hgrn_init_state(J, l, 0)
            nkt = J.T // 128 + (0 if J.ctx else 2)
            for blk in range(J.nb):
                t0 = blk * TB
                load_h(J, l, blk)
                OF = Ft[8]
                OBt = Ft[9]
                load(OBt, OBt.t[:].rearrange("p (h n) -> p h n", n=TB),
                     s_ob[:, :, t0:t0 + TB].rearrange("h p t -> p h t"), src=r_ob)
                hgrn_block(J, l, blk, 0, OF)
                tt(OF.t[:], OF.t[:], OBt.t[:], ALU.add, [OF, OBt], [OF])
                sq = Ft[0]
                act(sq.t[:], OF.t[:], AF.Square, [OF], [sq])
                outa, outb, outc, outd = Ht[6], Ht[0], Ht[6], Ht[0]
                wt, wv = wload(s_win[l], 8, SEG["a_g"], 512)
                rs = Ft[1]
                for h in range(4):
                    p = ps()
                    mm(p.t[:, 0:TB], onf.t[:, 0:128], sq.t[:, h * TB:(h + 1) * TB], True, True, [onf, sq], [p])
                    ts(rs.t[:, h * TB:(h + 1) * TB], p.t[:, 0:TB], 1.0 / 128, EPS, ALU.mult, ALU.add, [p], [rs])
                act(rs.t[:], rs.t[:], AF.Sqrt, [rs], [rs])
                recip(rs.t[:], rs.t[:], [rs], [rs])
                for h in range(4):
                    stt(rs.t[:, h * TB:(h + 1) * TB], OF.t[:, h * TB:(h + 1) * TB], hgnT.t[:, 0:1], rs.t[:, h * TB:(h + 1) * TB],
                        ALU.mult, ALU.mult, [OF, hgnT, rs], [rs])
                for h in range(4):
                    p2 = ps()
                    seg_fm(wv, wt, h, p2)
                    sg = Ft[2]
                    bc = bcol(SEG["a_g"]) + h
                    silu2(p2, bc, sg)
                    stt(outa.t[:, h * TB:(h + 1) * TB], sg.t[:, 0:TB], 0.5, rs.t[:, h * TB:(h + 1) * TB], ALU.mult, ALU.mult,
                        [rs, sg], [outa])
                def brB():
                    wt, wv = wload(s_win[l], 8, SEG["b_q"], 512)
                    qn_ = Ht[7]
                    qnv = qn_.t[:].rearrange("p (c n) -> p c n", n=TB)
                    for c in range(4):
                        p = ps()
                        seg_fm(wv, wt, c, p)
                        bc = bcol(SEG["b_q"]) + c
                        act(qnv[:, c, :], p.t[:, 0:TB], AF.Identity, [p, binT], [qn_], bias=binT.t[:, bc:bc + 1])
                    ON = Ft[0]
                    onv = ON.t[:].rearrange("p (c n) -> p c n", n=TB)
                    DEN = Ft[3]
                    dnv = DEN.t[:].rearrange("p (c n) -> p c n", n=TB)
                    if J.ctx:
                        ta, tb_ = 0, 1
                    else:
                        r0 = blk * 4
                        ta = min(max(r0 - 4, 0), 56) // 2
                        tb_ = (min(max(r0 + 3 - 4, 0), 56) + 7) // 2
                    nwt = tb_ - ta + 1
                    KW = sbkw
                    kwv = KW.t[:, 0:4 * nwt * 128].rearrange("p (c n) -> p c n", n=nwt * 128)
                    load(KW, kwv, s_nk[:, :, ta * 128:(tb_ + 1) * 128].rearrange("c p t -> p c t"), src=r_nk)
                    VW = sbvw
                    vwv = VW.t[:, 0:nwt * 512].rearrange("p (k n) -> p k n", n=512)
                    load(VW, vwv, s_nv[ta * 128:(tb_ + 1) * 128, :].rearrange("(k p) n -> p k n", p=128), src=r_nv)
                    if J.ctx:
                        for h in range(8):
                            c, p0 = h // 2, 64 * (h % 2)
                            keys = [(kwv[p0:p0 + 64, c, i * 128:(i + 1) * 128], vwv[:, i, c * 128:(c + 1) * 128], [KW, VW])
                                    for i in range(2)]
                            na_unit(qnv[p0:p0 + 64, c, :], TB, keys, None, 0, p0, dnv[p0:p0 + 64, c, :], onv[p0:p0 + 64, c, :], ON, [qn_],
                                    den_t=DEN)
                            yield
                    else:
                        ebv = EBt.t[:].rearrange("p (h v q) -> p h v q", v=NVAR, q=64)
                        for rr in range(4):
                            r = r0 + rr
                            rs_ = min(max(r - 4, 0), 56)
                            i0, i1 = rs_ // 2, (rs_ + 7) // 2
                            nloc = i1 - i0 + 1
                            for h in range(8):
                                c, p0 = h // 2, 64 * (h % 2)
                                keys = [(kwv[p0:p0 + 64, c, (i - ta) * 128:(i - ta + 1) * 128],
                                         vwv[:, i - ta, c * 128:(c + 1) * 128], [KW, VW]) for i in range(i0, i1 + 1)]
                                keys += [(ckT.t[p0:p0 + 64, c, i * 128:(i + 1) * 128], cvt_.t[:, i, c * 128:(c + 1) * 128],
                                          [ckT, cvt_]) for i in range(2)]
                                if nloc == 5:
                                    eb = ebv[:, h, 14:19, :]
                                else:
                                    v0 = (2 * i0 - r) + 7
                                    eb = ebv[:, h, v0:v0 + 7:2, :]
                                na_unit(qnv[p0:p0 + 64, c, rr * 64:(rr + 1) * 64], 64, keys, eb, nloc, p0,
                                        dnv[p0:p0 + 64, c, rr * 64:(rr + 1) * 64],
                                        onv[p0:p0 + 64, c, rr * 64:(rr + 1) * 64], ON, [qn_], den_t=DEN)
                                yield
                    na_flush()
                    recip(DEN.t[:], DEN.t[:], [DEN], [DEN])
                    tt(ON.t[:], ON.t[:], DEN.t[:], ALU.mult, [ON, DEN], [ON])
                    yield
                    wt, wv = wload(s_win[l], 8, SEG["b_g"], 512)
                    for c in range(4):
                        p = ps()
                        seg_fm(wv, wt, c, p)
                        sg = Ft[2]
                        bc = bcol(SEG["b_g"]) + c
                        silu2(p, bc, sg)
                        stt(outb.t[:, c * TB:(c + 1) * TB], sg.t[:, 0:TB], 0.5, onv[:, c, :], ALU.mult, ALU.mult, [ON, sg], [outb])
                    yield
                def brC():
                    wt, wv = wload(s_win[l], 8, SEG["c_qd"], 256)
                    qd = Ft[0]
                    qdv = qd.t[:, 0:2 * TB].rearrange("p (c n) -> p c n", n=TB)
                    sq = Ft[1]
                    pss = ps()
                    for c in range(2):
                        p = ps()
                        seg_fm(wv, wt, c, p)
                        bc = bcol(SEG["c_qd"]) + c
                        act(qdv[:, c, :], p.t[:, 0:TB], AF.Identity, [p, binT], [qd], bias=binT.t[:, bc:bc + 1])
                        act(sq.t[:, c * TB:(c + 1) * TB], qdv[:, c, :], AF.Square, [qd], [sq])
                        mm(pss.t[:, 0:TB], onf.t[:, 0:128], sq.t[:, c * TB:(c + 1) * TB], c == 0, c == 1, [onf, sq], [pss])
                    rs = Ft[2]
                    rstd_from(pss, TB, 1.0 / 256, rs)
                    qnb = Ht[7]
                    qnbv = qnb.t[:, 0:2 * TB].rearrange("p (c n) -> p c n", n=TB)
                    for c in range(2):
                        stt(qnbv[:, c, :], qdv[:, c, :], qngT.t[:, c:c + 1], rs.t[:, 0:TB], ALU.mult, ALU.mult,
                            [qd, qngT, rs], [qnb])
                    qlat = Ht[8]
                    qlv = qlat.t[:].rearrange("p (h n) -> p h n", n=TB)
                    qrope = Ht[9]
                    qrv = qrope.t[0:64, :].rearrange("p (h n) -> p h n", n=TB)
                    for h in range(4):
                        p = ps()
                        for kc in range(2):
                            mm(p.t[:, 0:TB], wqb_b.t[:, kc, h * 192:h * 192 + 128], qnbv[:, kc, :], kc == 0, kc == 1,
                               [wqb_b, qnb], [p])
                        qno = Ht[10]
                        cp(qno.t[:, 0:TB], p.t[:, 0:TB], [p], [qno])
                        p2 = ps()
                        mm(p2.t[:, 0:TB], wkT.t[:, h, :], qno.t[:, 0:TB], True, True, [wkT, qno], [p2])
                        acp(qlv[:, h, :], p2.t[:, 0:TB], [p2], [qlat])
                        p3 = ps()
                        for kc in range(2):
                            mm(p3.t[0:64, 0:TB], wqb_b.t[:, kc, h * 192 + 128:h * 192 + 192], qnbv[:, kc, :], kc == 0, kc == 1,
                               [wqb_b, qnb], [p3])
                        if J.ctx:
                            cp(qrv[:, h, :], p3.t[0:64, 0:TB], [p3], [qrope])
                        else:
                            p4 = ps()
                            for kc in range(2):
                                mm(p4.t[0:64, 0:TB], wqb_sw.t[:, kc, h, :], qnbv[:, kc, :], kc == 0, kc == 1, [wqb_sw, qnb], [p4])
                            t1 = Ft[3]
                            t2 = Ft[4]
                            tt(t1.t[0:64, 0:TB], p3.t[0:64, 0:TB], cosT.t[:, 0:TB], ALU.mult, [p3, cosT], [t1])
                            tt(t2.t[0:64, 0:TB], p4.t[0:64, 0:TB], cosT.t[:, TB:2 * TB], ALU.mult, [p4, cosT], [t2])
                            tt(qrv[:, h, :], t1.t[0:64, 0:TB], t2.t[0:64, 0:TB], ALU.add, [t1, t2], [qrope])
                        yield
                    wt, wv = wload(s_win[l], 8, SEG["c_g"], 512)
                    msc = 192.0 ** -0.5
                    for hp in range(2):
                        pacc_o = psacc()
                        pacc_d = psacc()
                        q2 = qlat.t[:, hp * 2 * TB:(hp * 2 + 2) * TB]
                        r2 = qrope.t[0:64, hp * 2 * TB:(hp * 2 + 2) * TB]

                        def mla_b(kt_i, Pb_):
                            first = (kt_i == 0)
                            last = (kt_i == nkt - 1)
                            mm(pacc_o.t[:, 0:2 * TB], vlat.t[:, kt_i, :], Pb_.t[:, 0:2 * TB], first, last, [vlat, Pb_], [pacc_o])
                            mm(pacc_d.t[:, 0:2 * TB], onb.t[:], Pb_.t[:, 0:2 * TB], first, last, [onb, Pb_], [pacc_d])
                        pend = None
                        for kt_i in range(nkt):
                            pS = ps()
                            mm(pS.t[:, 0:2 * TB], klat.t[:, kt_i * 128:(kt_i + 1) * 128], q2, True, False, [klat, qlat], [pS])
                            mm(pS.t[:, 0:2 * TB], kpeT.t[0:64, kt_i * 128:(kt_i + 1) * 128], r2, False, True, [kpeT, qrope], [pS])
                            Pb = (Ht[10], Ht[11], Ht[3])[kt_i % 3]
                            act(Pb.t[:, 0:2 * TB], pS.t[:, 0:2 * TB], AF.Exp, [pS], [Pb], scale=msc)
                            if pend is not None:
                                mla_b(*pend)
                            pend = (kt_i, Pb)
                            yield
                        mla_b(*pend)
                        rc = Ft[3]
                        recip(rc.t[:, 0:2 * TB], pacc_d.t[:, 0:2 * TB], [pacc_d], [rc])
                        oln = Ht[7]
                        tt(oln.t[:, TB * 2:TB * 4], pacc_o.t[:, 0:2 * TB], rc.t[:, 0:2 * TB], ALU.mult, [pacc_o, rc], [oln])
                        for j in range(2):
                            h = hp * 2 + j
                            p = ps()
                            mm(p.t[:, 0:TB], wkvb_b.t[:, h * 256 + 128:h * 256 + 256], oln.t[:, TB * (2 + j):TB * (3 + j)], True, True,
                               [wkvb_b, oln], [p])
                            p2 = ps()
                            seg_fm(wv, wt, h, p2)
                            sg = Ft[4]
                            bc = bcol(SEG["c_g"]) + h
                            silu2(p2, bc, sg)
                            stt(outc.t[:, h * TB:(h + 1) * TB], sg.t[:, 0:TB], 0.5, p.t[:, 0:TB], ALU.mult, ALU.mult, [p, sg], [outc])
                            yield
                    yield
                def brD():
                    wtc, wvc = wload(s_win[l], 8, SEG["d_c"], 512)
                    wtx, wvx = wload(s_win[l], 8, SEG["d_x"], 512)
                    uTs = [Ft[0], Ft[4]]
                    uvs = [t_.t[:, 0:2 * (TB + 4)].rearrange("p (c n) -> p c n", n=TB + 4) for t_ in uTs]
                    cvb = Ft[1]
                    for c in range(4):
                        uT = uTs[c // 2]
                        uvc = uvs[c // 2][:, c % 2, :]
                        bcc = bcol(SEG["d_c"]) + c
                        bcx = bcol(SEG["d_x"]) + c
                        p = ps()
                        seg_fm(wvc, wtc, c, p)
                        tc_ = Ft[2]
                        act(tc_.t[:, 0:TB], p.t[:, 0:TB], AF.Identity, [p, binT], [tc_], bias=binT.t[:, bcc:bcc + 1])
                        p2 = ps()
                        seg_fm(wvx, wtx, c, p2)
                        stt(uvc[:, 1:TB + 1], p2.t[:, 0:TB], binT.t[:, bcx:bcx + 1], tc_.t[:, 0:TB], ALU.add, ALU.mult,
                            [p2, binT, tc_], [uT])
                        p3 = ps()
                        seg_fm(wvc, wtc, c, p3, rhs_t=hTh, ntok=2)
                        act(tc_.t[:, 512:514], p3.t[:, 0:2], AF.Identity, [p3, binT], [tc_], bias=binT.t[:, bcc:bcc + 1])
                        p4 = ps()
                        seg_fm(wvx, wtx, c, p4, rhs_t=hTh, ntok=2)
                        hal = Ft[3]
                        stt(hal.t[:, 0:2], p4.t[:, 0:2], binT.t[:, bcx:bcx + 1], tc_.t[:, 512:514], ALU.add, ALU.mult,
                            [p4, binT, tc_], [hal])
                        if t0 == 0:
                            mset(uvc[:, 0:1], 0.0, [uT])
                        else:
                            cp(uvc[:, 0:1], hal.t[:, 0:1], [hal], [uT])
                        if t0 + TB >= J.T:
                            mset(uvc[:, TB + 1:TB + 2], 0.0, [uT])
                        else:
                            cp(uvc[:, TB + 1:TB + 2], hal.t[:, 1:2], [hal], [uT])
                        ts(cvb.t[:, c * TB:(c + 1) * TB], uvc[:, 0:TB], cwT.t[:, 0, c:c + 1], None, ALU.mult, None, [uT, cwT], [cvb])
                        stt(cvb.t[:, c * TB:(c + 1) * TB], uvc[:, 1:TB + 1], cwT.t[:, 1, c:c + 1], cvb.t[:, c * TB:(c + 1) * TB],
                            ALU.mult, ALU.add, [uT, cwT, cvb], [cvb])
                        stt(cvb.t[:, c * TB:(c + 1) * TB], uvc[:, 2:TB + 2], cwT.t[:, 2, c:c + 1], cvb.t[:, c * TB:(c + 1) * TB],
                            ALU.mult, ALU.add, [uT, cwT, cvb], [cvb])
                        yield
                    wtb, wvb = wload(s_win[l], 8, SEG["d_b"], 512)
                    wtg, wvg = wload(s_win[l], 8, SEG["d_g"], 512)
                    for c in range(4):
                        p = ps()
                        seg_fm(wvb, wtb, c, p)
                        bcb = bcol(SEG["d_b"]) + c
                        t2 = Ft[2]
                        stt(t2.t[:, 0:TB], p.t[:, 0:TB], binT.t[:, bcb:bcb + 1], cvb.t[:, c * TB:(c + 1) * TB], ALU.add, ALU.mult,
                            [p, binT, cvb], [t2])
                        p2 = ps()
                        seg_fm(wvg, wtg, c, p2)
                        sg = Ft[3]
                        bcg = bcol(SEG["d_g"]) + c
                        silu2(p2, bcg, sg)
                        stt(outd.t[:, c * TB:(c + 1) * TB], sg.t[:, 0:TB], 0.5, t2.t[:, 0:TB], ALU.mult, ALU.mult, [t2, sg], [outd])
                        yield
                    yield
                stopn = (dbg or {}).get('stop', 9)
                run_pair(merge_gen(l, 0, outa), brB() if stopn >= 1 else iter(()), 4)
                if stopn == 0:
                    return
                run_pair(merge_gen(l, 1, outb), brC() if stopn >= 2 else iter(()), 9)
                if stopn == 1:
                    return
                run_pair(merge_gen(l, 2, outc), brD() if stopn >= 3 else iter(()), 1)
                if stopn == 2:
                    return
                run_pair(merge_gen(l, 3, outd), iter(()), 1)
                if stopn == 3:
                    return
                ts(Ht[4].t[:, 0:1024], mT.t[:, 0:1024], 0.5, None, ALU.mult, None, [mT], [Ht[4]])
                act(Ht[5].t[:, 0:1024], mT.t[:, 1024:2048], AF.Identity, [mT], [Ht[5]], scale=0.5)
                mxvs = [Ht[4].t[:].rearrange("p (c n) -> p c n", n=TB), Ht[5].t[:].rearrange("p (c n) -> p c n", n=TB)]
                src = J.x0 if l == 0 else J.ymid
                wo = [wload(s_wout[l], 8, hf * 512, 512) for hf in range(2)]
                for k in range(TB // 128):
                    xt = xin[k % 2]
                    load(xt, xt.t[:], src[t0 + k * 128:t0 + (k + 1) * 128, :], src=(None if l == 0 else r_ymid[k % 2]))
                    yt = Ft[2 + k % 2]
                    for hf in range(2):
                        p = ps()
                        wt, wv = wo[hf]
                        for kc in range(8):
                            mm(p.t[:, 0:512], mxvs[kc // 4][:, kc % 4, k * 128:(k + 1) * 128], wv[:, kc, :], kc == 0, kc == 7,
                               [wt, Ht[4 + kc // 4]], [p])
                        tt(yt.t[:, hf * 512:(hf + 1) * 512], p.t[:, 0:512], gate_bc[J.cv].t[:, hf * 512:(hf + 1) * 512], ALU.mult,
                           [p, gate_bc[J.cv]], [yt])
                    stt(yt.t[:], xt.t[:], ALPHA, yt.t[:], ALU.mult, ALU.add, [xt, yt], [yt])
                    for hf in range(2):
                        P.op("vector", (lambda o, i: (lambda e: e.bn_stats(out=o, in_=i)))(bnst.t[:, hf, :], yt.t[:, hf * 512:(hf + 1) * 512]),
                             B([yt]), B([bnst]))
                    P.op("vector", (lambda o, i: (lambda e: e.bn_aggr(out=o, in_=i)))(bnag.t[:, 0:2], bnst.t[:].rearrange("p a b -> p (a b)")),
                         B([bnst]), B([bnag]))
                    ts(small.t[:, 40:41], bnag.t[:, 1:2], 1.0, EPS, ALU.mult, ALU.add, [bnag], [small])
                    act(small.t[:, 41:42], small.t[:, 40:41], AF.Sqrt, [small], [small])
                    recip(small.t[:, 42:43], small.t[:, 41:42], [small], [small])
                    ts(yt.t[:], yt.t[:], bnag.t[:, 0:1], small.t[:, 42:43], ALU.subtract, ALU.mult, [yt, bnag, small], [yt])
                    tt(yt.t[:], yt.t[:], lng_bc.t[:], ALU.mult, [yt, lng_bc], [yt])
                    tt(yt.t[:], yt.t[:], lnb_bc.t[:], ALU.add, [yt, lnb_bc], [yt])
                    if l == DEPTH - 1:
                        store(yt, J.yout[t0 + k * 128:t0 + (k + 1) * 128, :], yt.t[:])
                    else:
                        store(yt, J.ymid[t0 + k * 128:t0 + (k + 1) * 128, :], yt.t[:], dst=r_ymid[k % 2], nodep=True)
            hgrn_store_state(J, l, 0)

        sbkw = sb("KW", [128, 4 * 6 * 128], BF16)
        sbvw = sb("VW", [128, 6 * 512], BF16)
        mT = sb("mT", [128, 8 * TB], F32)
        WMb = sb("WMb", [128, 4 * 128], BF16)
        WMw = sb("WMw", [128, 8 * 256], BF16)

        class Half_:
            pass
        WMwH = []
        for i_ in range(2):
            hobj = Half_()
            hobj.t = WMw.t
            hobj.b = Buf("WMwH%d" % i_)
            hobj.chan = None
            hobj.v = WMw.t[:, i_ * 1024:(i_ + 1) * 1024].rearrange("p (k n) -> p k n", n=128)
            WMwH.append(hobj)

        dbg_ = dbg or {}
        for l in range(dbg_.get("layers", DEPTH)):
            if dbg_.get("params", True):
                layer_params(l)
            for ji, J in enumerate(jobs):
                if ji not in dbg_.get("jobs", (0, 1, 2)):
                    continue
                if "kv" in dbg_.get("passes", "kv,bwd,main"):
                    hgrn_init_state(J, l, 1)
                    kv_pass(J, l, after_block=(lambda blk, J=J, l=l: bwd_block(J, l, blk)))
                    hgrn_store_state(J, l, 1)
                if "main" in dbg_.get("passes", "kv,bwd,main"):
                    main_pass(J, l)
        P.wait_all("gpsimd", [t.b for t in [Sst] + SstH + Ft + Ht])
        P.emit()
    return nc


_NC_CACHE = {}
_DBG = None


def _const_inputs():
    idf = np.eye(128, dtype=np.float32)
    idb = np.eye(128).astype(ml_dtypes.bfloat16)
    t = np.arange(TS)
    row = (t // 64).astype(np.float32)
    col = (t % 64).astype(np.float32)
    inv = (1.0 / (np.float32(10000.0) ** (np.arange(16, dtype=np.float32) / np.float32(16)))).astype(np.float32)
    ang = np.concatenate([row[:, None] * inv, col[:, None] * inv], axis=-1).astype(np.float32)
    cos = np.cos(ang).astype(np.float32).T
    sin = np.sin(ang).astype(np.float32).T
    c_cos = np.ascontiguousarray(np.concatenate([cos, cos], axis=0))
    c_sin = np.ascontiguousarray(np.concatenate([-sin, sin], axis=0))
    p = np.arange(128)[:, None] % 64
    tt_ = np.arange(256)[None, :] % 64
    c_mf = (p <= tt_).astype(ml_dtypes.bfloat16)
    c_mb = (p >= tt_).astype(ml_dtypes.bfloat16)
    return dict(c_idf=idf, c_idb=idb, c_cos=c_cos, c_sin=c_sin, c_mf=c_mf, c_mb=c_mb)


def _na_table(na_rpb):
    rpb = np.asarray(na_rpb, dtype=np.float32)
    pairs = [(v - 7, v - 6, True, True) for v in range(14)]
    pairs += [(-5, -4, False, True), (-3, -2, True, True), (-1, 0, True, True), (1, 2, True, True), (3, 4, True, False)]
    kc = np.arange(64)[:, None]
    qc = np.arange(64)[None, :]
    cs = np.clip(qc - 8, 0, 48)
    win = (kc >= cs) & (kc < cs + 16)
    cidx = np.clip(kc - qc + 15, 0, 30)
    tab = np.full((DEPTH, 128, 8, NVAR, 64), -30000.0, dtype=np.float32)
    for v, (da, db, va, vb) in enumerate(pairs):
        for which, (dr, ok) in enumerate(((da, va), (db, vb))):
            if not ok:
                continue
            g = rpb[:, :, dr + 7, :][:, :, cidx]
            g = np.where(win[None, None], g, np.float32(-30000.0))
            tab[:, which * 64:(which + 1) * 64, :, v, :] = np.transpose(g, (0, 2, 1, 3))
    return np.ascontiguousarray(tab.reshape(DEPTH, 128, 8 * NVAR * 64))


def kernel(x_prompt, x_sample, state_hgrn, cache_na_k, cache_na_v, cache_mla_ckv, cache_mla_kpe,
           c, c_ctx, w_ada, b_ada, w_in, b_in, hg_lb_logits, hg_norm_g, na_rpb, mla_qnorm_g,
           mla_w_qb, mla_kvnorm_g, mla_w_kvb, conv_w, w_branch, w_out, ln_g, ln_b):
    f = lambda a: np.ascontiguousarray(np.asarray(a, dtype=np.float32))
    if "nc" not in _NC_CACHE:
        _NC_CACHE["nc"] = build_program(_DBG)
    nc = _NC_CACHE["nc"]
    shared = dict(
        w_ada=f(w_ada), b_ada=f(b_ada), w_in=f(w_in), b_in=f(b_in), lbl=f(hg_lb_logits), hgn=f(hg_norm_g),
        natab=_na_table(na_rpb), qng=f(mla_qnorm_g), wqb=f(mla_w_qb), kvg=f(mla_kvnorm_g), wkvb=f(mla_w_kvb),
        convw=f(conv_w), wbr=f(w_branch).reshape(DEPTH, 2048, D), wout=f(w_out), lng=f(ln_g), lnb=f(ln_b))
    shared.update(_const_inputs())
    x_prompt = f(x_prompt)
    x_sample = f(x_sample)
    state_hgrn = f(state_hgrn)
    cnk = f(cache_na_k).reshape(8, DEPTH, 256, 512)
    cnv = f(cache_na_v).reshape(8, DEPTH, 256, 512)
    cckv = f(cache_mla_ckv)
    ckpe = f(cache_mla_kpe)
    c = f(c)
    c_ctx = f(c_ctx)
    in_maps = []
    for i in range(NCORES):
        m = dict(shared)
        m["xs"] = x_sample[i]
        m["xp"] = x_prompt[2 * i:2 * i + 2]
        m["st_h"] = state_hgrn[i]
        m["cnk"] = cnk[i]
        m["cnv"] = cnv[i]
        m["cckv"] = cckv[i]
        m["ckpe"] = ckpe[i]
        m["cvec"] = np.ascontiguousarray(np.stack([c[i], c_ctx], axis=0))
        in_maps.append(m)
    res = run_bass_kernel_spmd(nc, in_maps, core_ids=list(range(NCORES)))
    R = res.results
    y_p = np.concatenate([np.asarray(r["y_p"], dtype=np.float32) for r in R], axis=0)
    y_s = np.stack([np.asarray(r["y_s"], dtype=np.float32) for r in R], axis=0)
    st = np.concatenate([np.asarray(r["o_st"], dtype=np.float32) for r in R], axis=0)
    nk = np.concatenate([np.asarray(r["o_nk"], dtype=np.float32) for r in R], axis=0).reshape(16, DEPTH, TP, 8, 64)
    nv = np.concatenate([np.asarray(r["o_nv"], dtype=np.float32) for r in R], axis=0).reshape(16, DEPTH, TP, 8, 64)
    ckv = np.concatenate([np.asarray(r["o_ckv"], dtype=np.float32) for r in R], axis=0)
    kpe = np.concatenate([np.asarray(r["o_kpe"], dtype=np.float32) for r in R], axis=0)
    return (y_p, y_s, st, nk, nv, ckv, kpe)
```

```python
import contextlib
import numpy as np
import ml_dtypes
import concourse.bass as bass
import concourse.mybir as mybir
from concourse.bass_utils import run_bass_kernel_spmd

F32 = mybir.dt.float32
BF16 = mybir.dt.bfloat16
AF = mybir.ActivationFunctionType
ALU = mybir.AluOpType

NCORES = 8
D = 1024
DEPTH = 2
TS = 4096
TP = 256
TB = 256
INW = 11712
EPS = 1e-6
ALPHA = (2 * DEPTH) ** 0.25
SEG = dict(a_q=0, a_ff=512, a_fb=1024, a_i=1536, a_g=2048, b_q=2560, b_k=3072, b_v=3584, b_g=4096,
           c_qd=4608, c_kvd=4864, c_kpe=4992, c_g=5056, d_b=5568, d_c=6080, d_x=6592, d_g=7104, mrg=7616)
NVAR = 19
ENGS = ("tensor", "vector", "scalar", "gpsimd", "sync")


def bcol(off):
    if off < 4992:
        return off // 128
    if off == 4992:
        return 39
    return 40 + (off - 5056) // 128


class Buf:
    __slots__ = ("name", "last_w", "readers")

    def __init__(self, name=""):
        self.name = name
        self.last_w = None
        self.readers = {}


class Prog:
    def __init__(self, nc):
        self.nc = nc
        self.ops = {e: [] for e in ENGS}
        self.chan_n = {}
        self.waited = {e: {} for e in ENGS}
        self.dma_chans = []
        self.needed = set()

    def dma_chan(self, name):
        c = "dma:%d" % len(self.dma_chans)
        self.dma_chans.append(c)
        self.chan_n[c] = 0
        return c

    def _deps(self, eng, reads, writes):
        deps = {}

        def add(tok):
            if tok is None:
                return
            c, i = tok
            if c.startswith("dma:"):
                i = self.chan_n[c] - 1
            if deps.get(c, -1) < i:
                deps[c] = i

        for r in reads:
            add(r.last_w)
        for w in writes:
            add(w.last_w)
            for c, i in w.readers.items():
                add((c, i))
        out = []
        wd = self.waited[eng]
        for c, i in deps.items():
            if c == "tensor" and eng == "tensor":
                continue
            if wd.get(c, -1) >= i:
                continue
            wd[c] = i
            out.append((c, i))
            self.needed.add((c, i))
        return out

    def _record(self, tok, reads, writes):
        c, i = tok
        for r in reads:
            if r.readers.get(c, -1) < i:
                r.readers[c] = i
        for w in writes:
            w.last_w = tok
            w.readers = {}

    def op(self, eng, fn, reads=(), writes=()):
        waits = self._deps(eng, reads, writes)
        idx = self.chan_n.get(eng, 0)
        self.chan_n[eng] = idx + 1
        tok = (eng, idx)
        self.ops[eng].append((fn, waits, tok, False))
        self._record(tok, reads, writes)
        return tok

    def dma(self, queue, chan, fn, reads=(), writes=(), nodep_writes=()):
        waits = self._deps(queue, reads, writes)
        idx = self.chan_n[chan]
        self.chan_n[chan] = idx + 1
        tok = (chan, idx)
        self.ops[queue].append((fn, waits, tok, True))
        self._record(tok, reads, list(writes) + list(nodep_writes))
        return tok

    def wait_all(self, eng, bufs):
        waits = self._deps(eng, (), bufs)
        self.ops[eng].append((None, waits, None, False))

    def emit(self):
        nc = self.nc
        with contextlib.ExitStack() as es:
            sems = {}
            for e in ENGS:
                sems[e] = es.enter_context(nc.semaphore("c_" + e))
            for k, c in enumerate(self.dma_chans):
                sems[c] = es.enter_context(nc.semaphore("d%d" % k))
            val = {}
            for e in ENGS:
                k = 0
                for (fn, waits, t, isdma) in self.ops[e]:
                    if t is None or isdma:
                        continue
                    if t in self.needed:
                        k += 1
                        val[t] = k
            block = es.enter_context(nc.Block())

            def run(e, handle):
                for (fn, waits, t, isdma) in self.ops[e]:
                    for (c, i) in waits:
                        if c.startswith("dma:"):
                            handle.wait_ge(sems[c], 16 * (i + 1))
                        else:
                            handle.wait_ge(sems[c], val[(c, i)])
                    if fn is None:
                        continue
                    ins = fn(handle)
                    if isdma:
                        ins.then_inc(sems[t[0]], 16)
                    elif t in self.needed:
                        ins.then_inc(sems[t[0]], 1)

            @block.sync
            def _(h):
                run("sync", h)

            @block.tensor
            def _(h):
                run("tensor", h)

            @block.vector
            def _(h):
                run("vector", h)

            @block.scalar
            def _(h):
                run("scalar", h)

            @block.gpsimd
            def _(h):
                run("gpsimd", h)


class TL:
    def __init__(self, t, name):
        self.t = t
        self.b = Buf(name)
        self.chan = None


class DR:
    def __init__(self, name):
        self.b = Buf(name)


def build_program(dbg=None):
    nc = bass.Bass("TRN2", target_bir_lowering=False)
    P = Prog(nc)

    def din(name, shape, dt=F32):
        return nc.dram_tensor(name, list(shape), dt, kind="ExternalInput").ap()

    def dout(name, shape):
        return nc.dram_tensor(name, list(shape), F32, kind="ExternalOutput").ap()

    def dscr(name, shape, dt):
        return nc.dram_tensor(name, list(shape), dt, kind="Internal").ap()

    xs = din("xs", [TS, D])
    xp = din("xp", [2, TP, D])
    st_h = din("st_h", [DEPTH, 2, 4, 128, 128])
    cnk = din("cnk", [DEPTH, 256, 512])
    cnv = din("cnv", [DEPTH, 256, 512])
    cckv = din("cckv", [DEPTH, 256, 128])
    ckpe = din("ckpe", [DEPTH, 256, 64])
    cvec = din("cvec", [2, D])
    w_ada = din("w_ada", [DEPTH, D, 3 * D])
    b_ada = din("b_ada", [DEPTH, 3 * D])
    w_in = din("w_in", [DEPTH, D, INW])
    b_in = din("b_in", [DEPTH, INW])
    lbl = din("lbl", [2, DEPTH, 512])
    hgn = din("hgn", [DEPTH, 128])
    natab = din("natab", [DEPTH, 128, 8 * NVAR * 64])
    qng = din("qng", [DEPTH, 256])
    wqb = din("wqb", [DEPTH, 256, 768])
    kvg = din("kvg", [DEPTH, 128])
    wkvb = din("wkvb", [DEPTH, 128, 1024])
    convw = din("convw", [DEPTH, 3, 512])
    wbr = din("wbr", [DEPTH, 2048, D])
    wout = din("wout", [DEPTH, D, D])
    lng = din("lng", [DEPTH, D])
    lnb = din("lnb", [DEPTH, D])
    c_idf = din("c_idf", [128, 128])
    c_idb = din("c_idb", [128, 128], BF16)
    c_cos = din("c_cos", [64, TS])
    c_sin = din("c_sin", [64, TS])
    c_mf = din("c_mf", [128, 256], BF16)
    c_mb = din("c_mb", [128, 256], BF16)
    y_p = dout("y_p", [2, TP, D])
    y_s = dout("y_s", [TS, D])
    o_st = dout("o_st", [2, DEPTH, 2, 4, 128, 128])
    o_nk = dout("o_nk", [2, DEPTH, TP, 512])
    o_nv = dout("o_nv", [2, DEPTH, TP, 512])
    o_ckv = dout("o_ckv", [2, DEPTH, TP, 128])
    o_kpe = dout("o_kpe", [2, DEPTH, TP, 64])
    s_win = dscr("s_win", [DEPTH, D, INW], BF16)
    s_wbr = dscr("s_wbr", [DEPTH, 2048, D], BF16)
    s_wout = dscr("s_wout", [DEPTH, D, D], BF16)
    s_ys = dscr("s_ys", [TS, D], F32)
    s_yp = dscr("s_yp", [2, TP, D], F32)
    s_ob = dscr("s_ob", [4, 128, TS], F32)
    s_nk = dscr("s_nk", [4, 128, TS], BF16)
    s_nv = dscr("s_nv", [TS, 512], BF16)
    s_hT = dscr("s_hT", [8, 128, TS], BF16)
    r_hT = DR("hTscr")
    r_w = DR("wscr")
    r_ymid = [DR("ymid0"), DR("ymid1")]
    r_ob = DR("ob")
    r_nk = DR("nk")
    r_nv = DR("nv")

    es = contextlib.ExitStack()
    with es:
        def sb(name, shape, dt):
            return TL(es.enter_context(nc.sbuf_tensor(name, list(shape), dt)), name)

        def pst(name, shape, dt):
            return TL(es.enter_context(nc.psum_tensor(name, list(shape), dt)), name)

        def B(xs_):
            return [x.b for x in xs_]

        def mm(out, lhsT, rhs, start, stop, R, W):
            P.op("tensor", lambda e: e.matmul(out, lhsT=lhsT, rhs=rhs, start=start, stop=stop), B(R), B(W))

        def tr(out, in_, ident, R, W):
            P.op("tensor", lambda e: e.transpose(out, in_, ident), B(R), B(W))

        def act(out, in_, func, R, W, bias=None, scale=None, accum=None):
            kw = {}
            if bias is not None:
                kw["bias"] = bias
            if scale is not None:
                kw["scale"] = scale
            if accum is not None:
                kw["accum_out"] = accum
            P.op("scalar", lambda e: e.activation(out=out, in_=in_, func=func, **kw), B(R), B(W))

        def tt(out, in0, in1, op, R, W, eng="vector"):
            P.op(eng, lambda e: e.tensor_tensor(out=out, in0=in0, in1=in1, op=op), B(R), B(W))

        def ts(out, in0, s1, s2, op0, op1, R, W, eng="vector"):
            if s2 is None:
                P.op(eng, lambda e: e.tensor_scalar(out=out, in0=in0, scalar1=s1, scalar2=None, op0=op0), B(R), B(W))
            else:
                P.op(eng, lambda e: e.tensor_scalar(out=out, in0=in0, scalar1=s1, scalar2=s2, op0=op0, op1=op1),
                     B(R), B(W))

        def stt(out, in0, scalar, in1, op0, op1, R, W, eng="vector"):
            P.op(eng, lambda e: e.scalar_tensor_tensor(out=out, in0=in0, scalar=scalar, in1=in1, op0=op0, op1=op1),
                 B(R), B(W))

        def cp(out, in_, R, W, eng="vector"):
            P.op(eng, lambda e: e.tensor_copy(out=out, in_=in_), B(R), B(W))

        def acp(out, in_, R, W):
            P.op("scalar", lambda e: e.copy(out=out, in_=in_), B(R), B(W))

        def mset(ap, v, W, eng="vector"):
            P.op(eng, lambda e: e.memset(ap, v), (), B(W))

        def recip(out, in_, R, W):
            P.op("vector", lambda e: e.reciprocal(out=out, in_=in_), B(R), B(W))

        def scan(out, d0, d1, R, W):
            P.op("vector", lambda e: e.tensor_tensor_scan(out=out, data0=d0, data1=d1, initial=0.0,
                                                          op0=ALU.mult, op1=ALU.add), B(R), B(W))

        def dma(q, tile, out, in_, R=(), W=(), ND=(), slow=False):
            if tile.chan is None:
                tile.chan = {}
            if q not in tile.chan:
                tile.chan[q] = P.dma_chan(tile.b.name + q)
            if slow:
                P.dma(q, tile.chan[q], lambda e: e.dma_start(out=out, in_=in_, allow_slow_non_contiguous=True), B(R), B(W), B(ND))
            else:
                P.dma(q, tile.chan[q], lambda e: e.dma_start(out=out, in_=in_), B(R), B(W), B(ND))

        def load(tile, out, in_, src=None, q="gpsimd"):
            dma(q, tile, out, in_, R=([src] if src is not None else []), W=[tile])

        def store(tile, out, in_, dst=None, q="gpsimd", nodep=False):
            if dst is None:
                dma(q, tile, out, in_, R=[tile])
            elif nodep:
                dma(q, tile, out, in_, R=[tile], ND=[dst])
            else:
                dma(q, tile, out, in_, R=[tile], W=[dst])

        PSR = [pst("psr%d" % i, [128, 512], F32) for i in range(5)]
        PSA = [pst("psa%d" % i, [128, 512], F32) for i in range(2)]
        PSB = pst("psb", [128, 1024], BF16)
        ring = [0]

        def ps():
            t = PSR[ring[0] % 5]
            ring[0] += 1
            return t
        acc_i = [0]

        def psacc():
            t = PSA[acc_i[0] % 2]
            acc_i[0] += 1
            return t

        idf = sb("idf", [128, 128], F32)
        idb = sb("idb", [128, 128], BF16)
        onb = sb("onb", [128, 128], BF16)
        onf = sb("onf", [128, 256], F32)
        mfw = sb("mfw", [128, 256], BF16)
        mbw = sb("mbw", [128, 256], BF16)
        load(idf, idf.t[:], c_idf[:, :])
        load(idb, idb.t[:], c_idb[:, :])
        load(mfw, mfw.t[:], c_mf[:, :])
        load(mbw, mbw.t[:], c_mb[:, :])
        mset(onb.t[:], 1.0, [onb])

        mset(onf.t[:], 1.0, [onf])

        NF = 10
        Ft = [sb("F%d" % i, [128, 1024], F32) for i in range(NF)]
        NH = 12
        Ht = [sb("H%d" % i, [128, 1024], BF16) for i in range(NH)]
        NWS = 2
        WS = [sb("WS%d" % i, [128, 8 * 512], BF16) for i in range(NWS)]
        ws_i = [0]

        cvt_i = [0]

        def convert(src2d, dst2d, rows, cols):
            CW = 1024
            for r0 in range(0, rows, 128):
                for c0 in range(0, cols, CW):
                    cw = min(CW, cols - c0)
                    i = cvt_i[0]
                    cvt_i[0] += 1
                    f = Ft[i % 4]
                    h = Ht[i % 4]
                    load(f, f.t[:, 0:cw], src2d[r0:r0 + 128, c0:c0 + cw], q="sync")
                    if i % 2 == 0:
                        cp(h.t[:, 0:cw], f.t[:, 0:cw], [f], [h])
                    else:
                        acp(h.t[:, 0:cw], f.t[:, 0:cw], [f], [h])
                    store(h, dst2d[r0:r0 + 128, c0:c0 + cw], h.t[:, 0:cw], dst=r_w, q="gpsimd", nodep=True)

        for l in range((dbg or {}).get("cvt_layers", DEPTH)):
            convert(w_in[l], s_win[l], D, INW)
            convert(wbr[l], s_wbr[l], 2048, D)
            convert(wout[l], s_wout[l], D, D)

        P.wait_all("sync", [Ht[i].b for i in range(4)])

        def wload(src2d, nk, c0, ncols):
            t = WS[ws_i[0] % NWS]
            ws_i[0] += 1
            v = t.t[:, 0:nk * ncols].rearrange("p (k n) -> p k n", n=ncols)
            srcv = src2d.rearrange("(k p) n -> p k n", p=128)
            dma("sync", t, v, srcv[:, 0:nk, c0:c0 + ncols], R=[r_w], W=[t])
            return t, v

        binT = sb("binT", [128, 92], F32)
        binH = sb("binH", [128, 92], F32)
        bkpesw = sb("bkpesw", [64, 1], F32)
        brow = sb("brow", [1, 1728], BF16)
        modT = [sb("modT%d" % i, [128, 24], F32) for i in range(2)]
        gate_bc = [sb("gatebc%d" % i, [128, 1024], F32) for i in range(2)]
        lng_bc = sb("lngbc", [128, 1024], F32)
        lnb_bc = sb("lnbbc", [128, 1024], F32)
        kvg_bc = sb("kvgbc", [128, 128], F32)
        lbT = sb("lbT", [128, 2, 4], F32)
        omlT = sb("omlT", [128, 2, 4], F32)
        hgnT = sb("hgnT", [128, 1], F32)
        qngT = sb("qngT", [128, 2], F32)
        cwT = sb("cwT", [128, 3, 4], F32)
        wqb_b = sb("wqb_b", [128, 2, 768], BF16)
        wqb_sw = sb("wqb_sw", [128, 2, 4, 64], BF16)
        wkvb_b = sb("wkvb_b", [128, 1024], BF16)
        wkT = sb("wkT", [128, 4, 128], BF16)
        EBt = sb("EB", [128, 8 * NVAR * 64], BF16)
        scT = sb("scT", [128, 2, 8], F32)
        rowst = Ft[5]
        ckT = sb("ckT", [128, 4, 256], BF16)
        cvt_ = sb("cv", [128, 2, 512], BF16)
        NKT = TS // 128 + 2
        klat = sb("klat", [128, NKT * 128], BF16)
        kpeT = sb("kpeT", [64, NKT * 128], BF16)
        vlat = sb("vlat", [128, NKT, 128], BF16)
        Sst = sb("Sst", [128, 4, 128], F32)
        Sb16 = sb("Sb16", [128, 4, 128], BF16)
        tmpS = sb("tmpS", [128, 4, 128], F32)

        class VW_:
            def __init__(self, name):
                self.b = Buf(name)
        SstH = [VW_("Sst%d" % h) for h in range(4)]
        Sb16H = [VW_("Sb16%d" % h) for h in range(4)]
        tmpSH = [VW_("tmpS%d" % h) for h in range(4)]
        sc3 = sb("sc3", [128, 3, 16], F32)
        amF = sb("amF", [128, 4, 64], BF16)
        amB = sb("amB", [128, 4, 64], BF16)
        mset(amF.t[:], 0.0, [amF])
        mset(amB.t[:], 0.0, [amB])
        ktok = sb("ktok", [128, 4, 128], BF16)
        hT = sb("hT", [128, 8, TB], BF16)
        hTh = sb("hTh", [128, 8, 2], BF16)
        xin = [sb("xin%d" % i, [128, 1024], F32) for i in range(2)]
        xhalo = Ft[7]
        cosT = sb("cosT", [64, 2 * TB], F32)
        sinT = None
        small = sb("small", [128, 64], F32)
        bnst = sb("bnst", [128, 2, 6], F32)
        bnag = sb("bnag", [128, 2], F32)
        wkpe_sw = sb("wkpe_sw", [128, 8, 64], BF16)

        def col_form(dst_ap, src_rows_ap, nrows, ncols, dst_tile):
            load(rowst, rowst.t[0:nrows, 0:ncols], src_rows_ap)
            p = ps()
            tr(p.t[0:ncols, 0:nrows], rowst.t[0:nrows, 0:ncols], idf.t[0:nrows, 0:nrows], [rowst, idf], [p])
            cp(dst_ap, p.t[0:ncols, 0:nrows], [p], [dst_tile])

        def bcast_rows(dst_tile, dst_ap, row_ap_f32, n, R):
            for c0 in range(0, n, 512):
                cw = min(512, n - c0)
                p = ps()
                mm(p.t[:, 0:cw], onf.t[0:1, 0:128], row_ap_f32[:, c0:c0 + cw], True, True, [onf] + R, [p])
                cp(dst_ap[:, c0:c0 + cw], p.t[:, 0:cw], [p], [dst_tile])

        def layer_params(l):
            mset(binT.t[:], 0.0, [binT])
            col_form(binT.t[:, 0:39], b_in[l, 0:4992].rearrange("(c p) -> c p", p=128), 39, 128, binT)
            col_form(binT.t[0:64, 39:40], b_in[l, 4992:5056].rearrange("(c p) -> c p", p=64), 1, 64, binT)
            col_form(binT.t[:, 40:92], b_in[l, 5056:INW].rearrange("(c p) -> c p", p=128), 52, 128, binT)
            ts(binH.t[:], binT.t[:], 0.5, None, ALU.mult, None, [binT], [binH])
            load(bkpesw, bkpesw.t[0:32, :], b_in[l, 5024:5056].rearrange("(p o) -> p o", o=1))
            load(bkpesw, bkpesw.t[32:64, :], b_in[l, 4992:5024].rearrange("(p o) -> p o", o=1))
            for (o, w, d0) in ():
                load(Ft[9], Ft[9].t[0:1, d0 - (0 if d0 < 1024 else 1024):d0 - (0 if d0 < 1024 else 1024) + w] if False else Ft[9 if d0 < 1024 else 8].t[0:1, (d0 % 1024):(d0 % 1024) + w],
                     b_in[l, o:o + w].rearrange("(o n) -> o n", o=1)) if False else None
            browf_parts = ((SEG["a_i"], 512, 0), (SEG["b_v"], 512, 512))
            for (o, w, d0) in browf_parts:
                load(Ft[9], Ft[9].t[0:1, d0:d0 + w], b_in[l, o:o + w].rearrange("(o n) -> o n", o=1))
            cp(brow.t[0:1, 0:1024], Ft[9].t[0:1, 0:1024], [Ft[9]], [brow])
            for (o, w, d0) in ((SEG["c_kvd"], 128, 0), (SEG["c_kpe"], 64, 128), (SEG["b_k"], 512, 192)):
                load(Ft[8], Ft[8].t[0:1, d0:d0 + w], b_in[l, o:o + w].rearrange("(o n) -> o n", o=1))
            cp(brow.t[0:1, 1024:1728], Ft[8].t[0:1, 0:704], [Ft[8]], [brow])
            for cv in range(2):
                load(rowst, rowst.t[0:8, 0:128], cvec[cv].rearrange("(c p) -> c p", p=128))
                p = ps()
                tr(p.t[:, 0:8], rowst.t[0:8, 0:128], idf.t[0:8, 0:8], [rowst, idf], [p])
                act(scT.t[:, cv, :], p.t[:, 0:8], AF.Silu, [p], [scT])
            pm = psacc()
            pg = [psacc(), ps()]
            pg2 = [ps(), ps()]
            for nr in range(3):
                for kc in range(8):
                    f = Ft[4 + (nr * 8 + kc) % 4]
                    load(f, f.t[:], w_ada[l, kc * 128:(kc + 1) * 128, nr * 1024:(nr + 1) * 1024], q="sync")
                    if nr < 2:
                        for j in range(8):
                            for cv in range(2):
                                col = (nr * 8 + j) * 2 + cv
                                mm(pm.t[:, col:col + 1], f.t[:, j * 128:(j + 1) * 128], scT.t[:, cv, kc:kc + 1],
                                   (nr == 0 and kc == 0 and j == 0 and cv == 0), (nr == 1 and kc == 7 and j == 7 and cv == 1),
                                   [f, scT], [pm])
                    else:
                        for cv in range(2):
                            for hf in range(2):
                                pt = pg[cv] if hf == 0 else pg2[cv]
                                mm(pt.t[0:1, 0:512], scT.t[:, cv, kc:kc + 1], f.t[:, hf * 512:(hf + 1) * 512],
                                   kc == 0, kc == 7, [f, scT], [pt])
            f = Ft[8]
            load(f, f.t[0:1, :], b_ada[l, 2048:3072].rearrange("(o n) -> o n", o=1))
            for cv in range(2):
                g = Ft[6 + cv]
                for hf in range(2):
                    pt = pg[cv] if hf == 0 else pg2[cv]
                    tt(g.t[0:1, hf * 512:(hf + 1) * 512], pt.t[0:1, 0:512], f.t[0:1, hf * 512:(hf + 1) * 512],
                       ALU.add, [pt, f], [g])
            col_form(small.t[:, 0:16], b_ada[l, 0:2048].rearrange("(c p) -> c p", p=128), 16, 128, small)
            pmv = pm.t[:, 0:32].rearrange("p (c v) -> p c v", v=2)
            for cv in range(2):
                tt(modT[cv].t[:, 0:16], pmv[:, :, cv], small.t[:, 0:16], ALU.add, [pm, small], [modT[cv]])
                ts(modT[cv].t[:, 8:16], modT[cv].t[:, 8:16], 1.0, None, ALU.add, None, [modT[cv]], [modT[cv]])
            for cv in range(2):
                g = Ft[6 + cv]
                bcast_rows(gate_bc[cv], gate_bc[cv].t, g.t[0:1, :], 1024, [g])
            f = Ft[8]
            load(f, f.t[0:1, :], lng[l].rearrange("(o n) -> o n", o=1))
            bcast_rows(lng_bc, lng_bc.t, f.t[0:1, :], 1024, [f])
            f = Ft[9]
            load(f, f.t[0:1, :], lnb[l].rearrange("(o n) -> o n", o=1))
            bcast_rows(lnb_bc, lnb_bc.t, f.t[0:1, :], 1024, [f])
            f = Ft[7]
            load(f, f.t[0:1, 0:128], kvg[l].rearrange("(o n) -> o n", o=1))
            bcast_rows(kvg_bc, kvg_bc.t, f.t[0:1, 0:128], 128, [f])
            if l == 0:
                mset(lbT.t[:], 0.0, [lbT])
                mset(omlT.t[:], 1.0, [omlT])
            else:
                for d in range(2):
                    col_form(small.t[:, 16:20], lbl[d, 0].rearrange("(c p) -> c p", p=128), 4, 128, small)
                    col_form(small.t[:, 20:24], lbl[d, 1].rearrange("(c p) -> c p", p=128), 4, 128, small)
                    tt(small.t[:, 24:28], small.t[:, 20:24], small.t[:, 16:20], ALU.subtract, [small], [small])
                    act(lbT.t[:, d, :], small.t[:, 24:28], AF.Sigmoid, [small], [lbT])
                ts(omlT.t[:], lbT.t[:], -1.0, 1.0, ALU.mult, ALU.add, [lbT], [omlT])
            ts(omlT.t[:], omlT.t[:], 0.5, None, ALU.mult, None, [omlT], [omlT])
            tt(lbT.t[:], lbT.t[:], omlT.t[:], ALU.add, [lbT, omlT], [lbT])
            col_form(hgnT.t[:, 0:1], hgn[l].rearrange("(c p) -> c p", p=128), 1, 128, hgnT)
            col_form(qngT.t[:, 0:2], qng[l].rearrange("(c p) -> c p", p=128), 2, 128, qngT)
            for k in range(3):
                col_form(cwT.t[:, k, :], convw[l, k].rearrange("(c p) -> c p", p=128), 4, 128, cwT)
            for kc in range(2):
                f = Ft[8 + kc]
                load(f, f.t[:, 0:768], wqb[l, kc * 128:(kc + 1) * 128, :])
                cp(wqb_b.t[:, kc, :], f.t[:, 0:768], [f], [wqb_b])
                for h in range(4):
                    o = h * 192 + 128
                    cp(wqb_sw.t[:, kc, h, 0:32], f.t[:, o + 32:o + 64], [f], [wqb_sw])
                    cp(wqb_sw.t[:, kc, h, 32:64], f.t[:, o:o + 32], [f], [wqb_sw])
            f = Ft[7]
            load(f, f.t[:], wkvb[l])
            cp(wkvb_b.t[:], f.t[:], [f], [wkvb_b])
            for h in range(4):
                p = ps()
                tr(p.t[:, 0:128], f.t[:, h * 256:h * 256 + 128], idf.t[:], [f, idf], [p])
                cp(wkT.t[:, h, :], p.t[:, 0:128], [p], [wkT])
            for c in range(0, 8 * NVAR * 64, 1024):
                cw = min(1024, 8 * NVAR * 64 - c)
                f = Ft[4 + (c // 1024) % 4]
                load(f, f.t[:, 0:cw], natab[l, :, c:c + cw], q="sync")
                act(EBt.t[:, c:c + cw], f.t[:, 0:cw], AF.Exp, [f], [EBt])

        class Job:
            pass

        jobs = []
        for j in range(3):
            J = Job()
            J.ctx = j > 0
            J.T = TP if J.ctx else TS
            J.nb = J.T // TB
            J.cv = 1 if J.ctx else 0
            J.pi = j - 1
            J.x0 = (xp[j - 1] if J.ctx else xs)
            J.ymid = (s_yp[j - 1] if J.ctx else s_ys)
            J.yout = (y_p[j - 1] if J.ctx else y_s)
            jobs.append(J)

        def make_h(J, l, blk, halo=False, rope=False):
            src = J.x0 if l == 0 else J.ymid
            t0 = blk * TB
            cv = J.cv
            for k in range(TB // 128):
                xt = xin[k % 2]
                load(xt, xt.t[:], src[t0 + k * 128:t0 + (k + 1) * 128, :], src=(None if l == 0 else r_ymid[k % 2]))
                for g in range(2):
                    p = ps()
                    for j in range(4):
                        dch = g * 4 + j
                        tr(p.t[:, j * 128:(j + 1) * 128], xt.t[:, dch * 128:(dch + 1) * 128], idf.t[:], [xt, idf], [p])
                    for j in range(4):
                        dch = g * 4 + j
                        act(hT.t[:, dch, k * 128:(k + 1) * 128], p.t[:, j * 128:(j + 1) * 128], AF.Identity,
                            [p, modT[cv]], [hT], bias=modT[cv].t[:, dch:dch + 1], scale=modT[cv].t[:, 8 + dch:9 + dch])
            if halo:
                lo = t0 - 1
                hi = t0 + TB
                mset(xhalo.t[0:2, :], 0.0, [xhalo])
                if lo >= 0:
                    load(xhalo, xhalo.t[0:1, :], src[lo:lo + 1, :], src=(None if l == 0 else r_ymid[1]))
                if hi < J.T:
                    load(xhalo, xhalo.t[1:2, :], src[hi:hi + 1, :], src=(None if l == 0 else r_ymid[0]))
                p = ps()
                for dch in range(8):
                    tr(p.t[:, dch * 2:dch * 2 + 2], xhalo.t[0:2, dch * 128:(dch + 1) * 128], idf.t[0:2, 0:2],
                       [xhalo, idf], [p])
                for dch in range(8):
                    act(hTh.t[:, dch, :], p.t[:, dch * 2:dch * 2 + 2], AF.Identity, [p, modT[cv]], [hTh],
                        bias=modT[cv].t[:, dch:dch + 1], scale=modT[cv].t[:, 8 + dch:9 + dch])
            if rope:
                load(cosT, cosT.t[:, 0:TB], c_cos[:, t0:t0 + TB])
                load(cosT, cosT.t[:, TB:2 * TB], c_sin[:, t0:t0 + TB])

        def load_h(J, l, blk):
            t0 = blk * TB
            load(hT, hT.t[:], s_hT[:, :, t0:t0 + TB].rearrange("c p t -> p c t"), src=r_hT)
            mset(hTh.t[:], 0.0, [hTh])
            lo, hi = t0 - 1, t0 + TB
            if lo >= 0:
                dma("gpsimd", hTh, hTh.t[:, :, 0:1], s_hT[:, :, lo:lo + 1].rearrange("c p t -> p c t"), R=[r_hT], W=[hTh], slow=True)
            if hi < J.T:
                dma("gpsimd", hTh, hTh.t[:, :, 1:2], s_hT[:, :, hi:hi + 1].rearrange("c p t -> p c t"), R=[r_hT], W=[hTh], slow=True)
            if not J.ctx:
                load(cosT, cosT.t[:, 0:TB], c_cos[:, t0:t0 + TB])
                load(cosT, cosT.t[:, TB:2 * TB], c_sin[:, t0:t0 + TB])

        def seg_fm(wv, wt, c, p, ncol=128, rhs_t=None, ntok=TB, pout=None):
            rt = hT if rhs_t is None else rhs_t
            out = (p.t[0:ncol, 0:ntok] if pout is None else pout)
            for kc in range(8):
                mm(out, wv[:, kc, c * 128:c * 128 + ncol], rt.t[:, kc, 0:ntok], kc == 0, kc == 7, [wt, rt], [p])

        def seg_tm(wv, wt, k, p, ncol, brow_off):
            for kc in range(8):
                mm(p.t[:, 0:ncol], hT.t[:, kc, k * 128:(k + 1) * 128], wv[:, kc, 0:ncol], kc == 0, False, [wt, hT], [p])
            mm(p.t[:, 0:ncol], onb.t[0:1, :], brow.t[0:1, brow_off:brow_off + ncol], False, True, [onb, brow], [p])

        def hgrn_block(J, l, blk, dirn, o_dst):
            Q, G, K, Bc, Dd, E = Ft[0], Ft[1], Ft[2], Ft[3], Ft[4], Ft[5]
            qt, kt, vtok = Ht[0], Ht[1], Ht[2]
            nch = TB // 64
            sgn = 1.0 if dirn == 0 else -1.0

            def v3(t):
                return t.t[:].rearrange("p (h n) -> p h n", n=TB)

            def v4(t):
                return t.t[:].rearrange("p (c l) -> p c l", l=64)
            wt, wv = wload(s_win[l], 8, SEG["a_q"], 512)
            for h in range(4):
                p = ps()
                seg_fm(wv, wt, h, p)
                bq_ = bcol(SEG["a_q"]) + h
                act(v3(E)[:, h, :], p.t[:, 0:TB], AF.Tanh, [p, binH], [E], bias=binH.t[:, bq_:bq_ + 1], scale=0.5)
                stt(v3(Dd)[:, h, :], p.t[:, 0:TB], binT.t[:, bq_:bq_ + 1], v3(E)[:, h, :], ALU.add, ALU.mult, [p, binT, E], [Dd])
                stt(v3(Q)[:, h, :], p.t[:, 0:TB], binT.t[:, bq_:bq_ + 1], v3(Dd)[:, h, :], ALU.add, ALU.add, [p, binT, Dd], [Q])
            fo = SEG["a_ff"] if dirn == 0 else SEG["a_fb"]
            wt, wv = wload(s_win[l], 8, fo, 512)
            for h in range(4):
                p = ps()
                seg_fm(wv, wt, h, p)
                act(v3(G)[:, h, :], p.t[:, 0:TB], AF.Tanh, [p, binH], [G],
                    bias=binH.t[:, bcol(fo) + h:bcol(fo) + h + 1], scale=0.5)
                ts(v3(G)[:, h, :], v3(G)[:, h, :], omlT.t[:, dirn, h:h + 1], lbT.t[:, dirn, h:h + 1],
                   ALU.mult, ALU.add, [G, omlT, lbT], [G])
            ts(K.t[:], G.t[:], -1.0, 1.0, ALU.mult, ALU.add, [G], [K])
            act(G.t[:], G.t[:], AF.Ln, [G], [G])
            for h in range(4):
                scan(v3(Bc)[:, h, :], onf.t[:, 0:TB], v3(G)[:, h, :], [onf, G], [Bc])
            tt(G.t[:], Bc.t[:], G.t[:], ALU.subtract, [Bc, G], [G])
            C = Bc if dirn == 0 else G
            tt(v4(Dd), v4(C), v4(C)[:, :, 32:33].to_broadcast([128, 4 * nch, 64]), ALU.subtract, [C], [Dd])
            act(E.t[:], Dd.t[:], AF.Exp, [Dd], [E], scale=sgn)
            stt(qt.t[:], Q.t[:], 0.5, E.t[:], ALU.mult, ALU.mult, [Q, E], [qt])
            act(E.t[:], Dd.t[:], AF.Exp, [Dd], [E], scale=-sgn)
            tt(kt.t[:], K.t[:], E.t[:], ALU.mult, [K, E], [kt])
            Cs = v4(G)[:, :, 0]
            Ce = v4(Bc)[:, :, 63]
            Rr = v4(C)[:, :, 32]
            n4 = 4 * nch
            if dirn == 0:
                tt(sc3.t[:, 0, 0:n4], Rr, Cs, ALU.subtract, [G, Bc], [sc3])
                tt(sc3.t[:, 1, 0:n4], Ce, Rr, ALU.subtract, [G, Bc], [sc3])
            else:
                tt(sc3.t[:, 0, 0:n4], Ce, Rr, ALU.subtract, [G, Bc], [sc3])
                tt(sc3.t[:, 1, 0:n4], Rr, Cs, ALU.subtract, [G, Bc], [sc3])
            tt(sc3.t[:, 2, 0:n4], Ce, Cs, ALU.subtract, [G, Bc], [sc3])
            act(sc3.t[:], sc3.t[:], AF.Exp, [sc3], [sc3])
            wt, wv = wload(s_win[l], 8, SEG["a_i"], 512)
            vv = vtok.t[:].rearrange("p (k n) -> p k n", n=512)
            for k in range(TB // 128):
                p = ps()
                seg_tm(wv, wt, k, p, 512, 0)
                acp(vv[:, k, :], p.t[:, 0:512], [p], [vtok])
            qv = v3(qt)
            kv_ = v3(kt)
            od = v3(o_dst)
            mask = mfw if dirn == 0 else mbw
            am = amF if dirn == 0 else amB
            mk = mask.t[:].rearrange("p (h n) -> p h n", n=64)
            tiles = range(TB // 128) if dirn == 0 else range(TB // 128 - 1, -1, -1)
            for k in tiles:
                pa = ps()
                for h in range(4):
                    mm(pa.t[:, h * 128:(h + 1) * 128], kv_[:, h, k * 128:(k + 1) * 128], qv[:, h, k * 128:(k + 1) * 128],
                       True, True, [kt, qt], [pa])
                pav = pa.t[:].rearrange("p (h n) -> p h n", n=128)
                for p0 in (0, 64):
                    pieces = ((0, 0, 64), (32, 32, 64)) if dirn == 0 else ((0, 0, 32), (32, 0, 64))
                    for (ro, ca, cb) in pieces:
                        tt(am.t[p0 + ro:p0 + ro + 32, :, ca:cb], pav[p0 + ro:p0 + ro + 32, :, p0 + ca:p0 + cb],
                           mk[p0 + ro:p0 + ro + 32, :, ca:cb], ALU.mult, [pa, mask], [am])
                for h in range(4):
                    tr(PSB.t[:, h * 128:(h + 1) * 128], kv_[:, h, k * 128:(k + 1) * 128], idb.t[:], [kt, idb], [PSB])
                cp(ktok.t[:].rearrange("p h n -> p (h n)"), PSB.t[:, 0:512], [PSB], [ktok])
                halves = (0, 64) if dirn == 0 else (64, 0)
                for p0 in halves:
                    c = k * 2 + p0 // 64
                    for h in range(4):
                        act(Sb16.t[:, h, :], Sst.t[:, h, :], AF.Identity, [SstH[h], sc3], [Sb16H[h]],
                            scale=sc3.t[:, 0, h * nch + c:h * nch + c + 1])
                    po = ps()
                    for h in range(4):
                        mm(po.t[:, h * 64:(h + 1) * 64], vv[p0:p0 + 64, k, h * 128:(h + 1) * 128], am.t[p0:p0 + 64, h, :],
                           True, False, [vtok, am], [po])
                        mm(po.t[:, h * 64:(h + 1) * 64], Sb16.t[:, h, :], qv[:, h, c * 64:(c + 1) * 64],
                           False, True, [Sb16H[h], qt], [po])
                    cp(od[:, :, c * 64:(c + 1) * 64], po.t[:, 0:256].rearrange("p (h n) -> p h n", n=64), [po], [o_dst])
                    pq = ps()
                    for h in range(4):
                        mm(pq.t[:, h * 128:(h + 1) * 128], ktok.t[p0:p0 + 64, h, :], vv[p0:p0 + 64, k, h * 128:(h + 1) * 128],
                           True, True, [ktok, vtok], [pq])
                    for h in range(4):
                        act(tmpS.t[:, h, :], pq.t[:, h * 128:(h + 1) * 128], AF.Identity, [pq, sc3], [tmpSH[h]],
                            scale=sc3.t[:, 1, h * nch + c:h * nch + c + 1])
                        stt(Sst.t[:, h, :], Sst.t[:, h, :], sc3.t[:, 2, h * nch + c:h * nch + c + 1], tmpS.t[:, h, :],
                            ALU.mult, ALU.add, [SstH[h], sc3, tmpSH[h]], [SstH[h]])

        def hgrn_init_state(J, l, dirn):
            if J.ctx:
                mset(Sst.t[:], 0.0, [Sst] + SstH)
            else:
                dma("gpsimd", Sst, Sst.t[:], st_h[l, dirn].rearrange("h d e -> d h e"), W=[Sst] + SstH)

        def hgrn_store_state(J, l, dirn):
            if J.ctx:
                dma("gpsimd", Sst, o_st[J.pi, l, dirn].rearrange("h d e -> d h e"), Sst.t[:], R=[Sst] + SstH)

        na_i = [0]

        na_pend = [None]

        def na_stage_b(ctx):
            (keys, Pb, nq, p0, out_ap, den_ap, out_t, den_t) = ctx
            nt = len(keys)
            po = ps()
            for i, (kap, vap, R) in enumerate(keys):
                mm(po.t[:, 0:nq], vap, Pb.t[:, i * nq:(i + 1) * nq], i == 0, i == nt - 1, R + [Pb], [po])
            for i in range(nt):
                mm(po.t[:, nq:2 * nq], onb.t[:], Pb.t[:, i * nq:(i + 1) * nq], i == 0, i == nt - 1, [onb, Pb], [po])
            cp(out_ap, po.t[p0:p0 + 64, 0:nq], [po], [out_t])
            cp(den_ap, po.t[p0:p0 + 64, nq:2 * nq], [po], [den_t])

        def na_flush():
            if na_pend[0] is not None:
                na_stage_b(na_pend[0])
                na_pend[0] = None

        def na_unit(qap, nq, keys, eb, nloc, p0, den_ap, out_ap, out_t, qR, den_t=None):
            nt = len(keys)
            pS = ps()
            for i, (kap, vap, R) in enumerate(keys):
                mm(pS.t[:, i * nq:(i + 1) * nq], kap, qap, True, True, R + qR, [pS])
            Pb = (Ht[3], Ht[10], Ht[11])[na_i[0] % 3]
            na_i[0] += 1
            act(Pb.t[:, 0:nt * nq], pS.t[:, 0:nt * nq], AF.Exp, [pS], [Pb], scale=0.125)
            if eb is not None:
                pv = Pb.t[:, 0:nloc * nq].rearrange("p (t q) -> p t q", q=nq)
                tt(pv, pv, eb, ALU.mult, [Pb, EBt], [Pb])
            if na_pend[0] is not None:
                na_stage_b(na_pend[0])
            na_pend[0] = (keys, Pb, nq, p0, out_ap, den_ap, out_t, den_t)

        def merge_gen(l, n, outn):
            ov = outn.t[:].rearrange("p (c n) -> p c n", n=TB)
            mv = mT.t[:].rearrange("p (c n) -> p c n", n=TB)
            bsrc = s_wbr[l][n * 512:(n + 1) * 512, :].rearrange("(k p) n -> p k n", p=128)
            wsrc = s_win[l].rearrange("(k p) n -> p k n", p=128)
            bv = WMb.t[:].rearrange("p (k n) -> p k n", n=128)
            wv = WMw.t[:].rearrange("p (k n) -> p k n", n=256)
            alone = (n == 3)
            if alone:
                ball = WS[0].t[:, 0:4 * 1024].rearrange("p (k n) -> p k n", n=1024)
                dma("sync", WS[0], ball, bsrc[:, :, 0:1024], R=[r_w], W=[WS[0]])
                wv1 = WS[1].t[:, 0:8 * 256].rearrange("p (k n) -> p k n", n=256)
            for q4 in range(4):
                c0 = SEG["mrg"] + n * 1024 + q4 * 256
                if alone and q4 % 2 == 1:
                    wt_t, wvq = WS[1], wv1
                else:
                    wt_t, wvq = WMw, wv
                dma("sync", wt_t, wvq, wsrc[:, :, c0:c0 + 256], R=[r_w], W=[wt_t])
                for j in range(2):
                    dc = q4 * 2 + j
                    pA = ps()
                    if alone:
                        for wc in range(4):
                            mm(pA.t[:, 0:TB], ball[:, wc, dc * 128:(dc + 1) * 128], ov[:, wc, :], wc == 0, wc == 3, [WS[0], outn], [pA])
                    else:
                        dma("sync", WMb, bv, bsrc[:, :, dc * 128:(dc + 1) * 128], R=[r_w], W=[WMb])
                        for wc in range(4):
                            mm(pA.t[:, 0:TB], bv[:, wc, :], ov[:, wc, :], wc == 0, wc == 3, [WMb, outn], [pA])
                    pB = ps()
                    for kc in range(8):
                        mm(pB.t[:, 0:TB], wvq[:, kc, j * 128:(j + 1) * 128], hT.t[:, kc, 0:TB], kc == 0, kc == 7, [wt_t, hT], [pB])
                    sg = (Ft[7], Ft[5])[dc % 2]
                    bc = bcol(SEG["mrg"]) + n * 8 + dc
                    act(sg.t[:, 0:TB], pB.t[:, 0:TB], AF.Tanh, [pB, binH], [sg], bias=binH.t[:, bc:bc + 1], scale=0.5)
                    if n == 0:
                        stt(mv[:, dc, :], sg.t[:, 0:TB], 1.0, pA.t[:, 0:TB], ALU.add, ALU.mult, [pA, sg], [mT])
                    else:
                        stt(sg.t[:, 0:TB], sg.t[:, 0:TB], 1.0, pA.t[:, 0:TB], ALU.add, ALU.mult, [pA, sg], [sg])
                        tt(mv[:, dc, :], mv[:, dc, :], sg.t[:, 0:TB], ALU.add, [mT, sg], [mT])
                    yield

        def run_pair(g1, g2, r):
            live1, live2 = True, True
            while live1 or live2:
                if live1:
                    try:
                        next(g1)
                    except StopIteration:
                        live1 = False
                for _ in range(r if live1 else 1000000):
                    if not live2:
                        break
                    try:
                        next(g2)
                    except StopIteration:
                        live2 = False

        def silu2(p, bc, sg, npart=128):
            act(sg.t[0:npart, TB:2 * TB], p.t[0:npart, 0:TB], AF.Tanh, [p, binH], [sg], bias=binH.t[0:npart, bc:bc + 1], scale=0.5)
            stt(sg.t[0:npart, 2 * TB:3 * TB], p.t[0:npart, 0:TB], binT.t[0:npart, bc:bc + 1], sg.t[0:npart, TB:2 * TB],
                ALU.add, ALU.mult, [p, binT, sg], [sg])
            stt(sg.t[0:npart, 0:TB], p.t[0:npart, 0:TB], binT.t[0:npart, bc:bc + 1], sg.t[0:npart, 2 * TB:3 * TB],
                ALU.add, ALU.add, [p, binT, sg], [sg])

        def rstd_from(pt, n, scale_, out_t):
            ts(out_t.t[:, 0:n], pt.t[:, 0:n], scale_, EPS, ALU.mult, ALU.add, [pt], [out_t])
            act(out_t.t[:, 0:n], out_t.t[:, 0:n], AF.Sqrt, [out_t], [out_t])
            recip(out_t.t[:, 0:n], out_t.t[:, 0:n], [out_t], [out_t])

        def kv_pass(J, l, after_block=None):
            nkt = J.T // 128
            for blk in range(J.nb - 1, -1, -1):
                t0 = blk * TB
                make_h(J, l, blk, rope=not J.ctx)
                store(hT, s_hT[:, :, t0:t0 + TB].rearrange("c p t -> p c t"), hT.t[:], dst=r_hT, nodep=True)
                if after_block is not None:
                    after_block(blk)
                kvs = (dbg or {}).get("kvstop", 9)
                if kvs <= 1:
                    continue
                wt, wv = wload(s_win[l], 8, SEG["b_k"], 512)
                kst = Ht[4]
                ksv = kst.t[:].rearrange("p (c n) -> p c n", n=TB)
                for c in range(4):
                    p = ps()
                    seg_fm(wv, wt, c, p)
                    bc = bcol(SEG["b_k"]) + c
                    act(ksv[:, c, :], p.t[:, 0:TB], AF.Identity, [p, binT], [kst], bias=binT.t[:, bc:bc + 1])
                store(kst, s_nk[:, :, t0:t0 + TB].rearrange("c p t -> p c t"), ksv, dst=r_nk, nodep=True)
                if J.ctx:
                    for k in range(TB // 128):
                        p = ps()
                        seg_tm(wv, wt, k, p, 512, 1216)
                        f = Ft[8 + k % 2]
                        cp(f.t[:, 0:512], p.t[:, 0:512], [p], [f])
                        store(f, o_nk[J.pi, l, t0 + k * 128:t0 + (k + 1) * 128, :], f.t[:, 0:512])
                if kvs <= 2:
                    continue
                wt, wv = wload(s_win[l], 8, SEG["b_v"], 512)
                vst = Ht[5]
                vsv = vst.t[:].rearrange("p (k n) -> p k n", n=512)
                for k in range(TB // 128):
                    p = ps()
                    seg_tm(wv, wt, k, p, 512, 512)
                    if J.ctx:
                        f = Ft[8 + k % 2]
                        cp(f.t[:, 0:512], p.t[:, 0:512], [p], [f])
                        acp(vsv[:, k, :], f.t[:, 0:512], [f], [vst])
                        store(f, o_nv[J.pi, l, t0 + k * 128:t0 + (k + 1) * 128, :], f.t[:, 0:512])
                    else:
                        acp(vsv[:, k, :], p.t[:, 0:512], [p], [vst])
                store(vst, s_nv[t0:t0 + TB, :].rearrange("(k p) n -> p k n", p=128), vsv, dst=r_nv, nodep=True)
                if kvs <= 3:
                    continue
                wt, wv = wload(s_win[l], 8, SEG["c_kvd"], 192)
                for k in range(TB // 128):
                    kt_i = blk * (TB // 128) + k
                    p = ps()
                    seg_tm(wv, wt, k, p, 128, 1024)
                    junk = Ft[6]
                    act(junk.t[:, 0:128], p.t[:, 0:128], AF.Square, [p], [junk])
                    P.op("vector", (lambda o, i: (lambda e: e.reduce_sum(out=o, in_=i, axis=mybir.AxisListType.X)))(small.t[:, 32:33], junk.t[:, 0:128]),
                         B([junk]), B([small]))
                    ts(small.t[:, 33:34], small.t[:, 32:33], 1.0 / 128, EPS, ALU.mult, ALU.add, [small], [small])
                    act(small.t[:, 34:35], small.t[:, 33:34], AF.Sqrt, [small], [small])
                    recip(small.t[:, 35:36], small.t[:, 34:35], [small], [small])
                    f = Ft[8 + k % 2]
                    stt(f.t[:, 512:640], p.t[:, 0:128], small.t[:, 35:36], kvg_bc.t[:], ALU.mult, ALU.mult,
                        [p, small, kvg_bc], [f])
                    if J.ctx:
                        store(f, o_ckv[J.pi, l, t0 + k * 128:t0 + (k + 1) * 128, :], f.t[:, 512:640])
                    cp(vlat.t[:, kt_i, :], f.t[:, 512:640], [f], [vlat])
                    tr(PSB.t[:, 0:128], vlat.t[:, kt_i, :], idb.t[:], [vlat, idb], [PSB])
                    cp(klat.t[:, kt_i * 128:(kt_i + 1) * 128], PSB.t[:, 0:128], [PSB], [klat])
                    if J.ctx:
                        p2 = ps()
                        for kc in range(8):
                            mm(p2.t[:, 0:64], hT.t[:, kc, k * 128:(k + 1) * 128], wv[:, kc, 128:192], kc == 0, False,
                               [wt, hT], [p2])
                        mm(p2.t[:, 0:64], onb.t[0:1, :], brow.t[0:1, 1152:1216], False, True, [onb, brow], [p2])
                        f2 = Ft[8 + (k + 1) % 2]
                        cp(f2.t[:, 512:576], p2.t[:, 0:64], [p2], [f2])
                        store(f2, o_kpe[J.pi, l, t0 + k * 128:t0 + (k + 1) * 128, :], f2.t[:, 512:576])
                p = ps()
                seg_fm(wv, wt, 1, p, ncol=64)
                bc = bcol(SEG["c_kpe"])
                kdst = kpeT.t[0:64, t0:t0 + TB]
                if J.ctx:
                    act(kdst, p.t[0:64, 0:TB], AF.Identity, [p, binT], [kpeT], bias=binT.t[0:64, bc:bc + 1])
                else:
                    cp(wkpe_sw.t[:, :, 0:32], wv[:, :, 160:192], [wt], [wkpe_sw])
                    cp(wkpe_sw.t[:, :, 32:64], wv[:, :, 128:160], [wt], [wkpe_sw])
                    p2 = ps()
                    for kc in range(8):
                        mm(p2.t[0:64, 0:TB], wkpe_sw.t[:, kc, :], hT.t[:, kc, :], kc == 0, kc == 7, [wkpe_sw, hT], [p2])
                    t1 = Ft[6]
                    t2 = Ft[7]
                    stt(t1.t[0:64, 0:TB], p.t[0:64, 0:TB], binT.t[0:64, bc:bc + 1], cosT.t[:, 0:TB], ALU.add, ALU.mult,
                        [p, binT, cosT], [t1])
                    stt(t2.t[0:64, 0:TB], p2.t[0:64, 0:TB], bkpesw.t[:, 0:1], cosT.t[:, TB:2 * TB], ALU.add, ALU.mult,
                        [p2, bkpesw, cosT], [t2])
                    tt(kdst, t1.t[0:64, 0:TB], t2.t[0:64, 0:TB], ALU.add, [t1, t2], [kpeT])
            if not J.ctx:
                for k in range(2):
                    kt_i = nkt + k
                    f = Ft[8 + k]
                    load(f, f.t[:, 0:128], cckv[l, k * 128:(k + 1) * 128, :])
                    load(f, f.t[:, 128:192], ckpe[l, k * 128:(k + 1) * 128, :])
                    cp(vlat.t[:, kt_i, :], f.t[:, 0:128], [f], [vlat])
                    tr(PSB.t[:, 0:128], vlat.t[:, kt_i, :], idb.t[:], [vlat, idb], [PSB])
                    cp(klat.t[:, kt_i * 128:(kt_i + 1) * 128], PSB.t[:, 0:128], [PSB], [klat])
                    p = ps()
                    tr(p.t[0:64, 0:128], f.t[:, 128:192], idf.t[:], [f, idf], [p])
                    cp(kpeT.t[0:64, kt_i * 128:(kt_i + 1) * 128], p.t[0:64, 0:128], [p], [kpeT])
                    g = Ft[6 + k]
                    load(g, g.t[:, 0:512], cnk[l, k * 128:(k + 1) * 128, :])
                    p = ps()
                    for c in range(4):
                        tr(p.t[:, c * 128:(c + 1) * 128], g.t[:, c * 128:(c + 1) * 128], idf.t[:], [g, idf], [p])
                    cp(ckT.t[:, :, k * 128:(k + 1) * 128], p.t[:, 0:512].rearrange("p (c n) -> p c n", n=128), [p], [ckT])
                    load(g, g.t[:, 512:1024], cnv[l, k * 128:(k + 1) * 128, :])
                    cp(cvt_.t[:, k, :], g.t[:, 512:1024], [g], [cvt_])

        def bwd_block(J, l, blk):
            if True:
                t0 = blk * TB
                OB = Ft[8]
                hgrn_block(J, l, blk, 1, OB)
                store(OB, s_ob[:, :, t0:t0 + TB].rearrange("h p t -> p h t"),
                      OB.t[:].rearrange("p (h n) -> p h n", n=TB), dst=r_ob, nodep=True)

        def main_pass(J, l):
            hgrn_init_state(J, l, 0)
            nkt = J.T // 128 + (0 if J.ctx else 2)
            for blk in range(J.nb):
                t0 = blk * TB
                load_h(J, l, blk)
                OF = Ft[8]
                OBt = Ft[9]
                load(OBt, OBt.t[:].rearrange("p (h n) -> p h n", n=TB),
                     s_ob[:, :, t0:t0 + TB].rearrange("h p t -> p h t"), src=r_ob)
                hgrn_block(J, l, blk, 0, OF)
                tt(OF.t[:], OF.t[:], OBt.t[:], ALU.add, [OF, OBt], [OF])
                sq = Ft[0]
                act(sq.t[:], OF.t[:], AF.Square, [OF], [sq])
                outa, outb, outc, outd = Ht[6], Ht[0], Ht[6], Ht[0]
                wt, wv = wload(s_win[l], 8, SEG["a_g"], 512)
                rs = Ft[1]
                for h in range(4):
                    p = ps()
                    mm(p.t[:, 0:TB], onf.t[:, 0:128], sq.t[:, h * TB:(h + 1) * TB], True, True, [onf, sq], [p])
                    ts(rs.t[:, h * TB:(h + 1) * TB], p.t[:, 0:TB], 1.0 / 128, EPS, ALU.mult, ALU.add, [p], [rs])
                act(rs.t[:], rs.t[:], AF.Sqrt, [rs], [rs])
                recip(rs.t[:], rs.t[:], [rs], [rs])
                for h in range(4):
                    stt(rs.t[:, h * TB:(h + 1) * TB], OF.t[:, h * TB:(h + 1) * TB], hgnT.t[:, 0:1], rs.t[:, h * TB:(h + 1) * TB],
                        ALU.mult, ALU.mult, [OF, hgnT, rs], [rs])
                for h in range(4):
                    p2 = ps()
                    seg_fm(wv, wt, h, p2)
                    sg = Ft[2]
                    bc = bcol(SEG["a_g"]) + h
                    silu2(p2, bc, sg)
                    stt(outa.t[:, h * TB:(h + 1) * TB], sg.t[:, 0:TB], 0.5, rs.t[:, h * TB:(h + 1) * TB], ALU.mult, ALU.mult,
                        [rs, sg], [outa])
                def brB():
                    wt, wv = wload(s_win[l], 8, SEG["b_q"], 512)
                    qn_ = Ht[7]
                    qnv = qn_.t[:].rearrange("p (c n) -> p c n", n=TB)
                    for c in range(4):
                        p = ps()
                        seg_fm(wv, wt, c, p)
                        bc = bcol(SEG["b_q"]) + c
                        act(qnv[:, c, :], p.t[:, 0:TB], AF.Identity, [p, binT], [qn_], bias=binT.t[:, bc:bc + 1])
                    ON = Ft[0]
                    onv = ON.t[:].rearrange("p (c n) -> p c n", n=TB)
                    DEN = Ft[3]
                    dnv = DEN.t[:].rearrange("p (c n) -> p c n", n=TB)
                    if J.ctx:
                        ta, tb_ = 0, 1
                    else:
                        r0 = blk * 4
                        ta = min(max(r0 - 4, 0), 56) // 2
                        tb_ = (min(max(r0 + 3 - 4, 0), 56) + 7) // 2
                    nwt = tb_ - ta + 1
                    KW = sbkw
                    kwv = KW.t[:, 0:4 * nwt * 128].rearrange("p (c n) -> p c n", n=nwt * 128)
                    load(KW, kwv, s_nk[:, :, ta * 128:(tb_ + 1) * 128].rearrange("c p t -> p c t"), src=r_nk)
                    VW = sbvw
                    vwv = VW.t[:, 0:nwt * 512].rearrange("p (k n) -> p k n", n=512)
                    load(VW, vwv, s_nv[ta * 128:(tb_ + 1) * 128, :].rearrange("(k p) n -> p k n", p=128), src=r_nv)
                    if J.ctx:
                        for h in range(8):
                            c, p0 = h // 2, 64 * (h % 2)
                            keys = [(kwv[p0:p0 + 64, c, i * 128:(i + 1) * 128], vwv[:, i, c * 128:(c + 1) * 128], [KW, VW])
                                    for i in range(2)]
                            na_unit(qnv[p0:p0 + 64, c, :], TB, keys, None, 0, p0, dnv[p0:p0 + 64, c, :], onv[p0:p0 + 64, c, :], ON, [qn_],
                                    den_t=DEN)
                            yield
                    else:
                        ebv = EBt.t[:].rearrange("p (h v q) -> p h v q", v=NVAR, q=64)
                        for rr in range(4):
                            r = r0 + rr
                            rs_ = min(max(r - 4, 0), 56)
                            i0, i1 = rs_ // 2, (rs_ + 7) // 2
                            nloc = i1 - i0 + 1
                            for h in range(8):
                                c, p0 = h // 2, 64 * (h % 2)
                                keys = [(kwv[p0:p0 + 64, c, (i - ta) * 128:(i - ta + 1) * 128],
                                         vwv[:, i - ta, c * 128:(c + 1) * 128], [KW, VW]) for i in range(i0, i1 + 1)]
                                keys += [(ckT.t[p0:p0 + 64, c, i * 128:(i + 1) * 128], cvt_.t[:, i, c * 128:(c + 1) * 128],
                                          [ckT, cvt_]) for i in range(2)]
                                if nloc == 5:
                                    eb = ebv[:, h, 14:19, :]
                                else:
                                    v0 = (2 * i0 - r) + 7
                                    eb = ebv[:, h, v0:v0 + 7:2, :]
                                na_unit(qnv[p0:p0 + 64, c, rr * 64:(rr + 1) * 64], 64, keys, eb, nloc, p0,
                                        dnv[p0:p0 + 64, c, rr * 64:(rr + 1) * 64],
                                        onv[p0:p0 + 64, c, rr * 64:(rr + 1) * 64], ON, [qn_], den_t=DEN)
                                yield
                    na_flush()
                    recip(DEN.t[:], DEN.t[:], [DEN], [DEN])
                    tt(ON.t[:], ON.t[:], DEN.t[:], ALU.mult, [ON, DEN], [ON])
                    yield
                    wt, wv = wload(s_win[l], 8, SEG["b_g"], 512)
                    for c in range(4):
                        p = ps()
                        seg_fm(wv, wt, c, p)
                        sg = Ft[2]
                        bc = bcol(SEG["b_g"]) + c
                        silu2(p, bc, sg)
                        stt(outb.t[:, c * TB:(c + 1) * TB], sg.t[:, 0:TB], 0.5, onv[:, c, :], ALU.mult, ALU.mult, [ON, sg], [outb])
                    yield
                def brC():
                    wt, wv = wload(s_win[l], 8, SEG["c_qd"], 256)
                    qd = Ft[0]
                    qdv = qd.t[:, 0:2 * TB].rearrange("p (c n) -> p c n", n=TB)
                    sq = Ft[1]
                    pss = ps()
                    for c in range(2):
                        p = ps()
                        seg_fm(wv, wt, c, p)
                        bc = bcol(SEG["c_qd"]) + c
                        act(qdv[:, c, :], p.t[:, 0:TB], AF.Identity, [p, binT], [qd], bias=binT.t[:, bc:bc + 1])
                        act(sq.t[:, c * TB:(c + 1) * TB], qdv[:, c, :], AF.Square, [qd], [sq])
                        mm(pss.t[:, 0:TB], onf.t[:, 0:128], sq.t[:, c * TB:(c + 1) * TB], c == 0, c == 1, [onf, sq], [pss])
                    rs = Ft[2]
                    rstd_from(pss, TB, 1.0 / 256, rs)
                    qnb = Ht[7]
                    qnbv = qnb.t[:, 0:2 * TB].rearrange("p (c n) -> p c n", n=TB)
                    for c in range(2):
                        stt(qnbv[:, c, :], qdv[:, c, :], qngT.t[:, c:c + 1], rs.t[:, 0:TB], ALU.mult, ALU.mult,
                            [qd, qngT, rs], [qnb])
                    qlat = Ht[8]
                    qlv = qlat.t[:].rearrange("p (h n) -> p h n", n=TB)
                    qrope = Ht[9]
                    qrv = qrope.t[0:64, :].rearrange("p (h n) -> p h n", n=TB)
                    for h in range(4):
                        p = ps()
                        for kc in range(2):
                            mm(p.t[:, 0:TB], wqb_b.t[:, kc, h * 192:h * 192 + 128], qnbv[:, kc, :], kc == 0, kc == 1,
                               [wqb_b, qnb], [p])
                        qno = Ht[10]
                        cp(qno.t[:, 0:TB], p.t[:, 0:TB], [p], [qno])
                        p2 = ps()
                        mm(p2.t[:, 0:TB], wkT.t[:, h, :], qno.t[:, 0:TB], True, True, [wkT, qno], [p2])
                        acp(qlv[:, h, :], p2.t[:, 0:TB], [p2], [qlat])
                        p3 = ps()
                        for kc in range(2):
                            mm(p3.t[0:64, 0:TB], wqb_b.t[:, kc, h * 192 + 128:h * 192 + 192], qnbv[:, kc, :], kc == 0, kc == 1,
                               [wqb_b, qnb], [p3])
                        if J.ctx:
                            cp(qrv[:, h, :], p3.t[0:64, 0:TB], [p3], [qrope])
                        else:
                            p4 = ps()
                            for kc in range(2):
                                mm(p4.t[0:64, 0:TB], wqb_sw.t[:, kc, h, :], qnbv[:, kc, :], kc == 0, kc == 1, [wqb_sw, qnb], [p4])
                            t1 = Ft[3]
                            t2 = Ft[4]
                            tt(t1.t[0:64, 0:TB], p3.t[0:64, 0:TB], cosT.t[:, 0:TB], ALU.mult, [p3, cosT], [t1])
                            tt(t2.t[0:64, 0:TB], p4.t[0:64, 0:TB], cosT.t[:, TB:2 * TB], ALU.mult, [p4, cosT], [t2])
                            tt(qrv[:, h, :], t1.t[0:64, 0:TB], t2.t[0:64, 0:TB], ALU.add, [t1, t2], [qrope])
                        yield
                    wt, wv = wload(s_win[l], 8, SEG["c_g"], 512)
                    msc = 192.0 ** -0.5
                    for hp in range(2):
                        pacc_o = psacc()
                        pacc_d = psacc()
                        dacc = Ft[1]
                        q2 = qlat.t[:, hp * 2 * TB:(hp * 2 + 2) * TB]
                        r2 = qrope.t[0:64, hp * 2 * TB:(hp * 2 + 2) * TB]

                        def mla_b(kt_i, Pb_):
                            first = (kt_i == 0)
                            last = (kt_i == nkt - 1)
                            mm(pacc_o.t[:, 0:2 * TB], vlat.t[:, kt_i, :], Pb_.t[:, 0:2 * TB], first, last, [vlat, Pb_], [pacc_o])
                            if first:
                                cp(dacc.t[:, 0:2 * TB], Pb_.t[:, 0:2 * TB], [Pb_], [dacc])
                            else:
                                tt(dacc.t[:, 0:2 * TB], dacc.t[:, 0:2 * TB], Pb_.t[:, 0:2 * TB], ALU.add, [dacc, Pb_], [dacc])
                        pend = None
                        for kt_i in range(nkt):
                            pS = ps()
                            mm(pS.t[:, 0:2 * TB], klat.t[:, kt_i * 128:(kt_i + 1) * 128], q2, True, False, [klat, qlat], [pS])
                            mm(pS.t[:, 0:2 * TB], kpeT.t[0:64, kt_i * 128:(kt_i + 1) * 128], r2, False, True, [kpeT, qrope], [pS])
                            Pb = (Ht[10], Ht[11], Ht[3])[kt_i % 3]
                            act(Pb.t[:, 0:2 * TB], pS.t[:, 0:2 * TB], AF.Exp, [pS], [Pb], scale=msc)
                            if pend is not None:
                                mla_b(*pend)
                            pend = (kt_i, Pb)
                            yield
                        mla_b(*pend)
                        mm(pacc_d.t[:, 0:2 * TB], onf.t[:, 0:128], dacc.t[:, 0:2 * TB], True, True, [onf, dacc], [pacc_d])
                        rc = Ft[3]
                        recip(rc.t[:, 0:2 * TB], pacc_d.t[:, 0:2 * TB], [pacc_d], [rc])
                        oln = Ht[7]
                        tt(oln.t[:, TB * 2:TB * 4], pacc_o.t[:, 0:2 * TB], rc.t[:, 0:2 * TB], ALU.mult, [pacc_o, rc], [oln])
                        for j in range(2):
                            h = hp * 2 + j
                            p = ps()
                            mm(p.t[:, 0:TB], wkvb_b.t[:, h * 256 + 128:h * 256 + 256], oln.t[:, TB * (2 + j):TB * (3 + j)], True, True,
                               [wkvb_b, oln], [p])
                            p2 = ps()
                            seg_fm(wv, wt, h, p2)
                            sg = Ft[4]
                            bc = bcol(SEG["c_g"]) + h
                            silu2(p2, bc, sg)
                            stt(outc.t[:, h * TB:(h + 1) * TB], sg.t[:, 0:TB], 0.5, p.t[:, 0:TB], ALU.mult, ALU.mult, [p, sg], [outc])
                            yield
                    yield
                def brD():
                    wtc, wvc = wload(s_win[l], 8, SEG["d_c"], 512)
                    wtx, wvx = wload(s_win[l], 8, SEG["d_x"], 512)
                    uTs = [Ft[0], Ft[4]]
                    uvs = [t_.t[:, 0:2 * (TB + 4)].rearrange("p (c n) -> p c n", n=TB + 4) for t_ in uTs]
                    cvb = Ft[1]
                    for c in range(4):
                        uT = uTs[c // 2]
                        uvc = uvs[c // 2][:, c % 2, :]
                        bcc = bcol(SEG["d_c"]) + c
                        bcx = bcol(SEG["d_x"]) + c
                        p = ps()
                        seg_fm(wvc, wtc, c, p)
                        tc_ = Ft[2]
                        act(tc_.t[:, 0:TB], p.t[:, 0:TB], AF.Identity, [p, binT], [tc_], bias=binT.t[:, bcc:bcc + 1])
                        p2 = ps()
                        seg_fm(wvx, wtx, c, p2)
                        stt(uvc[:, 1:TB + 1], p2.t[:, 0:TB], binT.t[:, bcx:bcx + 1], tc_.t[:, 0:TB], ALU.add, ALU.mult,
                            [p2, binT, tc_], [uT])
                        p3 = ps()
                        seg_fm(wvc, wtc, c, p3, rhs_t=hTh, ntok=2)
                        act(tc_.t[:, 512:514], p3.t[:, 0:2], AF.Identity, [p3, binT], [tc_], bias=binT.t[:, bcc:bcc + 1])
                        p4 = ps()
                        seg_fm(wvx, wtx, c, p4, rhs_t=hTh, ntok=2)
                        hal = Ft[3]
                        stt(hal.t[:, 0:2], p4.t[:, 0:2], binT.t[:, bcx:bcx + 1], tc_.t[:, 512:514], ALU.add, ALU.mult,
                            [p4, binT, tc_], [hal])
                        if t0 == 0:
                            mset(uvc[:, 0:1], 0.0, [uT])
                        else:
                            cp(uvc[:, 0:1], hal.t[:, 0:1], [hal], [uT])
                        if t0 + TB >= J.T:
                            mset(uvc[:, TB + 1:TB + 2], 0.0, [uT])
                        else:
                            cp(uvc[:, TB + 1:TB + 2], hal.t[:, 1:2], [hal], [uT])
                        ts(cvb.t[:, c * TB:(c + 1) * TB], uvc[:, 0:TB], cwT.t[:, 0, c:c + 1], None, ALU.mult, None, [uT, cwT], [cvb])
                        stt(cvb.t[:, c * TB:(c + 1) * TB], uvc[:, 1:TB + 1], cwT.t[:, 1, c:c + 1], cvb.t[:, c * TB:(c + 1) * TB],
                            ALU.mult, ALU.add, [uT, cwT, cvb], [cvb])
                        stt(cvb.t[:, c * TB:(c + 1) * TB], uvc[:, 2:TB + 2], cwT.t[:, 2, c:c + 1], cvb.t[:, c * TB:(c + 1) * TB],
                            ALU.mult, ALU.add, [uT, cwT, cvb], [cvb])
                        yield
                    wtb, wvb = wload(s_win[l], 8, SEG["d_b"], 512)
                    wtg, wvg = wload(s_win[l], 8, SEG["d_g"], 512)
                    for c in range(4):
                        p = ps()
                        seg_fm(wvb, wtb, c, p)
                        bcb = bcol(SEG["d_b"]) + c
                        t2 = Ft[2]
                        stt(t2.t[:, 0:TB], p.t[:, 0:TB], binT.t[:, bcb:bcb + 1], cvb.t[:, c * TB:(c + 1) * TB], ALU.add, ALU.mult,
                            [p, binT, cvb], [t2])
                        p2 = ps()
                        seg_fm(wvg, wtg, c, p2)
                        sg = Ft[3]
                        bcg = bcol(SEG["d_g"]) + c
                        silu2(p2, bcg, sg)
                        stt(outd.t[:, c * TB:(c + 1) * TB], sg.t[:, 0:TB], 0.5, t2.t[:, 0:TB], ALU.mult, ALU.mult, [t2, sg], [outd])
                        yield
                    yield
                stopn = (dbg or {}).get('stop', 9)
                run_pair(merge_gen(l, 0, outa), brB() if stopn >= 1 else iter(()), 4)
                if stopn == 0:
                    return
                run_pair(merge_gen(l, 1, outb), brC() if stopn >= 2 else iter(()), 9)
                if stopn == 1:
                    return
                run_pair(merge_gen(l, 2, outc), brD() if stopn >= 3 else iter(()), 1)
                if stopn == 2:
                    return
                run_pair(merge_gen(l, 3, outd), iter(()), 1)
                if stopn == 3:
                    return
                ts(Ht[4].t[:, 0:1024], mT.t[:, 0:1024], 0.5, None, ALU.mult, None, [mT], [Ht[4]])
                act(Ht[5].t[:, 0:1024], mT.t[:, 1024:2048], AF.Identity, [mT], [Ht[5]], scale=0.5)
                mxvs = [Ht[4].t[:].rearrange("p (c n) -> p c n", n=TB), Ht[5].t[:].rearrange("p (c n) -> p c n", n=TB)]
                src = J.x0 if l == 0 else J.ymid
                wo = [wload(s_wout[l], 8, hf * 512, 512) for hf in range(2)]
                for k in range(TB // 128):
                    xt = xin[k % 2]
                    load(xt, xt.t[:], src[t0 + k * 128:t0 + (k + 1) * 128, :], src=(None if l == 0 else r_ymid[k % 2]))
                    yt = Ft[2 + k % 2]
                    for hf in range(2):
                        p = ps()
                        wt, wv = wo[hf]
                        for kc in range(8):
                            mm(p.t[:, 0:512], mxvs[kc // 4][:, kc % 4, k * 128:(k + 1) * 128], wv[:, kc, :], kc == 0, kc == 7,
                               [wt, Ht[4 + kc // 4]], [p])
                        tt(yt.t[:, hf * 512:(hf + 1) * 512], p.t[:, 0:512], gate_bc[J.cv].t[:, hf * 512:(hf + 1) * 512], ALU.mult,
                           [p, gate_bc[J.cv]], [yt])
                    stt(yt.t[:], xt.t[:], ALPHA, yt.t[:], ALU.mult, ALU.add, [xt, yt], [yt])
                    for hf in range(2):
                        P.op("vector", (lambda o, i: (lambda e: e.bn_stats(out=o, in_=i)))(bnst.t[:, hf, :], yt.t[:, hf * 512:(hf + 1) * 512]),
                             B([yt]), B([bnst]))
                    P.op("vector", (lambda o, i: (lambda e: e.bn_aggr(out=o, in_=i)))(bnag.t[:, 0:2], bnst.t[:].rearrange("p a b -> p (a b)")),
                         B([bnst]), B([bnag]))
                    ts(small.t[:, 40:41], bnag.t[:, 1:2], 1.0, EPS, ALU.mult, ALU.add, [bnag], [small])
                    act(small.t[:, 41:42], small.t[:, 40:41], AF.Sqrt, [small], [small])
                    recip(small.t[:, 42:43], small.t[:, 41:42], [small], [small])
                    ts(yt.t[:], yt.t[:], bnag.t[:, 0:1], small.t[:, 42:43], ALU.subtract, ALU.mult, [yt, bnag, small], [yt])
                    tt(yt.t[:], yt.t[:], lng_bc.t[:], ALU.mult, [yt, lng_bc], [yt])
                    tt(yt.t[:], yt.t[:], lnb_bc.t[:], ALU.add, [yt, lnb_bc], [yt])
                    if l == DEPTH - 1:
                        store(yt, J.yout[t0 + k * 128:t0 + (k + 1) * 128, :], yt.t[:])
                    else:
                        store(yt, J.ymid[t0 + k * 128:t0 + (k + 1) * 128, :], yt.t[:], dst=r_ymid[k % 2], nodep=True)
            hgrn_store_state(J, l, 0)

        sbkw = sb("KW", [128, 4 * 6 * 128], BF16)
        sbvw = sb("VW", [128, 6 * 512], BF16)
        mT = sb("mT", [128, 8 * TB], F32)
        WMb = sb("WMb", [128, 4 * 128], BF16)
        WMw = sb("WMw", [128, 8 * 256], BF16)

        dbg_ = dbg or {}
        for l in range(dbg_.get("layers", DEPTH)):
            if dbg_.get("params", True):
                layer_params(l)
            for ji, J in enumerate(jobs):
                if ji not in dbg_.get("jobs", (0, 1, 2)):
                    continue
                if "kv" in dbg_.get("passes", "kv,bwd,main"):
                    hgrn_init_state(J, l, 1)
                    kv_pass(J, l, after_block=(lambda blk, J=J, l=l: bwd_block(J, l, blk)))
                    hgrn_store_state(J, l, 1)
                if "main" in dbg_.get("passes", "kv,bwd,main"):
                    main_pass(J, l)
        P.wait_all("gpsimd", [t.b for t in [Sst] + SstH + Ft + Ht])
        P.emit()
    return nc


_NC_CACHE = {}
_DBG = None


def _const_inputs():
    idf = np.eye(128, dtype=np.float32)
    idb = np.eye(128).astype(ml_dtypes.bfloat16)
    t = np.arange(TS)
    row = (t // 64).astype(np.float32)
    col = (t % 64).astype(np.float32)
    inv = (1.0 / (np.float32(10000.0) ** (np.arange(16, dtype=np.float32) / np.float32(16)))).astype(np.float32)
    ang = np.concatenate([row[:, None] * inv, col[:, None] * inv], axis=-1).astype(np.float32)
    cos = np.cos(ang).astype(np.float32).T
    sin = np.sin(ang).astype(np.float32).T
    c_cos = np.ascontiguousarray(np.concatenate([cos, cos], axis=0))
    c_sin = np.ascontiguousarray(np.concatenate([-sin, sin], axis=0))
    p = np.arange(128)[:, None] % 64
    tt_ = np.arange(256)[None, :] % 64
    c_mf = (p <= tt_).astype(ml_dtypes.bfloat16)
    c_mb = (p >= tt_).astype(ml_dtypes.bfloat16)
    return dict(c_idf=idf, c_idb=idb, c_cos=c_cos, c_sin=c_sin, c_mf=c_mf, c_mb=c_mb)


def _na_table(na_rpb):
    rpb = np.asarray(na_rpb, dtype=np.float32)
    pairs = [(v - 7, v - 6, True, True) for v in range(14)]
    pairs += [(-5, -4, False, True), (-3, -2, True, True), (-1, 0, True, True), (1, 2, True, True), (3, 4, True, False)]
    kc = np.arange(64)[:, None]
    qc = np.arange(64)[None, :]
    cs = np.clip(qc - 8, 0, 48)
    win = (kc >= cs) & (kc < cs + 16)
    cidx = np.clip(kc - qc + 15, 0, 30)
    tab = np.full((DEPTH, 128, 8, NVAR, 64), -30000.0, dtype=np.float32)
    for v, (da, db, va, vb) in enumerate(pairs):
        for which, (dr, ok) in enumerate(((da, va), (db, vb))):
            if not ok:
                continue
            g = rpb[:, :, dr + 7, :][:, :, cidx]
            g = np.where(win[None, None], g, np.float32(-30000.0))
            tab[:, which * 64:(which + 1) * 64, :, v, :] = np.transpose(g, (0, 2, 1, 3))
    return np.ascontiguousarray(tab.reshape(DEPTH, 128, 8 * NVAR * 64))


def kernel(x_prompt, x_sample, state_hgrn, cache_na_k, cache_na_v, cache_mla_ckv, cache_mla_kpe,
           c, c_ctx, w_ada, b_ada, w_in, b_in, hg_lb_logits, hg_norm_g, na_rpb, mla_qnorm_g,
           mla_w_qb, mla_kvnorm_g, mla_w_kvb, conv_w, w_branch, w_out, ln_g, ln_b):
    f = lambda a: np.ascontiguousarray(np.asarray(a, dtype=np.float32))
    if "nc" not in _NC_CACHE:
        _NC_CACHE["nc"] = build_program(_DBG)
    nc = _NC_CACHE["nc"]
    shared = dict(
        w_ada=f(w_ada), b_ada=f(b_ada), w_in=f(w_in), b_in=f(b_in), lbl=f(hg_lb_logits), hgn=f(hg_norm_g),
        natab=_na_table(na_rpb), qng=f(mla_qnorm_g), wqb=f(mla_w_qb), kvg=f(mla_kvnorm_g), wkvb=f(mla_w_kvb),
        convw=f(conv_w), wbr=f(w_branch).reshape(DEPTH, 2048, D), wout=f(w_out), lng=f(ln_g), lnb=f(ln_b))
    shared.update(_const_inputs())
    x_prompt = f(x_prompt)
    x_sample = f(x_sample)
    state_hgrn = f(state_hgrn)
    cnk = f(cache_na_k).reshape(8, DEPTH, 256, 512)
    cnv = f(cache_na_v).reshape(8, DEPTH, 256, 512)
    cckv = f(cache_mla_ckv)
    ckpe = f(cache_mla_kpe)
    c = f(c)
    c_ctx = f(c_ctx)
    in_maps = []
    for i in range(NCORES):
        m = dict(shared)
        m["xs"] = x_sample[i]
        m["xp"] = x_prompt[2 * i:2 * i + 2]
        m["st_h"] = state_hgrn[i]
        m["cnk"] = cnk[i]
        m["cnv"] = cnv[i]
        m["cckv"] = cckv[i]
        m["ckpe"] = ckpe[i]
        m["cvec"] = np.ascontiguousarray(np.stack([c[i], c_ctx], axis=0))
        in_maps.append(m)
    res = run_bass_kernel_spmd(nc, in_maps, core_ids=list(range(NCORES)))
    R = res.results
    y_p = np.concatenate([np.asarray(r["y_p"], dtype=np.float32) for r in R], axis=0)
    y_s = np.stack([np.asarray(r["y_s"], dtype=np.float32) for r in R], axis=0)
    st = np.concatenate([np.asarray(r["o_st"], dtype=np.float32) for r in R], axis=0)
    nk = np.concatenate([np.asarray(r["o_nk"], dtype=np.float32) for r in R], axis=0).reshape(16, DEPTH, TP, 8, 64)
    nv = np.concatenate([np.asarray(r["o_nv"], dtype=np.float32) for r in R], axis=0).reshape(16, DEPTH, TP, 8, 64)
    ckv = np.concatenate([np.asarray(r["o_ckv"], dtype=np.float32) for r in R], axis=0)
    kpe = np.concatenate([np.asarray(r["o_kpe"], dtype=np.float32) for r in R], axis=0)
    return (y_p, y_s, st, nk, nv, ckv, kpe)
```

```python
import contextlib
import numpy as np
import ml_dtypes
import concourse.bass as bass
import concourse.mybir as mybir
from concourse.bass_utils import run_bass_kernel_spmd

F32 = mybir.dt.float32
BF16 = mybir.dt.bfloat16
AF = mybir.ActivationFunctionType
ALU = mybir.AluOpType

NCORES = 8
D = 1024
DEPTH = 2
TS = 4096
TP = 256
TB = 256
INW = 11712
EPS = 1e-6
ALPHA = (2 * DEPTH) ** 0.25
SEG = dict(a_q=0, a_ff=512, a_fb=1024, a_i=1536, a_g=2048, b_q=2560, b_k=3072, b_v=3584, b_g=4096,
           c_qd=4608, c_kvd=4864, c_kpe=4992, c_g=5056, d_b=5568, d_c=6080, d_x=6592, d_g=7104, mrg=7616)
NVAR = 19
ENGS = ("tensor", "vector", "scalar", "gpsimd", "sync")


def bcol(off):
    if off < 4992:
        return off // 128
    if off == 4992:
        return 39
    return 40 + (off - 5056) // 128


class Buf:
    __slots__ = ("name", "last_w", "readers")

    def __init__(self, name=""):
        self.name = name
        self.last_w = None
        self.readers = {}


class Prog:
    def __init__(self, nc):
        self.nc = nc
        self.ops = {e: [] for e in ENGS}
        self.chan_n = {}
        self.waited = {e: {} for e in ENGS}
        self.dma_chans = []
        self.needed = set()

    def dma_chan(self, name):
        c = "dma:%d" % len(self.dma_chans)
        self.dma_chans.append(c)
        self.chan_n[c] = 0
        return c

    def _deps(self, eng, reads, writes):
        raw = {}
        oth = {}

        def add(d, tok):
            if tok is None:
                return
            c, i = tok
            if c.startswith("dma:"):
                i = self.chan_n[c] - 1
            if d.get(c, -1) < i:
                d[c] = i

        for r in reads:
            add(raw, r.last_w)
        for w in writes:
            add(oth, w.last_w)
            for c, i in w.readers.items():
                add(oth, (c, i))
        deps = dict(raw)
        for c, i in oth.items():
            if c == eng:
                continue
            if deps.get(c, -1) < i:
                deps[c] = i
        out = []
        wd = self.waited[eng]
        for c, i in deps.items():
            if c == "tensor" and eng == "tensor":
                continue
            if wd.get(c, -1) >= i:
                continue
            wd[c] = i
            out.append((c, i))
            self.needed.add((c, i))
        return out

    def _record(self, tok, reads, writes):
        c, i = tok
        for r in reads:
            if r.readers.get(c, -1) < i:
                r.readers[c] = i
        for w in writes:
            w.last_w = tok
            w.readers = {}

    def op(self, eng, fn, reads=(), writes=()):
        waits = self._deps(eng, reads, writes)
        idx = self.chan_n.get(eng, 0)
        self.chan_n[eng] = idx + 1
        tok = (eng, idx)
        self.ops[eng].append((fn, waits, tok, False))
        self._record(tok, reads, writes)
        return tok

    def dma(self, queue, chan, fn, reads=(), writes=(), nodep_writes=()):
        waits = self._deps(queue, reads, writes)
        idx = self.chan_n[chan]
        self.chan_n[chan] = idx + 1
        tok = (chan, idx)
        self.ops[queue].append((fn, waits, tok, True))
        self._record(tok, reads, list(writes) + list(nodep_writes))
        return tok

    def wait_all(self, eng, bufs):
        waits = self._deps(eng, (), bufs)
        self.ops[eng].append((None, waits, None, False))

    def emit(self):
        nc = self.nc
        with contextlib.ExitStack() as es:
            sems = {}
            for e in ENGS:
                sems[e] = es.enter_context(nc.semaphore("c_" + e))
            for k, c in enumerate(self.dma_chans):
                sems[c] = es.enter_context(nc.semaphore("d%d" % k))
            val = {}
            for e in ENGS:
                k = 0
                for (fn, waits, t, isdma) in self.ops[e]:
                    if t is None or isdma:
                        continue
                    if t in self.needed:
                        k += 1
                        val[t] = k
            block = es.enter_context(nc.Block())

            def run(e, handle):
                for (fn, waits, t, isdma) in self.ops[e]:
                    for (c, i) in waits:
                        if c.startswith("dma:"):
                            handle.wait_ge(sems[c], 16 * (i + 1))
                        else:
                            handle.wait_ge(sems[c], val[(c, i)])
                    if fn is None:
                        continue
                    ins = fn(handle)
                    if isdma:
                        ins.then_inc(sems[t[0]], 16)
                    elif t in self.needed:
                        ins.then_inc(sems[t[0]], 1)

            @block.sync
            def _(h):
                run("sync", h)

            @block.tensor
            def _(h):
                run("tensor", h)

            @block.vector
            def _(h):
                run("vector", h)

            @block.scalar
            def _(h):
                run("scalar", h)

            @block.gpsimd
            def _(h):
                run("gpsimd", h)


class TL:
    def __init__(self, t, name):
        self.t = t
        self.b = Buf(name)
        self.chan = None


class DR:
    def __init__(self, name):
        self.b = Buf(name)


def build_program(dbg=None):
    nc = bass.Bass("TRN2", target_bir_lowering=False)
    P = Prog(nc)

    def din(name, shape, dt=F32):
        return nc.dram_tensor(name, list(shape), dt, kind="ExternalInput").ap()

    def dout(name, shape):
        return nc.dram_tensor(name, list(shape), F32, kind="ExternalOutput").ap()

    def dscr(name, shape, dt):
        return nc.dram_tensor(name, list(shape), dt, kind="Internal").ap()

    xs = din("xs", [TS, D])
    xp = din("xp", [2, TP, D])
    st_h = din("st_h", [DEPTH, 2, 4, 128, 128])
    cnk = din("cnk", [DEPTH, 256, 512])
    cnv = din("cnv", [DEPTH, 256, 512])
    cckv = din("cckv", [DEPTH, 256, 128])
    ckpe = din("ckpe", [DEPTH, 256, 64])
    cvec = din("cvec", [2, D])
    w_ada = din("w_ada", [DEPTH, D, 3 * D])
    b_ada = din("b_ada", [DEPTH, 3 * D])
    w_in = din("w_in", [DEPTH, D, INW])
    b_in = din("b_in", [DEPTH, INW])
    lbl = din("lbl", [2, DEPTH, 512])
    hgn = din("hgn", [DEPTH, 128])
    natab = din("natab", [DEPTH, 128, 8 * NVAR * 64])
    qng = din("qng", [DEPTH, 256])
    wqb = din("wqb", [DEPTH, 256, 768])
    kvg = din("kvg", [DEPTH, 128])
    wkvb = din("wkvb", [DEPTH, 128, 1024])
    convw = din("convw", [DEPTH, 3, 512])
    wbr = din("wbr", [DEPTH, 2048, D])
    wout = din("wout", [DEPTH, D, D])
    lng = din("lng", [DEPTH, D])
    lnb = din("lnb", [DEPTH, D])
    c_idf = din("c_idf", [128, 128])
    c_idb = din("c_idb", [128, 128], BF16)
    c_cos = din("c_cos", [64, TS])
    c_sin = din("c_sin", [64, TS])
    c_mf = din("c_mf", [128, 256], BF16)
    c_mb = din("c_mb", [128, 256], BF16)
    y_p = dout("y_p", [2, TP, D])
    y_s = dout("y_s", [TS, D])
    o_st = dout("o_st", [2, DEPTH, 2, 4, 128, 128])
    o_nk = dout("o_nk", [2, DEPTH, TP, 512])
    o_nv = dout("o_nv", [2, DEPTH, TP, 512])
    o_ckv = dout("o_ckv", [2, DEPTH, TP, 128])
    o_kpe = dout("o_kpe", [2, DEPTH, TP, 64])
    s_win = dscr("s_win", [DEPTH, D, INW], BF16)
    s_wbr = dscr("s_wbr", [DEPTH, 2048, D], BF16)
    s_wout = dscr("s_wout", [DEPTH, D, D], BF16)
    s_ys = dscr("s_ys", [TS, D], F32)
    s_yp = dscr("s_yp", [2, TP, D], F32)
    s_ob = dscr("s_ob", [4, 128, TS], F32)
    s_nk = dscr("s_nk", [4, 128, TS], BF16)
    s_nv = dscr("s_nv", [TS, 512], BF16)
    s_hT = dscr("s_hT", [8, 128, TS], BF16)
    r_hT = DR("hTscr")
    r_w = DR("wscr")
    r_ymid = [DR("ymid0"), DR("ymid1")]
    r_ob = DR("ob")
    r_nk = DR("nk")
    r_nv = DR("nv")

    es = contextlib.ExitStack()
    with es:
        def sb(name, shape, dt):
            return TL(es.enter_context(nc.sbuf_tensor(name, list(shape), dt)), name)

        def pst(name, shape, dt):
            return TL(es.enter_context(nc.psum_tensor(name, list(shape), dt)), name)

        def B(xs_):
            return [x.b for x in xs_]

        def mm(out, lhsT, rhs, start, stop, R, W):
            P.op("tensor", lambda e: e.matmul(out, lhsT=lhsT, rhs=rhs, start=start, stop=stop), B(R), B(W))

        def tr(out, in_, ident, R, W):
            P.op("tensor", lambda e: e.transpose(out, in_, ident), B(R), B(W))

        def act(out, in_, func, R, W, bias=None, scale=None, accum=None):
            kw = {}
            if bias is not None:
                kw["bias"] = bias
            if scale is not None:
                kw["scale"] = scale
            if accum is not None:
                kw["accum_out"] = accum
            P.op("scalar", lambda e: e.activation(out=out, in_=in_, func=func, **kw), B(R), B(W))

        def tt(out, in0, in1, op, R, W, eng="vector"):
            P.op(eng, lambda e: e.tensor_tensor(out=out, in0=in0, in1=in1, op=op), B(R), B(W))

        def ts(out, in0, s1, s2, op0, op1, R, W, eng="vector"):
            if s2 is None:
                P.op(eng, lambda e: e.tensor_scalar(out=out, in0=in0, scalar1=s1, scalar2=None, op0=op0), B(R), B(W))
            else:
                P.op(eng, lambda e: e.tensor_scalar(out=out, in0=in0, scalar1=s1, scalar2=s2, op0=op0, op1=op1),
                     B(R), B(W))

        def stt(out, in0, scalar, in1, op0, op1, R, W, eng="vector"):
            P.op(eng, lambda e: e.scalar_tensor_tensor(out=out, in0=in0, scalar=scalar, in1=in1, op0=op0, op1=op1),
                 B(R), B(W))

        def cp(out, in_, R, W, eng="vector"):
            P.op(eng, lambda e: e.tensor_copy(out=out, in_=in_), B(R), B(W))

        def acp(out, in_, R, W):
            P.op("scalar", lambda e: e.copy(out=out, in_=in_), B(R), B(W))

        def mset(ap, v, W, eng="vector"):
            P.op(eng, lambda e: e.memset(ap, v), (), B(W))

        def recip(out, in_, R, W):
            P.op("vector", lambda e: e.reciprocal(out=out, in_=in_), B(R), B(W))

        def scan(out, d0, d1, R, W):
            P.op("vector", lambda e: e.tensor_tensor_scan(out=out, data0=d0, data1=d1, initial=0.0,
                                                          op0=ALU.mult, op1=ALU.add), B(R), B(W))

        def dma(q, tile, out, in_, R=(), W=(), ND=(), slow=False):
            if tile.chan is None:
                tile.chan = {}
            if q not in tile.chan:
                tile.chan[q] = P.dma_chan(tile.b.name + q)
            if slow:
                P.dma(q, tile.chan[q], lambda e: e.dma_start(out=out, in_=in_, allow_slow_non_contiguous=True), B(R), B(W), B(ND))
            else:
                P.dma(q, tile.chan[q], lambda e: e.dma_start(out=out, in_=in_), B(R), B(W), B(ND))

        def load(tile, out, in_, src=None, q="gpsimd"):
            dma(q, tile, out, in_, R=([src] if src is not None else []), W=[tile])

        def store(tile, out, in_, dst=None, q="gpsimd", nodep=False):
            if dst is None:
                dma(q, tile, out, in_, R=[tile])
            elif nodep:
                dma(q, tile, out, in_, R=[tile], ND=[dst])
            else:
                dma(q, tile, out, in_, R=[tile], W=[dst])

        PSR = [pst("psr%d" % i, [128, 512], F32) for i in range(5)]
        PSA = [pst("psa%d" % i, [128, 512], F32) for i in range(2)]
        PSB = pst("psb", [128, 1024], BF16)
        ring = [0]

        def ps():
            t = PSR[ring[0] % 5]
            ring[0] += 1
            return t
        acc_i = [0]

        def psacc():
            t = PSA[acc_i[0] % 2]
            acc_i[0] += 1
            return t

        idf = sb("idf", [128, 128], F32)
        idb = sb("idb", [128, 128], BF16)
        onb = sb("onb", [128, 128], BF16)
        onf = sb("onf", [128, 256], F32)
        mfw = sb("mfw", [128, 256], BF16)
        mbw = sb("mbw", [128, 256], BF16)
        load(idf, idf.t[:], c_idf[:, :])
        load(idb, idb.t[:], c_idb[:, :])
        load(mfw, mfw.t[:], c_mf[:, :])
        load(mbw, mbw.t[:], c_mb[:, :])
        mset(onb.t[:], 1.0, [onb])

        mset(onf.t[:], 1.0, [onf])

        NF = 10
        Ft = [sb("F%d" % i, [128, 1024], F32) for i in range(NF)]
        NH = 12
        Ht = [sb("H%d" % i, [128, 1024], BF16) for i in range(NH)]
        NWS = 2
        WS = [sb("WS%d" % i, [128, 8 * 512], BF16) for i in range(NWS)]
        ws_i = [0]

        cvt_i = [0]

        def convert(src2d, dst2d, rows, cols):
            CW = 1024
            for r0 in range(0, rows, 128):
                for c0 in range(0, cols, CW):
                    cw = min(CW, cols - c0)
                    i = cvt_i[0]
                    cvt_i[0] += 1
                    f = Ft[i % 4]
                    h = Ht[i % 4]
                    load(f, f.t[:, 0:cw], src2d[r0:r0 + 128, c0:c0 + cw], q="sync")
                    if i % 2 == 0:
                        cp(h.t[:, 0:cw], f.t[:, 0:cw], [f], [h])
                    else:
                        acp(h.t[:, 0:cw], f.t[:, 0:cw], [f], [h])
                    store(h, dst2d[r0:r0 + 128, c0:c0 + cw], h.t[:, 0:cw], dst=r_w, q="gpsimd", nodep=True)

        for l in range((dbg or {}).get("cvt_layers", DEPTH)):
            convert(w_in[l], s_win[l], D, INW)
            convert(wbr[l], s_wbr[l], 2048, D)
            convert(wout[l], s_wout[l], D, D)

        P.wait_all("sync", [Ht[i].b for i in range(4)])

        def wload(src2d, nk, c0, ncols):
            t = WS[ws_i[0] % NWS]
            ws_i[0] += 1
            v = t.t[:, 0:nk * ncols].rearrange("p (k n) -> p k n", n=ncols)
            srcv = src2d.rearrange("(k p) n -> p k n", p=128)
            dma("sync", t, v, srcv[:, 0:nk, c0:c0 + ncols], R=[r_w], W=[t])
            return t, v

        binT = sb("binT", [128, 92], F32)
        binH = sb("binH", [128, 92], F32)
        bkpesw = sb("bkpesw", [64, 1], F32)
        brow = sb("brow", [1, 1728], BF16)
        modT = [sb("modT%d" % i, [128, 24], F32) for i in range(2)]
        gate_bc = [sb("gatebc%d" % i, [128, 1024], F32) for i in range(2)]
        lng_bc = sb("lngbc", [128, 1024], F32)
        lnb_bc = sb("lnbbc", [128, 1024], F32)
        kvg_bc = sb("kvgbc", [128, 128], F32)
        lbT = sb("lbT", [128, 2, 4], F32)
        omlT = sb("omlT", [128, 2, 4], F32)
        hgnT = sb("hgnT", [128, 1], F32)
        qngT = sb("qngT", [128, 2], F32)
        cwT = sb("cwT", [128, 3, 4], F32)
        wqb_b = sb("wqb_b", [128, 2, 768], BF16)
        wqb_sw = sb("wqb_sw", [128, 2, 4, 64], BF16)
        wkvb_b = sb("wkvb_b", [128, 1024], BF16)
        wkT = sb("wkT", [128, 4, 128], BF16)
        EBt = sb("EB", [128, 8 * NVAR * 64], BF16)
        scT = sb("scT", [128, 2, 8], F32)
        rowst = Ft[5]
        ckT = sb("ckT", [128, 4, 256], BF16)
        cvt_ = sb("cv", [128, 2, 512], BF16)
        NKT = TS // 128 + 2
        klat = sb("klat", [128, NKT * 128], BF16)
        kpeT = sb("kpeT", [64, NKT * 128], BF16)
        vlat = sb("vlat", [128, NKT, 128], BF16)
        Sst = sb("Sst", [128, 4, 128], F32)
        Sb16 = sb("Sb16", [128, 4, 128], BF16)
        tmpS = sb("tmpS", [128, 4, 128], F32)

        class VW_:
            def __init__(self, name):
                self.b = Buf(name)
        SstH = [VW_("Sst%d" % h) for h in range(4)]
        Sb16H = [VW_("Sb16%d" % h) for h in range(4)]
        tmpSH = [VW_("tmpS%d" % h) for h in range(4)]
        sc3 = sb("sc3", [128, 3, 16], F32)
        amF = sb("amF", [128, 4, 64], BF16)
        amB = sb("amB", [128, 4, 64], BF16)
        mset(amF.t[:], 0.0, [amF])
        mset(amB.t[:], 0.0, [amB])
        ktok = sb("ktok", [128, 4, 128], BF16)
        hT = sb("hT", [128, 8, TB], BF16)
        hTh = sb("hTh", [128, 8, 2], BF16)
        xin = [sb("xin%d" % i, [128, 1024], F32) for i in range(2)]
        xhalo = Ft[7]
        cosT = sb("cosT", [64, 2 * TB], F32)
        sinT = None
        small = sb("small", [128, 64], F32)
        bnst = sb("bnst", [128, 2, 6], F32)
        bnag = sb("bnag", [128, 2], F32)
        wkpe_sw = sb("wkpe_sw", [128, 8, 64], BF16)

        def col_form(dst_ap, src_rows_ap, nrows, ncols, dst_tile):
            load(rowst, rowst.t[0:nrows, 0:ncols], src_rows_ap)
            p = ps()
            tr(p.t[0:ncols, 0:nrows], rowst.t[0:nrows, 0:ncols], idf.t[0:nrows, 0:nrows], [rowst, idf], [p])
            cp(dst_ap, p.t[0:ncols, 0:nrows], [p], [dst_tile])

        def bcast_rows(dst_tile, dst_ap, row_ap_f32, n, R):
            for c0 in range(0, n, 512):
                cw = min(512, n - c0)
                p = ps()
                mm(p.t[:, 0:cw], onf.t[0:1, 0:128], row_ap_f32[:, c0:c0 + cw], True, True, [onf] + R, [p])
                cp(dst_ap[:, c0:c0 + cw], p.t[:, 0:cw], [p], [dst_tile])

        def layer_params(l):
            mset(binT.t[:], 0.0, [binT])
            col_form(binT.t[:, 0:39], b_in[l, 0:4992].rearrange("(c p) -> c p", p=128), 39, 128, binT)
            col_form(binT.t[0:64, 39:40], b_in[l, 4992:5056].rearrange("(c p) -> c p", p=64), 1, 64, binT)
            col_form(binT.t[:, 40:92], b_in[l, 5056:INW].rearrange("(c p) -> c p", p=128), 52, 128, binT)
            ts(binH.t[:], binT.t[:], 0.5, None, ALU.mult, None, [binT], [binH])
            load(bkpesw, bkpesw.t[0:32, :], b_in[l, 5024:5056].rearrange("(p o) -> p o", o=1))
            load(bkpesw, bkpesw.t[32:64, :], b_in[l, 4992:5024].rearrange("(p o) -> p o", o=1))
            for (o, w, d0) in ():
                load(Ft[9], Ft[9].t[0:1, d0 - (0 if d0 < 1024 else 1024):d0 - (0 if d0 < 1024 else 1024) + w] if False else Ft[9 if d0 < 1024 else 8].t[0:1, (d0 % 1024):(d0 % 1024) + w],
                     b_in[l, o:o + w].rearrange("(o n) -> o n", o=1)) if False else None
            browf_parts = ((SEG["a_i"], 512, 0), (SEG["b_v"], 512, 512))
            for (o, w, d0) in browf_parts:
                load(Ft[9], Ft[9].t[0:1, d0:d0 + w], b_in[l, o:o + w].rearrange("(o n) -> o n", o=1))
            cp(brow.t[0:1, 0:1024], Ft[9].t[0:1, 0:1024], [Ft[9]], [brow])
            for (o, w, d0) in ((SEG["c_kvd"], 128, 0), (SEG["c_kpe"], 64, 128), (SEG["b_k"], 512, 192)):
                load(Ft[8], Ft[8].t[0:1, d0:d0 + w], b_in[l, o:o + w].rearrange("(o n) -> o n", o=1))
            cp(brow.t[0:1, 1024:1728], Ft[8].t[0:1, 0:704], [Ft[8]], [brow])
            for cv in range(2):
                load(rowst, rowst.t[0:8, 0:128], cvec[cv].rearrange("(c p) -> c p", p=128))
                p = ps()
                tr(p.t[:, 0:8], rowst.t[0:8, 0:128], idf.t[0:8, 0:8], [rowst, idf], [p])
                act(scT.t[:, cv, :], p.t[:, 0:8], AF.Silu, [p], [scT])
            pm = psacc()
            pg = [psacc(), ps()]
            pg2 = [ps(), ps()]
            for nr in range(3):
                for kc in range(8):
                    f = Ft[4 + (nr * 8 + kc) % 4]
                    load(f, f.t[:], w_ada[l, kc * 128:(kc + 1) * 128, nr * 1024:(nr + 1) * 1024], q="sync")
                    if nr < 2:
                        for j in range(8):
                            for cv in range(2):
                                col = (nr * 8 + j) * 2 + cv
                                mm(pm.t[:, col:col + 1], f.t[:, j * 128:(j + 1) * 128], scT.t[:, cv, kc:kc + 1],
                                   (nr == 0 and kc == 0 and j == 0 and cv == 0), (nr == 1 and kc == 7 and j == 7 and cv == 1),
                                   [f, scT], [pm])
                    else:
                        for cv in range(2):
                            for hf in range(2):
                                pt = pg[cv] if hf == 0 else pg2[cv]
                                mm(pt.t[0:1, 0:512], scT.t[:, cv, kc:kc + 1], f.t[:, hf * 512:(hf + 1) * 512],
                                   kc == 0, kc == 7, [f, scT], [pt])
            f = Ft[8]
            load(f, f.t[0:1, :], b_ada[l, 2048:3072].rearrange("(o n) -> o n", o=1))
            for cv in range(2):
                g = Ft[6 + cv]
                for hf in range(2):
                    pt = pg[cv] if hf == 0 else pg2[cv]
                    tt(g.t[0:1, hf * 512:(hf + 1) * 512], pt.t[0:1, 0:512], f.t[0:1, hf * 512:(hf + 1) * 512],
                       ALU.add, [pt, f], [g])
            col_form(small.t[:, 0:16], b_ada[l, 0:2048].rearrange("(c p) -> c p", p=128), 16, 128, small)
            pmv = pm.t[:, 0:32].rearrange("p (c v) -> p c v", v=2)
            for cv in range(2):
                tt(modT[cv].t[:, 0:16], pmv[:, :, cv], small.t[:, 0:16], ALU.add, [pm, small], [modT[cv]])
                ts(modT[cv].t[:, 8:16], modT[cv].t[:, 8:16], 1.0, None, ALU.add, None, [modT[cv]], [modT[cv]])
            for cv in range(2):
                g = Ft[6 + cv]
                bcast_rows(gate_bc[cv], gate_bc[cv].t, g.t[0:1, :], 1024, [g])
            f = Ft[8]
            load(f, f.t[0:1, :], lng[l].rearrange("(o n) -> o n", o=1))
            bcast_rows(lng_bc, lng_bc.t, f.t[0:1, :], 1024, [f])
            f = Ft[9]
            load(f, f.t[0:1, :], lnb[l].rearrange("(o n) -> o n", o=1))
            bcast_rows(lnb_bc, lnb_bc.t, f.t[0:1, :], 1024, [f])
            f = Ft[7]
            load(f, f.t[0:1, 0:128], kvg[l].rearrange("(o n) -> o n", o=1))
            bcast_rows(kvg_bc, kvg_bc.t, f.t[0:1, 0:128], 128, [f])
            if l == 0:
                mset(lbT.t[:], 0.0, [lbT])
                mset(omlT.t[:], 1.0, [omlT])
            else:
                for d in range(2):
                    col_form(small.t[:, 16:20], lbl[d, 0].rearrange("(c p) -> c p", p=128), 4, 128, small)
                    col_form(small.t[:, 20:24], lbl[d, 1].rearrange("(c p) -> c p", p=128), 4, 128, small)
                    tt(small.t[:, 24:28], small.t[:, 20:24], small.t[:, 16:20], ALU.subtract, [small], [small])
                    act(lbT.t[:, d, :], small.t[:, 24:28], AF.Sigmoid, [small], [lbT])
                ts(omlT.t[:], lbT.t[:], -1.0, 1.0, ALU.mult, ALU.add, [lbT], [omlT])
            ts(omlT.t[:], omlT.t[:], 0.5, None, ALU.mult, None, [omlT], [omlT])
            tt(lbT.t[:], lbT.t[:], omlT.t[:], ALU.add, [lbT, omlT], [lbT])
            col_form(hgnT.t[:, 0:1], hgn[l].rearrange("(c p) -> c p", p=128), 1, 128, hgnT)
            col_form(qngT.t[:, 0:2], qng[l].rearrange("(c p) -> c p", p=128), 2, 128, qngT)
            for k in range(3):
                col_form(cwT.t[:, k, :], convw[l, k].rearrange("(c p) -> c p", p=128), 4, 128, cwT)
            for kc in range(2):
                f = Ft[8 + kc]
                load(f, f.t[:, 0:768], wqb[l, kc * 128:(kc + 1) * 128, :])
                cp(wqb_b.t[:, kc, :], f.t[:, 0:768], [f], [wqb_b])
                for h in range(4):
                    o = h * 192 + 128
                    cp(wqb_sw.t[:, kc, h, 0:32], f.t[:, o + 32:o + 64], [f], [wqb_sw])
                    cp(wqb_sw.t[:, kc, h, 32:64], f.t[:, o:o + 32], [f], [wqb_sw])
            f = Ft[7]
            load(f, f.t[:], wkvb[l])
            cp(wkvb_b.t[:], f.t[:], [f], [wkvb_b])
            for h in range(4):
                p = ps()
                tr(p.t[:, 0:128], f.t[:, h * 256:h * 256 + 128], idf.t[:], [f, idf], [p])
                cp(wkT.t[:, h, :], p.t[:, 0:128], [p], [wkT])
            for c in range(0, 8 * NVAR * 64, 1024):
                cw = min(1024, 8 * NVAR * 64 - c)
                f = Ft[4 + (c // 1024) % 4]
                load(f, f.t[:, 0:cw], natab[l, :, c:c + cw], q="sync")
                act(EBt.t[:, c:c + cw], f.t[:, 0:cw], AF.Exp, [f], [EBt])

        class Job:
            pass

        jobs = []
        for j in range(3):
            J = Job()
            J.ctx = j > 0
            J.T = TP if J.ctx else TS
            J.nb = J.T // TB
            J.cv = 1 if J.ctx else 0
            J.pi = j - 1
            J.x0 = (xp[j - 1] if J.ctx else xs)
            J.ymid = (s_yp[j - 1] if J.ctx else s_ys)
            J.yout = (y_p[j - 1] if J.ctx else y_s)
            jobs.append(J)

        def make_h(J, l, blk, halo=False, rope=False):
            src = J.x0 if l == 0 else J.ymid
            t0 = blk * TB
            cv = J.cv
            for k in range(TB // 128):
                xt = xin[k % 2]
                load(xt, xt.t[:], src[t0 + k * 128:t0 + (k + 1) * 128, :], src=(None if l == 0 else r_ymid[k % 2]))
                for g in range(2):
                    p = ps()
                    for j in range(4):
                        dch = g * 4 + j
                        tr(p.t[:, j * 128:(j + 1) * 128], xt.t[:, dch * 128:(dch + 1) * 128], idf.t[:], [xt, idf], [p])
                    for j in range(4):
                        dch = g * 4 + j
                        act(hT.t[:, dch, k * 128:(k + 1) * 128], p.t[:, j * 128:(j + 1) * 128], AF.Identity,
                            [p, modT[cv]], [hT], bias=modT[cv].t[:, dch:dch + 1], scale=modT[cv].t[:, 8 + dch:9 + dch])
            if halo:
                lo = t0 - 1
                hi = t0 + TB
                mset(xhalo.t[0:2, :], 0.0, [xhalo])
                if lo >= 0:
                    load(xhalo, xhalo.t[0:1, :], src[lo:lo + 1, :], src=(None if l == 0 else r_ymid[1]))
                if hi < J.T:
                    load(xhalo, xhalo.t[1:2, :], src[hi:hi + 1, :], src=(None if l == 0 else r_ymid[0]))
                p = ps()
                for dch in range(8):
                    tr(p.t[:, dch * 2:dch * 2 + 2], xhalo.t[0:2, dch * 128:(dch + 1) * 128], idf.t[0:2, 0:2],
                       [xhalo, idf], [p])
                for dch in range(8):
                    act(hTh.t[:, dch, :], p.t[:, dch * 2:dch * 2 + 2], AF.Identity, [p, modT[cv]], [hTh],
                        bias=modT[cv].t[:, dch:dch + 1], scale=modT[cv].t[:, 8 + dch:9 + dch])
            if rope:
                load(cosT, cosT.t[:, 0:TB], c_cos[:, t0:t0 + TB])
                load(cosT, cosT.t[:, TB:2 * TB], c_sin[:, t0:t0 + TB])

        def load_h(J, l, blk):
            t0 = blk * TB
            load(hT, hT.t[:], s_hT[:, :, t0:t0 + TB].rearrange("c p t -> p c t"), src=r_hT)
            mset(hTh.t[:], 0.0, [hTh])
            lo, hi = t0 - 1, t0 + TB
            if lo >= 0:
                dma("gpsimd", hTh, hTh.t[:, :, 0:1], s_hT[:, :, lo:lo + 1].rearrange("c p t -> p c t"), R=[r_hT], W=[hTh], slow=True)
            if hi < J.T:
                dma("gpsimd", hTh, hTh.t[:, :, 1:2], s_hT[:, :, hi:hi + 1].rearrange("c p t -> p c t"), R=[r_hT], W=[hTh], slow=True)
            if not J.ctx:
                load(cosT, cosT.t[:, 0:TB], c_cos[:, t0:t0 + TB])
                load(cosT, cosT.t[:, TB:2 * TB], c_sin[:, t0:t0 + TB])

        def seg_fm(wv, wt, c, p, ncol=128, rhs_t=None, ntok=TB, pout=None):
            rt = hT if rhs_t is None else rhs_t
            out = (p.t[0:ncol, 0:ntok] if pout is None else pout)
            for kc in range(8):
                mm(out, wv[:, kc, c * 128:c * 128 + ncol], rt.t[:, kc, 0:ntok], kc == 0, kc == 7, [wt, rt], [p])

        def seg_tm(wv, wt, k, p, ncol, brow_off):
            for kc in range(8):
                mm(p.t[:, 0:ncol], hT.t[:, kc, k * 128:(k + 1) * 128], wv[:, kc, 0:ncol], kc == 0, False, [wt, hT], [p])
            mm(p.t[:, 0:ncol], onb.t[0:1, :], brow.t[0:1, brow_off:brow_off + ncol], False, True, [onb, brow], [p])

        def hgrn_block(J, l, blk, dirn, o_dst):
            Q, G, K, Bc, Dd, E = Ft[0], Ft[1], Ft[2], Ft[3], Ft[4], Ft[5]
            qt, kt, vtok = Ht[0], Ht[1], Ht[2]
            nch = TB // 64
            sgn = 1.0 if dirn == 0 else -1.0

            def v3(t):
                return t.t[:].rearrange("p (h n) -> p h n", n=TB)

            def v4(t):
                return t.t[:].rearrange("p (c l) -> p c l", l=64)
            wt, wv = wload(s_win[l], 8, SEG["a_q"], 512)
            for h in range(4):
                p = ps()
                seg_fm(wv, wt, h, p)
                bq_ = bcol(SEG["a_q"]) + h
                act(v3(E)[:, h, :], p.t[:, 0:TB], AF.Tanh, [p, binH], [E], bias=binH.t[:, bq_:bq_ + 1], scale=0.5)
                stt(v3(Dd)[:, h, :], p.t[:, 0:TB], binT.t[:, bq_:bq_ + 1], v3(E)[:, h, :], ALU.add, ALU.mult, [p, binT, E], [Dd])
                stt(v3(Q)[:, h, :], p.t[:, 0:TB], binT.t[:, bq_:bq_ + 1], v3(Dd)[:, h, :], ALU.add, ALU.add, [p, binT, Dd], [Q])
            fo = SEG["a_ff"] if dirn == 0 else SEG["a_fb"]
            wt, wv = wload(s_win[l], 8, fo, 512)
            for h in range(4):
                p = ps()
                seg_fm(wv, wt, h, p)
                act(v3(G)[:, h, :], p.t[:, 0:TB], AF.Tanh, [p, binH], [G],
                    bias=binH.t[:, bcol(fo) + h:bcol(fo) + h + 1], scale=0.5)
                ts(v3(G)[:, h, :], v3(G)[:, h, :], omlT.t[:, dirn, h:h + 1], lbT.t[:, dirn, h:h + 1],
                   ALU.mult, ALU.add, [G, omlT, lbT], [G])
            ts(K.t[:], G.t[:], -1.0, 1.0, ALU.mult, ALU.add, [G], [K])
            act(G.t[:], G.t[:], AF.Ln, [G], [G])
            for h in range(4):
                scan(v3(Bc)[:, h, :], onf.t[:, 0:TB], v3(G)[:, h, :], [onf, G], [Bc])
            tt(G.t[:], Bc.t[:], G.t[:], ALU.subtract, [Bc, G], [G])
            C = Bc if dirn == 0 else G
            tt(v4(Dd), v4(C), v4(C)[:, :, 32:33].to_broadcast([128, 4 * nch, 64]), ALU.subtract, [C], [Dd])
            act(E.t[:], Dd.t[:], AF.Exp, [Dd], [E], scale=sgn)
            stt(qt.t[:], Q.t[:], 0.5, E.t[:], ALU.mult, ALU.mult, [Q, E], [qt])
            act(E.t[:], Dd.t[:], AF.Exp, [Dd], [E], scale=-sgn)
            tt(kt.t[:], K.t[:], E.t[:], ALU.mult, [K, E], [kt])
            Cs = v4(G)[:, :, 0]
            Ce = v4(Bc)[:, :, 63]
            Rr = v4(C)[:, :, 32]
            n4 = 4 * nch
            if dirn == 0:
                tt(sc3.t[:, 0, 0:n4], Rr, Cs, ALU.subtract, [G, Bc], [sc3])
                tt(sc3.t[:, 1, 0:n4], Ce, Rr, ALU.subtract, [G, Bc], [sc3])
            else:
                tt(sc3.t[:, 0, 0:n4], Ce, Rr, ALU.subtract, [G, Bc], [sc3])
                tt(sc3.t[:, 1, 0:n4], Rr, Cs, ALU.subtract, [G, Bc], [sc3])
            tt(sc3.t[:, 2, 0:n4], Ce, Cs, ALU.subtract, [G, Bc], [sc3])
            act(sc3.t[:], sc3.t[:], AF.Exp, [sc3], [sc3])
            wt, wv = wload(s_win[l], 8, SEG["a_i"], 512)
            vv = vtok.t[:].rearrange("p (k n) -> p k n", n=512)
            for k in range(TB // 128):
                p = ps()
                seg_tm(wv, wt, k, p, 512, 0)
                acp(vv[:, k, :], p.t[:, 0:512], [p], [vtok])
            qv = v3(qt)
            kv_ = v3(kt)
            od = v3(o_dst)
            mask = mfw if dirn == 0 else mbw
            am = amF if dirn == 0 else amB
            mk = mask.t[:].rearrange("p (h n) -> p h n", n=64)
            tiles = range(TB // 128) if dirn == 0 else range(TB // 128 - 1, -1, -1)
            for k in tiles:
                pa = ps()
                for h in range(4):
                    mm(pa.t[:, h * 128:(h + 1) * 128], kv_[:, h, k * 128:(k + 1) * 128], qv[:, h, k * 128:(k + 1) * 128],
                       True, True, [kt, qt], [pa])
                pav = pa.t[:].rearrange("p (h n) -> p h n", n=128)
                for p0 in (0, 64):
                    pieces = ((0, 0, 64), (32, 32, 64)) if dirn == 0 else ((0, 0, 32), (32, 0, 64))
                    for (ro, ca, cb) in pieces:
                        tt(am.t[p0 + ro:p0 + ro + 32, :, ca:cb], pav[p0 + ro:p0 + ro + 32, :, p0 + ca:p0 + cb],
                           mk[p0 + ro:p0 + ro + 32, :, ca:cb], ALU.mult, [pa, mask], [am])
                for h in range(4):
                    tr(PSB.t[:, h * 128:(h + 1) * 128], kv_[:, h, k * 128:(k + 1) * 128], idb.t[:], [kt, idb], [PSB])
                cp(ktok.t[:].rearrange("p h n -> p (h n)"), PSB.t[:, 0:512], [PSB], [ktok])
                halves = (0, 64) if dirn == 0 else (64, 0)
                for p0 in halves:
                    c = k * 2 + p0 // 64
                    for h in range(4):
                        act(Sb16.t[:, h, :], Sst.t[:, h, :], AF.Identity, [SstH[h], sc3], [Sb16H[h]],
                            scale=sc3.t[:, 0, h * nch + c:h * nch + c + 1])
                    po = ps()
                    for h in range(4):
                        mm(po.t[:, h * 64:(h + 1) * 64], vv[p0:p0 + 64, k, h * 128:(h + 1) * 128], am.t[p0:p0 + 64, h, :],
                           True, False, [vtok, am], [po])
                        mm(po.t[:, h * 64:(h + 1) * 64], Sb16.t[:, h, :], qv[:, h, c * 64:(c + 1) * 64],
                           False, True, [Sb16H[h], qt], [po])
                    cp(od[:, :, c * 64:(c + 1) * 64], po.t[:, 0:256].rearrange("p (h n) -> p h n", n=64), [po], [o_dst])
                    pq = ps()
                    for h in range(4):
                        mm(pq.t[:, h * 128:(h + 1) * 128], ktok.t[p0:p0 + 64, h, :], vv[p0:p0 + 64, k, h * 128:(h + 1) * 128],
                           True, True, [ktok, vtok], [pq])
                    for h in range(4):
                        act(tmpS.t[:, h, :], pq.t[:, h * 128:(h + 1) * 128], AF.Identity, [pq, sc3], [tmpSH[h]],
                            scale=sc3.t[:, 1, h * nch + c:h * nch + c + 1])
                        stt(Sst.t[:, h, :], Sst.t[:, h, :], sc3.t[:, 2, h * nch + c:h * nch + c + 1], tmpS.t[:, h, :],
                            ALU.mult, ALU.add, [SstH[h], sc3, tmpSH[h]], [SstH[h]])

        def hgrn_init_state(J, l, dirn):
            if J.ctx:
                mset(Sst.t[:], 0.0, [Sst] + SstH)
            else:
                dma("gpsimd", Sst, Sst.t[:], st_h[l, dirn].rearrange("h d e -> d h e"), W=[Sst] + SstH)

        def hgrn_store_state(J, l, dirn):
            if J.ctx:
                dma("gpsimd", Sst, o_st[J.pi, l, dirn].rearrange("h d e -> d h e"), Sst.t[:], R=[Sst] + SstH)

        na_i = [0]

        na_pend = [None]

        def na_stage_b(ctx):
            (keys, Pb, nq, p0, out_ap, den_ap, out_t, den_t) = ctx
            nt = len(keys)
            po = ps()
            for i, (kap, vap, R) in enumerate(keys):
                mm(po.t[:, 0:nq], vap, Pb.t[:, i * nq:(i + 1) * nq], i == 0, i == nt - 1, R + [Pb], [po])
            for i in range(nt):
                mm(po.t[:, nq:2 * nq], onb.t[:], Pb.t[:, i * nq:(i + 1) * nq], i == 0, i == nt - 1, [onb, Pb], [po])
            cp(out_ap, po.t[p0:p0 + 64, 0:nq], [po], [out_t])
            cp(den_ap, po.t[p0:p0 + 64, nq:2 * nq], [po], [den_t])

        def na_flush():
            if na_pend[0] is not None:
                na_stage_b(na_pend[0])
                na_pend[0] = None

        def na_unit(qap, nq, keys, eb, nloc, p0, den_ap, out_ap, out_t, qR, den_t=None):
            nt = len(keys)
            pS = ps()
            for i, (kap, vap, R) in enumerate(keys):
                mm(pS.t[:, i * nq:(i + 1) * nq], kap, qap, True, True, R + qR, [pS])
            Pb = (Ht[3], Ht[10], Ht[11])[na_i[0] % 3]
            na_i[0] += 1
            act(Pb.t[:, 0:nt * nq], pS.t[:, 0:nt * nq], AF.Exp, [pS], [Pb], scale=0.125)
            if eb is not None:
                pv = Pb.t[:, 0:nloc * nq].rearrange("p (t q) -> p t q", q=nq)
                tt(pv, pv, eb, ALU.mult, [Pb, EBt], [Pb])
            if na_pend[0] is not None:
                na_stage_b(na_pend[0])
            na_pend[0] = (keys, Pb, nq, p0, out_ap, den_ap, out_t, den_t)

        def merge_gen(l, n, outn):
            ov = outn.t[:].rearrange("p (c n) -> p c n", n=TB)
            mv = mT.t[:].rearrange("p (c n) -> p c n", n=TB)
            bsrc = s_wbr[l][n * 512:(n + 1) * 512, :].rearrange("(k p) n -> p k n", p=128)
            wsrc = s_win[l].rearrange("(k p) n -> p k n", p=128)
            bv = WMb.t[:].rearrange("p (k n) -> p k n", n=128)
            wv = WMw.t[:].rearrange("p (k n) -> p k n", n=256)
            alone = (n == 3)
            if alone:
                ball = WS[0].t[:, 0:4 * 1024].rearrange("p (k n) -> p k n", n=1024)
                dma("sync", WS[0], ball, bsrc[:, :, 0:1024], R=[r_w], W=[WS[0]])
                wv1 = WS[1].t[:, 0:8 * 256].rearrange("p (k n) -> p k n", n=256)
            for q4 in range(4):
                c0 = SEG["mrg"] + n * 1024 + q4 * 256
                if alone and q4 % 2 == 1:
                    wt_t, wvq = WS[1], wv1
                else:
                    wt_t, wvq = WMw, wv
                dma("sync", wt_t, wvq, wsrc[:, :, c0:c0 + 256], R=[r_w], W=[wt_t])
                for j in range(2):
                    dc = q4 * 2 + j
                    pA = ps()
                    if alone:
                        for wc in range(4):
                            mm(pA.t[:, 0:TB], ball[:, wc, dc * 128:(dc + 1) * 128], ov[:, wc, :], wc == 0, wc == 3, [WS[0], outn], [pA])
                    else:
                        dma("sync", WMb, bv, bsrc[:, :, dc * 128:(dc + 1) * 128], R=[r_w], W=[WMb])
                        for wc in range(4):
                            mm(pA.t[:, 0:TB], bv[:, wc, :], ov[:, wc, :], wc == 0, wc == 3, [WMb, outn], [pA])
                    pB = ps()
                    for kc in range(8):
                        mm(pB.t[:, 0:TB], wvq[:, kc, j * 128:(j + 1) * 128], hT.t[:, kc, 0:TB], kc == 0, kc == 7, [wt_t, hT], [pB])
                    sg = (Ft[7], Ft[5])[dc % 2]
                    bc = bcol(SEG["mrg"]) + n * 8 + dc
                    act(sg.t[:, 0:TB], pB.t[:, 0:TB], AF.Tanh, [pB, binH], [sg], bias=binH.t[:, bc:bc + 1], scale=0.5)
                    if n == 0:
                        stt(mv[:, dc, :], sg.t[:, 0:TB], 1.0, pA.t[:, 0:TB], ALU.add, ALU.mult, [pA, sg], [mT])
                    else:
                        stt(sg.t[:, 0:TB], sg.t[:, 0:TB], 1.0, pA.t[:, 0:TB], ALU.add, ALU.mult, [pA, sg], [sg])
                        tt(mv[:, dc, :], mv[:, dc, :], sg.t[:, 0:TB], ALU.add, [mT, sg], [mT])
                    yield

        def run_pair(g1, g2, r):
            live1, live2 = True, True
            while live1 or live2:
                if live1:
                    try:
                        next(g1)
                    except StopIteration:
                        live1 = False
                for _ in range(r if live1 else 1000000):
                    if not live2:
                        break
                    try:
                        next(g2)
                    except StopIteration:
                        live2 = False

        def silu2(p, bc, sg, npart=128):
            act(sg.t[0:npart, TB:2 * TB], p.t[0:npart, 0:TB], AF.Tanh, [p, binH], [sg], bias=binH.t[0:npart, bc:bc + 1], scale=0.5)
            stt(sg.t[0:npart, 2 * TB:3 * TB], p.t[0:npart, 0:TB], binT.t[0:npart, bc:bc + 1], sg.t[0:npart, TB:2 * TB],
                ALU.add, ALU.mult, [p, binT, sg], [sg])
            stt(sg.t[0:npart, 0:TB], p.t[0:npart, 0:TB], binT.t[0:npart, bc:bc + 1], sg.t[0:npart, 2 * TB:3 * TB],
                ALU.add, ALU.add, [p, binT, sg], [sg])

        def rstd_from(pt, n, scale_, out_t):
            ts(out_t.t[:, 0:n], pt.t[:, 0:n], scale_, EPS, ALU.mult, ALU.add, [pt], [out_t])
            act(out_t.t[:, 0:n], out_t.t[:, 0:n], AF.Sqrt, [out_t], [out_t])
            recip(out_t.t[:, 0:n], out_t.t[:, 0:n], [out_t], [out_t])

        def kv_pass(J, l, after_block=None):
            nkt = J.T // 128
            for blk in range(J.nb - 1, -1, -1):
                t0 = blk * TB
                make_h(J, l, blk, rope=not J.ctx)
                store(hT, s_hT[:, :, t0:t0 + TB].rearrange("c p t -> p c t"), hT.t[:], dst=r_hT, nodep=True)
                if after_block is not None:
                    after_block(blk)
                kvs = (dbg or {}).get("kvstop", 9)
                if kvs <= 1:
                    continue
                wt, wv = wload(s_win[l], 8, SEG["b_k"], 512)
                kst = Ht[4]
                ksv = kst.t[:].rearrange("p (c n) -> p c n", n=TB)
                for c in range(4):
                    p = ps()
                    seg_fm(wv, wt, c, p)
                    bc = bcol(SEG["b_k"]) + c
                    act(ksv[:, c, :], p.t[:, 0:TB], AF.Identity, [p, binT], [kst], bias=binT.t[:, bc:bc + 1])
                store(kst, s_nk[:, :, t0:t0 + TB].rearrange("c p t -> p c t"), ksv, dst=r_nk, nodep=True)
                if J.ctx:
                    for k in range(TB // 128):
                        p = ps()
                        seg_tm(wv, wt, k, p, 512, 1216)
                        f = Ft[8 + k % 2]
                        cp(f.t[:, 0:512], p.t[:, 0:512], [p], [f])
                        store(f, o_nk[J.pi, l, t0 + k * 128:t0 + (k + 1) * 128, :], f.t[:, 0:512])
                if kvs <= 2:
                    continue
                wt, wv = wload(s_win[l], 8, SEG["b_v"], 512)
                vst = Ht[5]
                vsv = vst.t[:].rearrange("p (k n) -> p k n", n=512)
                for k in range(TB // 128):
                    p = ps()
                    seg_tm(wv, wt, k, p, 512, 512)
                    if J.ctx:
                        f = Ft[8 + k % 2]
                        cp(f.t[:, 0:512], p.t[:, 0:512], [p], [f])
                        acp(vsv[:, k, :], f.t[:, 0:512], [f], [vst])
                        store(f, o_nv[J.pi, l, t0 + k * 128:t0 + (k + 1) * 128, :], f.t[:, 0:512])
                    else:
                        acp(vsv[:, k, :], p.t[:, 0:512], [p], [vst])
                store(vst, s_nv[t0:t0 + TB, :].rearrange("(k p) n -> p k n", p=128), vsv, dst=r_nv, nodep=True)
                if kvs <= 3:
                    continue
                wt, wv = wload(s_win[l], 8, SEG["c_kvd"], 192)
                for k in range(TB // 128):
                    kt_i = blk * (TB // 128) + k
                    p = ps()
                    seg_tm(wv, wt, k, p, 128, 1024)
                    junk = Ft[6]
                    act(junk.t[:, 0:128], p.t[:, 0:128], AF.Square, [p], [junk])
                    P.op("vector", (lambda o, i: (lambda e: e.reduce_sum(out=o, in_=i, axis=mybir.AxisListType.X)))(small.t[:, 32:33], junk.t[:, 0:128]),
                         B([junk]), B([small]))
                    ts(small.t[:, 33:34], small.t[:, 32:33], 1.0 / 128, EPS, ALU.mult, ALU.add, [small], [small])
                    act(small.t[:, 34:35], small.t[:, 33:34], AF.Sqrt, [small], [small])
                    recip(small.t[:, 35:36], small.t[:, 34:35], [small], [small])
                    f = Ft[8 + k % 2]
                    stt(f.t[:, 512:640], p.t[:, 0:128], small.t[:, 35:36], kvg_bc.t[:], ALU.mult, ALU.mult,
                        [p, small, kvg_bc], [f])
                    if J.ctx:
                        store(f, o_ckv[J.pi, l, t0 + k * 128:t0 + (k + 1) * 128, :], f.t[:, 512:640])
                    cp(vlat.t[:, kt_i, :], f.t[:, 512:640], [f], [vlat])
                    tr(PSB.t[:, 0:128], vlat.t[:, kt_i, :], idb.t[:], [vlat, idb], [PSB])
                    cp(klat.t[:, kt_i * 128:(kt_i + 1) * 128], PSB.t[:, 0:128], [PSB], [klat])
                    if J.ctx:
                        p2 = ps()
                        for kc in range(8):
                            mm(p2.t[:, 0:64], hT.t[:, kc, k * 128:(k + 1) * 128], wv[:, kc, 128:192], kc == 0, False,
                               [wt, hT], [p2])
                        mm(p2.t[:, 0:64], onb.t[0:1, :], brow.t[0:1, 1152:1216], False, True, [onb, brow], [p2])
                        f2 = Ft[8 + (k + 1) % 2]
                        cp(f2.t[:, 512:576], p2.t[:, 0:64], [p2], [f2])
                        store(f2, o_kpe[J.pi, l, t0 + k * 128:t0 + (k + 1) * 128, :], f2.t[:, 512:576])
                p = ps()
                seg_fm(wv, wt, 1, p, ncol=64)
                bc = bcol(SEG["c_kpe"])
                kdst = kpeT.t[0:64, t0:t0 + TB]
                if J.ctx:
                    act(kdst, p.t[0:64, 0:TB], AF.Identity, [p, binT], [kpeT], bias=binT.t[0:64, bc:bc + 1])
                else:
                    cp(wkpe_sw.t[:, :, 0:32], wv[:, :, 160:192], [wt], [wkpe_sw])
                    cp(wkpe_sw.t[:, :, 32:64], wv[:, :, 128:160], [wt], [wkpe_sw])
                    p2 = ps()
                    for kc in range(8):
                        mm(p2.t[0:64, 0:TB], wkpe_sw.t[:, kc, :], hT.t[:, kc, :], kc == 0, kc == 7, [wkpe_sw, hT], [p2])
                    t1 = Ft[6]
                    t2 = Ft[7]
                    stt(t1.t[0:64, 0:TB], p.t[0:64, 0:TB], binT.t[0:64, bc:bc + 1], cosT.t[:, 0:TB], ALU.add, ALU.mult,
                        [p, binT, cosT], [t1])
                    stt(t2.t[0:64, 0:TB], p2.t[0:64, 0:TB], bkpesw.t[:, 0:1], cosT.t[:, TB:2 * TB], ALU.add, ALU.mult,
                        [p2, bkpesw, cosT], [t2])
                    tt(kdst, t1.t[0:64, 0:TB], t2.t[0:64, 0:TB], ALU.add, [t1, t2], [kpeT])
            if not J.ctx:
                for k in range(2):
                    kt_i = nkt + k
                    f = Ft[8 + k]
                    load(f, f.t[:, 0:128], cckv[l, k * 128:(k + 1) * 128, :])
                    load(f, f.t[:, 128:192], ckpe[l, k * 128:(k + 1) * 128, :])
                    cp(vlat.t[:, kt_i, :], f.t[:, 0:128], [f], [vlat])
                    tr(PSB.t[:, 0:128], vlat.t[:, kt_i, :], idb.t[:], [vlat, idb], [PSB])
                    cp(klat.t[:, kt_i * 128:(kt_i + 1) * 128], PSB.t[:, 0:128], [PSB], [klat])
                    p = ps()
                    tr(p.t[0:64, 0:128], f.t[:, 128:192], idf.t[:], [f, idf], [p])
                    cp(kpeT.t[0:64, kt_i * 128:(kt_i + 1) * 128], p.t[0:64, 0:128], [p], [kpeT])
                    g = Ft[6 + k]
                    load(g, g.t[:, 0:512], cnk[l, k * 128:(k + 1) * 128, :])
                    p = ps()
                    for c in range(4):
                        tr(p.t[:, c * 128:(c + 1) * 128], g.t[:, c * 128:(c + 1) * 128], idf.t[:], [g, idf], [p])
                    cp(ckT.t[:, :, k * 128:(k + 1) * 128], p.t[:, 0:512].rearrange("p (c n) -> p c n", n=128), [p], [ckT])
                    load(g, g.t[:, 512:1024], cnv[l, k * 128:(k + 1) * 128, :])
                    cp(cvt_.t[:, k, :], g.t[:, 512:1024], [g], [cvt_])

        def bwd_block(J, l, blk):
            if True:
                t0 = blk * TB
                OB = Ft[8]
                hgrn_block(J, l, blk, 1, OB)
                store(OB, s_ob[:, :, t0:t0 + TB].rearrange("h p t -> p h t"),
                      OB.t[:].rearrange("p (h n) -> p h n", n=TB), dst=r_ob, nodep=True)

        def main_pass(J, l):
            hgrn_init_state(J, l, 0)
            nkt = J.T // 128 + (0 if J.ctx else 2)
            for blk in range(J.nb):
                t0 = blk * TB
                load_h(J, l, blk)
                OF = Ft[8]
                OBt = Ft[9]
                load(OBt, OBt.t[:].rearrange("p (h n) -> p h n", n=TB),
                     s_ob[:, :, t0:t0 + TB].rearrange("h p t -> p h t"), src=r_ob)
                hgrn_block(J, l, blk, 0, OF)
                tt(OF.t[:], OF.t[:], OBt.t[:], ALU.add, [OF, OBt], [OF])
                sq = Ft[0]
                act(sq.t[:], OF.t[:], AF.Square, [OF], [sq])
                outa, outb, outc, outd = Ht[6], Ht[0], Ht[6], Ht[0]
                wt, wv = wload(s_win[l], 8, SEG["a_g"], 512)
                rs = Ft[1]
                for h in range(4):
                    p = ps()
                    mm(p.t[:, 0:TB], onf.t[:, 0:128], sq.t[:, h * TB:(h + 1) * TB], True, True, [onf, sq], [p])
                    ts(rs.t[:, h * TB:(h + 1) * TB], p.t[:, 0:TB], 1.0 / 128, EPS, ALU.mult, ALU.add, [p], [rs])
                act(rs.t[:], rs.t[:], AF.Sqrt, [rs], [rs])
                recip(rs.t[:], rs.t[:], [rs], [rs])
                for h in range(4):
                    stt(rs.t[:, h * TB:(h + 1) * TB], OF.t[:, h * TB:(h + 1) * TB], hgnT.t[:, 0:1], rs.t[:, h * TB:(h + 1) * TB],
                        ALU.mult, ALU.mult, [OF, hgnT, rs], [rs])
                for h in range(4):
                    p2 = ps()
                    seg_fm(wv, wt, h, p2)
                    sg = Ft[2]
                    bc = bcol(SEG["a_g"]) + h
                    silu2(p2, bc, sg)
                    stt(outa.t[:, h * TB:(h + 1) * TB], sg.t[:, 0:TB], 0.5, rs.t[:, h * TB:(h + 1) * TB], ALU.mult, ALU.mult,
                        [rs, sg], [outa])
                def brB():
                    wt, wv = wload(s_win[l], 8, SEG["b_q"], 512)
                    qn_ = Ht[7]
                    qnv = qn_.t[:].rearrange("p (c n) -> p c n", n=TB)
                    for c in range(4):
                        p = ps()
                        seg_fm(wv, wt, c, p)
                        bc = bcol(SEG["b_q"]) + c
                        act(qnv[:, c, :], p.t[:, 0:TB], AF.Identity, [p, binT], [qn_], bias=binT.t[:, bc:bc + 1])
                    ON = Ft[0]
                    onv = ON.t[:].rearrange("p (c n) -> p c n", n=TB)
                    DEN = Ft[3]
                    dnv = DEN.t[:].rearrange("p (c n) -> p c n", n=TB)
                    if J.ctx:
                        ta, tb_ = 0, 1
                    else:
                        r0 = blk * 4
                        ta = min(max(r0 - 4, 0), 56) // 2
                        tb_ = (min(max(r0 + 3 - 4, 0), 56) + 7) // 2
                    nwt = tb_ - ta + 1
                    KW = sbkw
                    kwv = KW.t[:, 0:4 * nwt * 128].rearrange("p (c n) -> p c n", n=nwt * 128)
                    load(KW, kwv, s_nk[:, :, ta * 128:(tb_ + 1) * 128].rearrange("c p t -> p c t"), src=r_nk)
                    VW = sbvw
                    vwv = VW.t[:, 0:nwt * 512].rearrange("p (k n) -> p k n", n=512)
                    load(VW, vwv, s_nv[ta * 128:(tb_ + 1) * 128, :].rearrange("(k p) n -> p k n", p=128), src=r_nv)
                    if J.ctx:
                        for h in range(8):
                            c, p0 = h // 2, 64 * (h % 2)
                            keys = [(kwv[p0:p0 + 64, c, i * 128:(i + 1) * 128], vwv[:, i, c * 128:(c + 1) * 128], [KW, VW])
                                    for i in range(2)]
                            na_unit(qnv[p0:p0 + 64, c, :], TB, keys, None, 0, p0, dnv[p0:p0 + 64, c, :], onv[p0:p0 + 64, c, :], ON, [qn_],
                                    den_t=DEN)
                            yield
                    else:
                        ebv = EBt.t[:].rearrange("p (h v q) -> p h v q", v=NVAR, q=64)
                        for rr in range(4):
                            r = r0 + rr
                            rs_ = min(max(r - 4, 0), 56)
                            i0, i1 = rs_ // 2, (rs_ + 7) // 2
                            nloc = i1 - i0 + 1
                            for h in range(8):
                                c, p0 = h // 2, 64 * (h % 2)
                                keys = [(kwv[p0:p0 + 64, c, (i - ta) * 128:(i - ta + 1) * 128],
                                         vwv[:, i - ta, c * 128:(c + 1) * 128], [KW, VW]) for i in range(i0, i1 + 1)]
                                keys += [(ckT.t[p0:p0 + 64, c, i * 128:(i + 1) * 128], cvt_.t[:, i, c * 128:(c + 1) * 128],
                                          [ckT, cvt_]) for i in range(2)]
                                if nloc == 5:
                                    eb = ebv[:, h, 14:19, :]
                                else:
                                    v0 = (2 * i0 - r) + 7
                                    eb = ebv[:, h, v0:v0 + 7:2, :]
                                na_unit(qnv[p0:p0 + 64, c, rr * 64:(rr + 1) * 64], 64, keys, eb, nloc, p0,
                                        dnv[p0:p0 + 64, c, rr * 64:(rr + 1) * 64],
                                        onv[p0:p0 + 64, c, rr * 64:(rr + 1) * 64], ON, [qn_], den_t=DEN)
                                yield
                    na_flush()
                    recip(DEN.t[:], DEN.t[:], [DEN], [DEN])
                    tt(ON.t[:], ON.t[:], DEN.t[:], ALU.mult, [ON, DEN], [ON])
                    yield
                    wt, wv = wload(s_win[l], 8, SEG["b_g"], 512)
                    for c in range(4):
                        p = ps()
                        seg_fm(wv, wt, c, p)
                        sg = Ft[2]
                        bc = bcol(SEG["b_g"]) + c
                        silu2(p, bc, sg)
                        stt(outb.t[:, c * TB:(c + 1) * TB], sg.t[:, 0:TB], 0.5, onv[:, c, :], ALU.mult, ALU.mult, [ON, sg], [outb])
                    yield
                def brC():
                    wt, wv = wload(s_win[l], 8, SEG["c_qd"], 256)
                    qd = Ft[0]
                    qdv = qd.t[:, 0:2 * TB].rearrange("p (c n) -> p c n", n=TB)
                    sq = Ft[1]
                    pss = ps()
                    for c in range(2):
                        p = ps()
                        seg_fm(wv, wt, c, p)
                        bc = bcol(SEG["c_qd"]) + c
                        act(qdv[:, c, :], p.t[:, 0:TB], AF.Identity, [p, binT], [qd], bias=binT.t[:, bc:bc + 1])
                        act(sq.t[:, c * TB:(c + 1) * TB], qdv[:, c, :], AF.Square, [qd], [sq])
                        mm(pss.t[:, 0:TB], onf.t[:, 0:128], sq.t[:, c * TB:(c + 1) * TB], c == 0, c == 1, [onf, sq], [pss])
                    rs = Ft[2]
                    rstd_from(pss, TB, 1.0 / 256, rs)
                    qnb = Ht[7]
                    qnbv = qnb.t[:, 0:2 * TB].rearrange("p (c n) -> p c n", n=TB)
                    for c in range(2):
                        stt(qnbv[:, c, :], qdv[:, c, :], qngT.t[:, c:c + 1], rs.t[:, 0:TB], ALU.mult, ALU.mult,
                            [qd, qngT, rs], [qnb])
                    qlat = Ht[8]
                    qlv = qlat.t[:].rearrange("p (h n) -> p h n", n=TB)
                    qrope = Ht[9]
                    qrv = qrope.t[0:64, :].rearrange("p (h n) -> p h n", n=TB)
                    for h in range(4):
                        p = ps()
                        for kc in range(2):
                            mm(p.t[:, 0:TB], wqb_b.t[:, kc, h * 192:h * 192 + 128], qnbv[:, kc, :], kc == 0, kc == 1,
                               [wqb_b, qnb], [p])
                        qno = Ht[10]
                        cp(qno.t[:, 0:TB], p.t[:, 0:TB], [p], [qno])
                        p2 = ps()
                        mm(p2.t[:, 0:TB], wkT.t[:, h, :], qno.t[:, 0:TB], True, True, [wkT, qno], [p2])
                        acp(qlv[:, h, :], p2.t[:, 0:TB], [p2], [qlat])
                        p3 = ps()
                        for kc in range(2):
                            mm(p3.t[0:64, 0:TB], wqb_b.t[:, kc, h * 192 + 128:h * 192 + 192], qnbv[:, kc, :], kc == 0, kc == 1,
                               [wqb_b, qnb], [p3])
                        if J.ctx:
                            cp(qrv[:, h, :], p3.t[0:64, 0:TB], [p3], [qrope])
                        else:
                            p4 = ps()
                            for kc in range(2):
                                mm(p4.t[0:64, 0:TB], wqb_sw.t[:, kc, h, :], qnbv[:, kc, :], kc == 0, kc == 1, [wqb_sw, qnb], [p4])
                            t1 = Ft[3]
                            t2 = Ft[4]
                            tt(t1.t[0:64, 0:TB], p3.t[0:64, 0:TB], cosT.t[:, 0:TB], ALU.mult, [p3, cosT], [t1])
                            tt(t2.t[0:64, 0:TB], p4.t[0:64, 0:TB], cosT.t[:, TB:2 * TB], ALU.mult, [p4, cosT], [t2])
                            tt(qrv[:, h, :], t1.t[0:64, 0:TB], t2.t[0:64, 0:TB], ALU.add, [t1, t2], [qrope])
                        yield
                    wt, wv = wload(s_win[l], 8, SEG["c_g"], 512)
                    msc = 192.0 ** -0.5
                    for hp in range(2):
                        pacc_o = psacc()
                        pacc_d = psacc()
                        dacc = Ft[1]
                        q2 = qlat.t[:, hp * 2 * TB:(hp * 2 + 2) * TB]
                        r2 = qrope.t[0:64, hp * 2 * TB:(hp * 2 + 2) * TB]

                        def mla_b(kt_i, Pb_):
                            first = (kt_i == 0)
                            last = (kt_i == nkt - 1)
                            mm(pacc_o.t[:, 0:2 * TB], vlat.t[:, kt_i, :], Pb_.t[:, 0:2 * TB], first, last, [vlat, Pb_], [pacc_o])
                            if first:
                                cp(dacc.t[:, 0:2 * TB], Pb_.t[:, 0:2 * TB], [Pb_], [dacc])
                            else:
                                tt(dacc.t[:, 0:2 * TB], dacc.t[:, 0:2 * TB], Pb_.t[:, 0:2 * TB], ALU.add, [dacc, Pb_], [dacc])
                        pend = None
                        for kt_i in range(nkt):
                            pS = ps()
                            mm(pS.t[:, 0:2 * TB], klat.t[:, kt_i * 128:(kt_i + 1) * 128], q2, True, False, [klat, qlat], [pS])
                            mm(pS.t[:, 0:2 * TB], kpeT.t[0:64, kt_i * 128:(kt_i + 1) * 128], r2, False, True, [kpeT, qrope], [pS])
                            Pb = (Ht[10], Ht[11], Ht[3])[kt_i % 3]
                            act(Pb.t[:, 0:2 * TB], pS.t[:, 0:2 * TB], AF.Exp, [pS], [Pb], scale=msc)
                            if pend is not None:
                                mla_b(*pend)
                            pend = (kt_i, Pb)
                            yield
                        mla_b(*pend)
                        mm(pacc_d.t[:, 0:2 * TB], onf.t[:, 0:128], dacc.t[:, 0:2 * TB], True, True, [onf, dacc], [pacc_d])
                        rc = Ft[3]
                        recip(rc.t[:, 0:2 * TB], pacc_d.t[:, 0:2 * TB], [pacc_d], [rc])
                        oln = Ht[7]
                        tt(oln.t[:, TB * 2:TB * 4], pacc_o.t[:, 0:2 * TB], rc.t[:, 0:2 * TB], ALU.mult, [pacc_o, rc], [oln])
                        for j in range(2):
                            h = hp * 2 + j
                            p = ps()
                            mm(p.t[:, 0:TB], wkvb_b.t[:, h * 256 + 128:h * 256 + 256], oln.t[:, TB * (2 + j):TB * (3 + j)], True, True,
                               [wkvb_b, oln], [p])
                            p2 = ps()
                            seg_fm(wv, wt, h, p2)
                            sg = Ft[4]
                            bc = bcol(SEG["c_g"]) + h
                            silu2(p2, bc, sg)
                            stt(outc.t[:, h * TB:(h + 1) * TB], sg.t[:, 0:TB], 0.5, p.t[:, 0:TB], ALU.mult, ALU.mult, [p, sg], [outc])
                            yield
                    yield
                def brD():
                    wtc, wvc = wload(s_win[l], 8, SEG["d_c"], 512)
                    wtx, wvx = wload(s_win[l], 8, SEG["d_x"], 512)
                    uTs = [Ft[0], Ft[4]]
                    uvs = [t_.t[:, 0:2 * (TB + 4)].rearrange("p (c n) -> p c n", n=TB + 4) for t_ in uTs]
                    cvb = Ft[1]
                    for c in range(4):
                        uT = uTs[c // 2]
                        uvc = uvs[c // 2][:, c % 2, :]
                        bcc = bcol(SEG["d_c"]) + c
                        bcx = bcol(SEG["d_x"]) + c
                        p = ps()
                        seg_fm(wvc, wtc, c, p)
                        tc_ = Ft[2]
                        act(tc_.t[:, 0:TB], p.t[:, 0:TB], AF.Identity, [p, binT], [tc_], bias=binT.t[:, bcc:bcc + 1])
                        p2 = ps()
                        seg_fm(wvx, wtx, c, p2)
                        stt(uvc[:, 1:TB + 1], p2.t[:, 0:TB], binT.t[:, bcx:bcx + 1], tc_.t[:, 0:TB], ALU.add, ALU.mult,
                            [p2, binT, tc_], [uT])
                        p3 = ps()
                        seg_fm(wvc, wtc, c, p3, rhs_t=hTh, ntok=2)
                        act(tc_.t[:, 512:514], p3.t[:, 0:2], AF.Identity, [p3, binT], [tc_], bias=binT.t[:, bcc:bcc + 1])
                        p4 = ps()
                        seg_fm(wvx, wtx, c, p4, rhs_t=hTh, ntok=2)
                        hal = Ft[3]
                        stt(hal.t[:, 0:2], p4.t[:, 0:2], binT.t[:, bcx:bcx + 1], tc_.t[:, 512:514], ALU.add, ALU.mult,
                            [p4, binT, tc_], [hal])
                        if t0 == 0:
                            mset(uvc[:, 0:1], 0.0, [uT])
                        else:
                            cp(uvc[:, 0:1], hal.t[:, 0:1], [hal], [uT])
                        if t0 + TB >= J.T:
                            mset(uvc[:, TB + 1:TB + 2], 0.0, [uT])
                        else:
                            cp(uvc[:, TB + 1:TB + 2], hal.t[:, 1:2], [hal], [uT])
                        ts(cvb.t[:, c * TB:(c + 1) * TB], uvc[:, 0:TB], cwT.t[:, 0, c:c + 1], None, ALU.mult, None, [uT, cwT], [cvb])
                        stt(cvb.t[:, c * TB:(c + 1) * TB], uvc[:, 1:TB + 1], cwT.t[:, 1, c:c + 1], cvb.t[:, c * TB:(c + 1) * TB],
                            ALU.mult, ALU.add, [uT, cwT, cvb], [cvb])
                        stt(cvb.t[:, c * TB:(c + 1) * TB], uvc[:, 2:TB + 2], cwT.t[:, 2, c:c + 1], cvb.t[:, c * TB:(c + 1) * TB],
                            ALU.mult, ALU.add, [uT, cwT, cvb], [cvb])
                        yield
                    wtb, wvb = wload(s_win[l], 8, SEG["d_b"], 512)
                    wtg, wvg = wload(s_win[l], 8, SEG["d_g"], 512)
                    for c in range(4):
                        p = ps()
                        seg_fm(wvb, wtb, c, p)
                        bcb = bcol(SEG["d_b"]) + c
                        t2 = Ft[2]
                        stt(t2.t[:, 0:TB], p.t[:, 0:TB], binT.t[:, bcb:bcb + 1], cvb.t[:, c * TB:(c + 1) * TB], ALU.add, ALU.mult,
                            [p, binT, cvb], [t2])
                        p2 = ps()
                        seg_fm(wvg, wtg, c, p2)
                        sg = Ft[3]
                        bcg = bcol(SEG["d_g"]) + c
                        silu2(p2, bcg, sg)
                        stt(outd.t[:, c * TB:(c + 1) * TB], sg.t[:, 0:TB], 0.5, t2.t[:, 0:TB], ALU.mult, ALU.mult, [t2, sg], [outd])
                        yield
                    yield
                stopn = (dbg or {}).get('stop', 9)
                run_pair(merge_gen(l, 0, outa), brB() if stopn >= 1 else iter(()), 4)
                if stopn == 0:
                    return
                run_pair(merge_gen(l, 1, outb), brC() if stopn >= 2 else iter(()), 9)
                if stopn == 1:
                    return
                run_pair(merge_gen(l, 2, outc), brD() if stopn >= 3 else iter(()), 1)
                if stopn == 2:
                    return
                run_pair(merge_gen(l, 3, outd), iter(()), 1)
                if stopn == 3:
                    return
                ts(Ht[4].t[:, 0:1024], mT.t[:, 0:1024], 0.5, None, ALU.mult, None, [mT], [Ht[4]])
                act(Ht[5].t[:, 0:1024], mT.t[:, 1024:2048], AF.Identity, [mT], [Ht[5]], scale=0.5)
                mxvs = [Ht[4].t[:].rearrange("p (c n) -> p c n", n=TB), Ht[5].t[:].rearrange("p (c n) -> p c n", n=TB)]
                src = J.x0 if l == 0 else J.ymid
                wo = [wload(s_wout[l], 8, hf * 512, 512) for hf in range(2)]
                for k in range(TB // 128):
                    xt = xin[k % 2]
                    load(xt, xt.t[:], src[t0 + k * 128:t0 + (k + 1) * 128, :], src=(None if l == 0 else r_ymid[k % 2]))
                    yt = Ft[2 + k % 2]
                    for hf in range(2):
                        p = ps()
                        wt, wv = wo[hf]
                        for kc in range(8):
                            mm(p.t[:, 0:512], mxvs[kc // 4][:, kc % 4, k * 128:(k + 1) * 128], wv[:, kc, :], kc == 0, kc == 7,
                               [wt, Ht[4 + kc // 4]], [p])
                        tt(yt.t[:, hf * 512:(hf + 1) * 512], p.t[:, 0:512], gate_bc[J.cv].t[:, hf * 512:(hf + 1) * 512], ALU.mult,
                           [p, gate_bc[J.cv]], [yt])
                    stt(yt.t[:], xt.t[:], ALPHA, yt.t[:], ALU.mult, ALU.add, [xt, yt], [yt])
                    for hf in range(2):
                        P.op("vector", (lambda o, i: (lambda e: e.bn_stats(out=o, in_=i)))(bnst.t[:, hf, :], yt.t[:, hf * 512:(hf + 1) * 512]),
                             B([yt]), B([bnst]))
                    P.op("vector", (lambda o, i: (lambda e: e.bn_aggr(out=o, in_=i)))(bnag.t[:, 0:2], bnst.t[:].rearrange("p a b -> p (a b)")),
                         B([bnst]), B([bnag]))
                    ts(small.t[:, 40:41], bnag.t[:, 1:2], 1.0, EPS, ALU.mult, ALU.add, [bnag], [small])
                    act(small.t[:, 41:42], small.t[:, 40:41], AF.Sqrt, [small], [small])
                    recip(small.t[:, 42:43], small.t[:, 41:42], [small], [small])
                    ts(yt.t[:], yt.t[:], bnag.t[:, 0:1], small.t[:, 42:43], ALU.subtract, ALU.mult, [yt, bnag, small], [yt])
                    tt(yt.t[:], yt.t[:], lng_bc.t[:], ALU.mult, [yt, lng_bc], [yt])
                    tt(yt.t[:], yt.t[:], lnb_bc.t[:], ALU.add, [yt, lnb_bc], [yt])
                    if l == DEPTH - 1:
                        store(yt, J.yout[t0 + k * 128:t0 + (k + 1) * 128, :], yt.t[:])
                    else:
                        store(yt, J.ymid[t0 + k * 128:t0 + (k + 1) * 128, :], yt.t[:], dst=r_ymid[k % 2], nodep=True)
            hgrn_store_state(J, l, 0)

        sbkw = sb("KW", [128, 4 * 6 * 128], BF16)
        sbvw = sb("VW", [128, 6 * 512], BF16)
        mT = sb("mT", [128, 8 * TB], F32)
        WMb = sb("WMb", [128, 4 * 128], BF16)
        WMw = sb("WMw", [128, 8 * 256], BF16)

        dbg_ = dbg or {}
        for l in range(dbg_.get("layers", DEPTH)):
            if dbg_.get("params", True):
                layer_params(l)
            for ji, J in enumerate(jobs):
                if ji not in dbg_.get("jobs", (0, 1, 2)):
                    continue
                if "kv" in dbg_.get("passes", "kv,bwd,main"):
                    hgrn_init_state(J, l, 1)
                    kv_pass(J, l, after_block=(lambda blk, J=J, l=l: bwd_block(J, l, blk)))
                    hgrn_store_state(J, l, 1)
                if "main" in dbg_.get("passes", "kv,bwd,main"):
                    main_pass(J, l)
        P.wait_all("gpsimd", [t.b for t in [Sst] + SstH + Ft + Ht])
        P.emit()
    return nc


_NC_CACHE = {}
_DBG = None


def _const_inputs():
    idf = np.eye(128, dtype=np.float32)
    idb = np.eye(128).astype(ml_dtypes.bfloat16)
    t = np.arange(TS)
    row = (t // 64).astype(np.float32)
    col = (t % 64).astype(np.float32)
    inv = (1.0 / (np.float32(10000.0) ** (np.arange(16, dtype=np.float32) / np.float32(16)))).astype(np.float32)
    ang = np.concatenate([row[:, None] * inv, col[:, None] * inv], axis=-1).astype(np.float32)
    cos = np.cos(ang).astype(np.float32).T
    sin = np.sin(ang).astype(np.float32).T
    c_cos = np.ascontiguousarray(np.concatenate([cos, cos], axis=0))
    c_sin = np.ascontiguousarray(np.concatenate([-sin, sin], axis=0))
    p = np.arange(128)[:, None] % 64
    tt_ = np.arange(256)[None, :] % 64
    c_mf = (p <= tt_).astype(ml_dtypes.bfloat16)
    c_mb = (p >= tt_).astype(ml_dtypes.bfloat16)
    return dict(c_idf=idf, c_idb=idb, c_cos=c_cos, c_sin=c_sin, c_mf=c_mf, c_mb=c_mb)


def _na_table(na_rpb):
    rpb = np.asarray(na_rpb, dtype=np.float32)
    pairs = [(v - 7, v - 6, True, True) for v in range(14)]
    pairs += [(-5, -4, False, True), (-3, -2, True, True), (-1, 0, True, True), (1, 2, True, True), (3, 4, True, False)]
    kc = np.arange(64)[:, None]
    qc = np.arange(64)[None, :]
    cs = np.clip(qc - 8, 0, 48)
    win = (kc >= cs) & (kc < cs + 16)
    cidx = np.clip(kc - qc + 15, 0, 30)
    tab = np.full((DEPTH, 128, 8, NVAR, 64), -30000.0, dtype=np.float32)
    for v, (da, db, va, vb) in enumerate(pairs):
        for which, (dr, ok) in enumerate(((da, va), (db, vb))):
            if not ok:
                continue
            g = rpb[:, :, dr + 7, :][:, :, cidx]
            g = np.where(win[None, None], g, np.float32(-30000.0))
            tab[:, which * 64:(which + 1) * 64, :, v, :] = np.transpose(g, (0, 2, 1, 3))
    return np.ascontiguousarray(tab.reshape(DEPTH, 128, 8 * NVAR * 64))


def kernel(x_prompt, x_sample, state_hgrn, cache_na_k, cache_na_v, cache_mla_ckv, cache_mla_kpe,
           c, c_ctx, w_ada, b_ada, w_in, b_in, hg_lb_logits, hg_norm_g, na_rpb, mla_qnorm_g,
           mla_w_qb, mla_kvnorm_g, mla_w_kvb, conv_w, w_branch, w_out, ln_g, ln_b):
    f = lambda a: np.ascontiguousarray(np.asarray(a, dtype=np.float32))
    if "nc" not in _NC_CACHE:
        _NC_CACHE["nc"] = build_program(_DBG)
    nc = _NC_CACHE["nc"]
    shared = dict(
        w_ada=f(w_ada), b_ada=f(b_ada), w_in=f(w_in), b_in=f(b_in), lbl=f(hg_lb_logits), hgn=f(hg_norm_g),
        natab=_na_table(na_rpb), qng=f(mla_qnorm_g), wqb=f(mla_w_qb), kvg=f(mla_kvnorm_g), wkvb=f(mla_w_kvb),
        convw=f(conv_w), wbr=f(w_branch).reshape(DEPTH, 2048, D), wout=f(w_out), lng=f(ln_g), lnb=f(ln_b))
    shared.update(_const_inputs())
    x_prompt = f(x_prompt)
    x_sample = f(x_sample)
    state_hgrn = f(state_hgrn)
    cnk = f(cache_na_k).reshape(8, DEPTH, 256, 512)
    cnv = f(cache_na_v).reshape(8, DEPTH, 256, 512)
    cckv = f(cache_mla_ckv)
    ckpe = f(cache_mla_kpe)
    c = f(c)
    c_ctx = f(c_ctx)
    in_maps = []
    for i in range(NCORES):
        m = dict(shared)
        m["xs"] = x_sample[i]
        m["xp"] = x_prompt[2 * i:2 * i + 2]
        m["st_h"] = state_hgrn[i]
        m["cnk"] = cnk[i]
        m["cnv"] = cnv[i]
        m["cckv"] = cckv[i]
        m["ckpe"] = ckpe[i]
        m["cvec"] = np.ascontiguousarray(np.stack([c[i], c_ctx], axis=0))
        in_maps.append(m)
    res = run_bass_kernel_spmd(nc, in_maps, core_ids=list(range(NCORES)))
    R = res.results
    y_p = np.concatenate([np.asarray(r["y_p"], dtype=np.float32) for r in R], axis=0)
    y_s = np.stack([np.asarray(r["y_s"], dtype=np.float32) for r in R], axis=0)
    st = np.concatenate([np.asarray(r["o_st"], dtype=np.float32) for r in R], axis=0)
    nk = np.concatenate([np.asarray(r["o_nk"], dtype=np.float32) for r in R], axis=0).reshape(16, DEPTH, TP, 8, 64)
    nv = np.concatenate([np.asarray(r["o_nv"], dtype=np.float32) for r in R], axis=0).reshape(16, DEPTH, TP, 8, 64)
    ckv = np.concatenate([np.asarray(r["o_ckv"], dtype=np.float32) for r in R], axis=0)
    kpe = np.concatenate([np.asarray(r["o_kpe"], dtype=np.float32) for r in R], axis=0)
    return (y_p, y_s, st, nk, nv, ckv, kpe)
```
